# Optimizing a Trainium2 kernel written in Bass

```python
import jax, jax.numpy as jnp
from jax import lax
import numpy as np

D_MODEL = 1024
BATCH = 8
SEQ = 4096
DEPTH = 4

GRID_W = 64
CTX_LEN = 256
EPS = 1e-6
A_WIDTH = D_MODEL // 2
A_CONV_W = 31
B_GROUPS = 4
B_WIDTH = D_MODEL // 2
B_GROUP_DIM = B_WIDTH // B_GROUPS
C_WIDTH = D_MODEL // 2
C_CONV_W = 3
HEAD_DIM = 64
N_Q_HEADS = (D_MODEL // 2) // HEAD_DIM
N_KV_HEADS = max(N_Q_HEADS // 4, 1)
Q_GROUP = N_Q_HEADS // N_KV_HEADS
ATT_WIDTH = N_Q_HEADS * HEAD_DIM
KV_WIDTH = N_KV_HEADS * HEAD_DIM
WINDOW = 128
BLOCK = 128
ROPE_BASE = 10000.0
NEG_INF = -1e30
EVEN_IN = 3 * A_WIDTH + 2 * B_WIDTH
EVEN_MIX = A_WIDTH + B_WIDTH
ODD_IN = 4 * C_WIDTH + 2 * ATT_WIDTH + 2 * KV_WIDTH
ODD_MIX = C_WIDTH + ATT_WIDTH

kernel_name = 'hybrid_conformer_fnet_shortconv_swa_dit'


def _rmsnorm(x, g):
    xf = x.astype(jnp.float32)
    y = xf * lax.rsqrt(jnp.mean(xf * xf, axis=-1, keepdims=True) + EPS)
    return (y * g.astype(jnp.float32)).astype(x.dtype)


def _layernorm(x, g, b):
    xf = x.astype(jnp.float32)
    mu = jnp.mean(xf, axis=-1, keepdims=True)
    var = jnp.mean(jnp.square(xf - mu), axis=-1, keepdims=True)
    y = (xf - mu) * lax.rsqrt(var + EPS)
    return (y * g.astype(jnp.float32) + b.astype(jnp.float32)).astype(x.dtype)


def _split(t, widths):
    cuts, acc = [], 0
    for w in widths[:-1]:
        acc += w
        cuts.append(acc)
    return jnp.split(t, cuts, axis=-1)


def _modulation(cvec, w_mod, b_mod):
    h = jax.nn.silu(cvec) @ w_mod + b_mod
    shift, scale, gate = jnp.split(h[:, None, :], 3, axis=-1)
    return shift, scale, gate


def _pre(x, g, shift, scale):
    return _rmsnorm(x, g) * (1 + scale) + shift


def _dwconv(u, w):
    pad = (w.shape[0] - 1) // 2
    return lax.conv_general_dilated(
        u, w[:, None, :].astype(u.dtype), window_strides=(1,), padding=[(pad, pad)],
        dimension_numbers=('NWC', 'WIO', 'NWC'), feature_group_count=u.shape[-1])


def _fourier_mix(v, fw):
    b, l, _ = v.shape
    vg = v.reshape(b, l, B_GROUPS, B_GROUP_DIM).astype(jnp.float32)
    f = jnp.fft.fftn(vg, axes=(1, 3), norm='ortho').real.astype(v.dtype)
    return jnp.einsum('blgd,gde->blge', f, fw).reshape(b, l, B_WIDTH)


def _even_mix(h, w_in, a_conv_w, a_conv_b, a_ln_g, a_ln_b, a_pw_w, a_pw_b, b_fw, w_out):
    a_val, a_glu, a_gate, b_val, b_gate = _split(h @ w_in, [A_WIDTH, A_WIDTH, A_WIDTH, B_WIDTH, B_WIDTH])
    u = a_val * jax.nn.sigmoid(a_glu)
    u = jax.nn.silu(_layernorm(_dwconv(u, a_conv_w) + a_conv_b, a_ln_g, a_ln_b))
    y_a = (u @ a_pw_w + a_pw_b) * jax.nn.silu(a_gate)
    y_b = _fourier_mix(b_val, b_fw) * jax.nn.silu(b_gate)
    return jnp.concatenate([y_a, y_b], axis=-1) @ w_out


def _rope_tables(row, col):
    nf = HEAD_DIM // 4
    inv = jnp.power(ROPE_BASE, -jnp.arange(nf, dtype=jnp.float32) / nf)
    ang = jnp.stack([row.astype(jnp.float32)[:, None] * inv,
                     col.astype(jnp.float32)[:, None] * inv], axis=1)
    return jnp.cos(ang), jnp.sin(ang)


def _apply_rope(t, cos, sin):
    b, l, h, d = t.shape
    tr = t.astype(jnp.float32).reshape(b, l, h, 2, 2, d // 4)
    t1, t2 = tr[..., 0, :], tr[..., 1, :]
    cs, sn = cos[None, :, None], sin[None, :, None]
    out = jnp.stack([t1 * cs - t2 * sn, t2 * cs + t1 * sn], axis=-2)
    return out.reshape(b, l, h, d).astype(t.dtype)


def _sink_softmax(logits, sink):
    m = sink
    for s in logits:
        m = jnp.maximum(m, jnp.max(s, axis=-1, keepdims=True))
    ps = [jnp.exp(s - m) for s in logits]
    denom = jnp.exp(sink - m)
    for p in ps:
        denom = denom + jnp.sum(p, axis=-1, keepdims=True)
    return [p / denom for p in ps]


def _window_attention(q, k, v, kc, vc, sink):
    b, l, hq, hd = q.shape
    nb = l // BLOCK
    scale = HEAD_DIM ** -0.5
    qb = q.reshape(b, nb, BLOCK, N_KV_HEADS, Q_GROUP, hd)

    def band(t):
        tp = jnp.pad(t, ((0, 0), (BLOCK, BLOCK), (0, 0), (0, 0))).reshape(b, nb + 2, BLOCK, N_KV_HEADS, hd)
        return jnp.concatenate([tp[:, :-2], tp[:, 1:-1], tp[:, 2:]], axis=2)

    kw, vw = band(k), band(v)
    qpos = jnp.arange(l).reshape(nb, BLOCK)
    kpos = jnp.arange(-BLOCK, l + BLOCK).reshape(nb + 2, BLOCK)
    kwpos = jnp.concatenate([kpos[:-2], kpos[1:-1], kpos[2:]], axis=1)
    valid = ((jnp.abs(qpos[:, :, None] - kwpos[:, None, :]) <= WINDOW)
             & (kwpos >= 0)[:, None, :] & (kwpos < l)[:, None, :])
    s_win = jnp.einsum('bnqhgd,bnkhd->bnhgqk', qb, kw).astype(jnp.float32) * scale
    s_win = jnp.where(valid[None, :, None, None], s_win, NEG_INF)
    s_ctx = jnp.einsum('bnqhgd,bchd->bnhgqc', qb, kc).astype(jnp.float32) * scale
    sink_b = sink.astype(jnp.float32).reshape(N_KV_HEADS, Q_GROUP)[None, None, :, :, None, None]
    p_win, p_ctx = _sink_softmax([s_win, s_ctx], sink_b)
    o = (jnp.einsum('bnhgqk,bnkhd->bnqhgd', p_win.astype(v.dtype), vw)
         + jnp.einsum('bnhgqc,bchd->bnqhgd', p_ctx.astype(v.dtype), vc))
    return o.reshape(b, l, hq * hd)


def _context_attention(q, k, v, sink):
    b, lc, hq, hd = q.shape
    qg = q.reshape(b, lc, N_KV_HEADS, Q_GROUP, hd)
    s = jnp.einsum('bqhgd,bkhd->bhgqk', qg, k).astype(jnp.float32) * (HEAD_DIM ** -0.5)
    sink_b = sink.astype(jnp.float32).reshape(N_KV_HEADS, Q_GROUP)[None, :, :, None, None]
    (p,) = _sink_softmax([s], sink_b)
    o = jnp.einsum('bhgqk,bkhd->bqhgd', p.astype(v.dtype), v)
    return o.reshape(b, lc, hq * hd)


def _odd_proj(h, w_in):
    cb, cc, cx, cg, q, k, v, ag = _split(
        h @ w_in, [C_WIDTH, C_WIDTH, C_WIDTH, C_WIDTH, ATT_WIDTH, KV_WIDTH, KV_WIDTH, ATT_WIDTH])
    b, l, _ = h.shape
    return (cb, cc, cx, cg,
            q.reshape(b, l, N_Q_HEADS, HEAD_DIM),
            k.reshape(b, l, N_KV_HEADS, HEAD_DIM),
            v.reshape(b, l, N_KV_HEADS, HEAD_DIM), ag)


def _odd_out(cb, cc, cx, cg, att, ag, c_conv_w, w_out):
    y_c = cb * _dwconv(cc * cx, c_conv_w) * jax.nn.silu(cg)
    y_d = att * jax.nn.silu(ag)
    return jnp.concatenate([y_c, y_d], axis=-1) @ w_out


def setup_inputs(seed: int = 0) -> dict:
    key = jax.random.key(seed)
    ks = jax.random.split(key, 24)
    ne, no = (DEPTH + 1) // 2, DEPTH // 2

    def nrm(k, shape):
        return jax.random.normal(k, shape, jnp.float32)

    return {
        'x': nrm(ks[0], (BATCH, SEQ, D_MODEL)),
        'c': nrm(ks[1], (BATCH, D_MODEL)),
        'ctx': nrm(ks[2], (BATCH, CTX_LEN, D_MODEL)),
        'c_ctx': nrm(ks[3], (D_MODEL,)),
        'e_norm': 1.0 + 0.02 * nrm(ks[4], (ne, D_MODEL)),
        'e_w_mod': nrm(ks[5], (ne, D_MODEL, 3 * D_MODEL)) * (0.5 * D_MODEL ** -0.5),
        'e_b_mod': 0.02 * nrm(ks[6], (ne, 3 * D_MODEL)),
        'e_w_in': nrm(ks[7], (ne, D_MODEL, EVEN_IN)) * D_MODEL ** -0.5,
        'e_a_conv_w': nrm(ks[8], (ne, A_CONV_W, A_WIDTH)) * A_CONV_W ** -0.5,
        'e_a_conv_b': 0.02 * nrm(ks[9], (ne, A_WIDTH)),
        'e_a_ln_g': 1.0 + 0.02 * nrm(ks[10], (ne, A_WIDTH)),
        'e_a_ln_b': 0.02 * nrm(ks[11], (ne, A_WIDTH)),
        'e_a_pw_w': nrm(ks[12], (ne, A_WIDTH, A_WIDTH)) * A_WIDTH ** -0.5,
        'e_a_pw_b': 0.02 * nrm(ks[13], (ne, A_WIDTH)),
        'e_b_fw': nrm(ks[14], (ne, B_GROUPS, B_GROUP_DIM, B_GROUP_DIM)) * B_GROUP_DIM ** -0.5,
        'e_w_out': nrm(ks[15], (ne, EVEN_MIX, D_MODEL)) * EVEN_MIX ** -0.5,
        'o_norm': 1.0 + 0.02 * nrm(ks[16], (no, D_MODEL)),
        'o_w_mod': nrm(ks[17], (no, D_MODEL, 3 * D_MODEL)) * (0.5 * D_MODEL ** -0.5),
        'o_b_mod': 0.02 * nrm(ks[18], (no, 3 * D_MODEL)),
        'o_w_in': nrm(ks[19], (no, D_MODEL, ODD_IN)) * D_MODEL ** -0.5,
        'o_c_conv_w': nrm(ks[20], (no, C_CONV_W, C_WIDTH)) * C_CONV_W ** -0.5,
        'o_sink': 0.5 * nrm(ks[21], (no, N_Q_HEADS)),
        'o_w_out': nrm(ks[22], (no, ODD_MIX, D_MODEL)) * ODD_MIX ** -0.5,
        'final_norm': 1.0 + 0.02 * nrm(ks[23], (D_MODEL,)),
    }


def reference(x, c, ctx, c_ctx, e_norm, e_w_mod, e_b_mod, e_w_in, e_a_conv_w, e_a_conv_b,
              e_a_ln_g, e_a_ln_b, e_a_pw_w, e_a_pw_b, e_b_fw, e_w_out,
              o_norm, o_w_mod, o_b_mod, o_w_in, o_c_conv_w, o_sink, o_w_out, final_norm):
    l = x.shape[1]
    rows = l // GRID_W
    row = jnp.broadcast_to(jnp.arange(rows)[:, None], (rows, GRID_W)).reshape(l)
    col = jnp.broadcast_to(jnp.arange(GRID_W)[None, :], (rows, GRID_W)).reshape(l)
    cos, sin = _rope_tables(row, col)
    c_ctx2 = c_ctx[None, :]
    xc = ctx
    for layer in range(DEPTH):
        i = layer // 2
        ctx_needed = layer < DEPTH - 1
        if layer % 2 == 0:
            args = (e_w_in[i], e_a_conv_w[i], e_a_conv_b[i], e_a_ln_g[i], e_a_ln_b[i],
                    e_a_pw_w[i], e_a_pw_b[i], e_b_fw[i], e_w_out[i])
            sh, sc, gt = _modulation(c, e_w_mod[i], e_b_mod[i])
            x_new = x + gt * _even_mix(_pre(x, e_norm[i], sh, sc), *args)
            if ctx_needed:
                sh_c, sc_c, gt_c = _modulation(c_ctx2, e_w_mod[i], e_b_mod[i])
                xc = xc + gt_c * _even_mix(_pre(xc, e_norm[i], sh_c, sc_c), *args)
            x = x_new
        else:
            sh, sc, gt = _modulation(c, o_w_mod[i], o_b_mod[i])
            sh_c, sc_c, gt_c = _modulation(c_ctx2, o_w_mod[i], o_b_mod[i])
            cb_c, cc_c, cx_c, cg_c, q_c, k_c, v_c, ag_c = _odd_proj(_pre(xc, o_norm[i], sh_c, sc_c), o_w_in[i])
            cb, cc, cx, cg, q, k, v, ag = _odd_proj(_pre(x, o_norm[i], sh, sc), o_w_in[i])
            att = _window_attention(_apply_rope(q, cos, sin), _apply_rope(k, cos, sin), v, k_c, v_c, o_sink[i])
            x_new = x + gt * _odd_out(cb, cc, cx, cg, att, ag, o_c_conv_w[i], o_w_out[i])
            if ctx_needed:
                att_c = _context_attention(q_c, k_c, v_c, o_sink[i])
                xc = xc + gt_c * _odd_out(cb_c, cc_c, cx_c, cg_c, att_c, ag_c, o_c_conv_w[i], o_w_out[i])
            x = x_new
    return _rmsnorm(x, final_norm)
```

```python
import math
import numpy as np
import ml_dtypes
import concourse.bass as bass
import concourse.mybir as mybir
from concourse.bass_utils import run_bass_kernel_spmd

F32 = mybir.dt.float32
BF16 = mybir.dt.bfloat16
U8 = mybir.dt.uint8
AF = mybir.ActivationFunctionType
ALU = mybir.AluOpType
AX = mybir.AxisListType
NPBF = ml_dtypes.bfloat16

D = 1024
KC = 8
EPS = 1e-6
NEG = -1e30
UPAD = 16
NCS = 4
NST = 4
ARENA = 166 * 1024


class Buf:
    __slots__ = ("w", "r")

    def __init__(self):
        self.w = None
        self.r = {}


class TT:
    def __init__(self, ap):
        self.ap = ap
        self.bufs = {}

    def b(self, key=0):
        x = self.bufs.get(key)
        if x is None:
            x = self.bufs[key] = Buf()
        return x


class Sched:
    NRING = 20

    def __init__(self, nc):
        self.nc = nc
        self.eng = dict(pe=nc.tensor, dve=nc.vector, act=nc.scalar, pool=nc.gpsimd, sp=nc.sync)
        self.nk = 0
        self.sem = {}
        self.semk = {}
        self.cnt = {}
        self.seen = {}
        for n in self.eng:
            self.sem[n] = nc.alloc_semaphore("e_" + n)
            self.semk[n] = self._k()
            self.cnt[n] = 0
            self.seen[n] = {}
        self.ring = {}
        self.ringk = {}
        self.rval = {}
        self.rpos = {}
        for q in ("sp", "pool"):
            self.ring[q] = [nc.alloc_semaphore("q_%s_%d" % (q, i)) for i in range(self.NRING)]
            self.ringk[q] = [self._k() for _ in range(self.NRING)]
            self.rval[q] = [0] * self.NRING
            self.rpos[q] = 0

    def _k(self):
        self.nk += 1
        return self.nk

    def _wait(self, e, tok):
        k, sem, val, src = tok
        if src == "pe" and e == "pe":
            return
        if self.seen[e].get(k, 0) >= val:
            return
        self.eng[e].wait_ge(sem, val)
        self.seen[e][k] = val

    def _deps(self, e, R, W):
        for b in R:
            if b.w is not None:
                self._wait(e, b.w)
        for b in W:
            if b.w is not None:
                self._wait(e, b.w)
            for t in b.r.values():
                self._wait(e, t)

    @staticmethod
    def _commit(tok, R, W):
        for b in R:
            b.r[tok[0]] = tok
        for b in W:
            b.w = tok
            b.r = {}

    def op(self, e, fn, R=(), W=()):
        self._deps(e, R, W)
        inst = fn(self.eng[e])
        self.cnt[e] += 1
        inst.then_inc(self.sem[e], 1)
        self._commit((self.semk[e], self.sem[e], self.cnt[e], e), R, W)

    def dma(self, q, out, in_, R=(), W=(), **kw):
        self._deps(q, R, W)
        i = self.rpos[q]
        self.rpos[q] = (i + 1) % self.NRING
        sem = self.ring[q][i]
        k = self.ringk[q][i]
        prev = self.rval[q][i]
        if prev:
            self._wait(q, (k, sem, prev, "dma"))
        inst = self.eng[q].dma_start(out=out, in_=in_, **kw)
        inst.then_inc(sem, 16)
        self.rval[q][i] = prev + 16
        self._commit((k, sem, prev + 16, "dma"), R, W)

    def barrier(self):
        toks = []
        for n in self.eng:
            if self.cnt[n] > 0:
                toks.append((self.semk[n], self.sem[n], self.cnt[n], n))
        for q in self.ring:
            for i in range(self.NRING):
                if self.rval[q][i] > 0:
                    toks.append((self.ringk[q][i], self.ring[q][i], self.rval[q][i], "dma"))
        for e in self.eng:
            for t in toks:
                if t[3] == e and e in ("pe", "sp"):
                    continue
                self._wait(e, t)

    def mm(self, out, lhsT, rhs, start, stop, R, W):
        self.op("pe", lambda e: e.matmul(out, lhsT, rhs, start=start, stop=stop), R, W)

    def tr(self, out, in_, ident, R, W):
        self.op("pe", lambda e: e.transpose(out, in_, ident), R, W)

    def act(self, out, in_, func, R, W, bias=None, scale=None, accum=None):
        kw = {}
        if bias is not None:
            kw["bias"] = bias
        if scale is not None:
            kw["scale"] = scale
        if accum is not None:
            kw["accum_out"] = accum
        self.op("act", lambda e: e.activation(out, in_, func, **kw), R, W)

    def tt(self, out, in0, in1, op, R, W, e="dve"):
        self.op(e, lambda g: g.tensor_tensor(out, in0, in1, op), R, W)

    def ts(self, out, in0, s1, s2, op0, op1, R, W, e="dve"):
        self.op(e, lambda g: g.tensor_scalar(out, in0, s1, s2, op0, op1), R, W)

    def stt(self, out, in0, scalar, in1, op0, op1, R, W):
        self.op("dve", lambda g: g.scalar_tensor_tensor(out, in0, scalar, in1, op0, op1), R, W)

    def cp(self, out, in_, R, W, e="dve"):
        if e == "act":
            self.op("act", lambda g: g.copy(out, in_), R, W)
        else:
            self.op(e, lambda g: g.tensor_copy(out, in_), R, W)


class Arena:
    def __init__(self, ap_u8, size):
        self.ap = ap_u8
        self.size = size
        self.off = 0
        self.top = size

    def reset(self):
        self.off = 0
        self.top = self.size

    def alloc_top(self, shape, dtype):
        esz = 4 if dtype == F32 else 2
        n = esz
        for d in shape[1:]:
            n *= d
        off = (self.top - n) // 64 * 64
        save = self.off
        self.off = off
        t = self.alloc(shape, dtype)
        self.off = save
        self.top = off
        assert self.off <= self.top
        return t

    def alloc(self, shape, dtype):
        esz = 4 if dtype == F32 else 2
        n = esz
        for d in shape[1:]:
            n *= d
        off = (self.off + 63) // 64 * 64
        assert off + n <= self.top, ("arena overflow", off + n, self.top)
        v = self.ap[:, off:off + n].bitcast(dtype)
        if len(shape) == 3:
            v = v.rearrange("p (a b) -> p a b", a=shape[1])
        elif len(shape) == 4:
            v = v.rearrange("p (a b c) -> p a b c", a=shape[1], b=shape[2])
        self.off = off + n
        return TT(v)


EVEN_IN = 2560
ODD_IN = 3328 + 128


def build(L, LC, layers, dbg=False):
    T = L + LC
    NTX = L // 128
    NTC = LC // 128
    NTILES = NTX + NTC
    NKB = L // 512
    NSG = NTX // NCS
    UW = T + 4 * UPAD

    def ucol(t):
        return t + UPAD if t < L else t + 3 * UPAD

    blocks = [(0, kb * 512, 512) for kb in range(NKB)] + [(1, L, LC)]
    NB = len(blocks)

    nc = bass.Bass("TRN2", target_bir_lowering=False)
    s = Sched(nc)

    def din(name, shape, dt=F32):
        return nc.dram_tensor(name, list(shape), dt, kind="ExternalInput").ap()

    x_d = din("x", [L, D])
    ctx_d = din("ctx", [LC, D])
    ccols_d = din("ccols", [128, KC, 2])
    fin_d = din("fin", [128, KC])
    finbc_d = din("finbc", [128, D])
    csd_d = din("csd", [128, 256], BF16)
    NSG2 = NTX // 8
    dfth_d = din("dfth", [NKB, NSG2, 128, 2, 8, 256], BF16)
    alt_d = din("alt", [128, 2], BF16)
    dftc_d = din("dftc", [128, 2, NTC, LC], BF16)
    ropec_d = din("ropec", [128, L])
    ropes_d = din("ropes", [128, L])
    ropecq_d = din("ropecq", [128, L])
    ropesq_d = din("ropesq", [128, L])
    mask_d = din("mask", [128, 384], BF16)
    identb_d = din("identb", [128, 128], BF16)
    identf_d = din("identf", [128, 128])
    LW = {}
    for l in layers:
        w = {}
        w["wmod"] = din("wmod%d" % l, [D, 3 * D])
        w["bmod"] = din("bmod%d" % l, [128, 24])
        w["norm"] = din("norm%d" % l, [128, KC])
        w["wout"] = din("wout%d" % l, [D, D])
        if l % 2 == 0:
            w["win"] = din("win%d" % l, [D, EVEN_IN])
            w["cw"] = din("cw%d" % l, [128, 4, 31])
            w["v4"] = din("v4%d" % l, [128, 4, 4])
            w["pw"] = din("pw%d" % l, [512, 512])
            w["fw"] = din("fw%d" % l, [4, 128, 128])
        else:
            w["win"] = din("win%d" % l, [D, ODD_IN])
            w["cw"] = din("cw%d" % l, [128, 4, 3])
            w["sink"] = din("sink%d" % l, [128, 8])
        LW[l] = w
    out_d = nc.dram_tensor("out", [L, D], F32, kind="ExternalOutput").ap()

    def dscr(name, shape, dt):
        return TT(nc.dram_tensor(name, list(shape), dt, kind="Internal").ap())

    xres = dscr("xres", [128, KC, T], F32)
    pqd = dscr("pqd", [NTILES, 128, 1024], BF16)
    sA = dscr("sA", [128, 4, T], BF16)
    sB = dscr("sB", [128, 4, T], BF16)
    sC = dscr("sC", [128, 4, T], BF16)
    sD = dscr("sD", [128, 4, T], BF16)

    def sb(name, shape, dt):
        return TT(nc.alloc_sbuf_tensor(name, list(shape), dt).ap())

    identb = sb("identb_s", [128, 128], BF16)
    identf = sb("identf_s", [128, 128], F32)
    onesb = sb("onesb", [128, 128], BF16)
    ones512 = sb("ones512", [128, 128], BF16)
    epsc = sb("epsc", [128, 1], F32)
    csd = sb("csd_s", [128, 256], BF16)
    maskt = sb("mask_s", [128, 384], BF16)
    dftc = sb("dftc_s", [128, 2, NTC, LC], BF16)
    ccols = sb("ccols_s", [128, KC, 2], F32)
    silc = sb("silc", [128, KC, 2], BF16)
    fin = sb("fin_s", [128, KC], F32)
    X = [sb("X%d" % i, [128, KC, 512], F32) for i in range(2)]
    modc = {l: sb("modc%d" % l, [128, 24, 2], F32) for l in layers}
    Gc = {l: sb("G%d" % l, [128, KC, 2], F32) for l in layers}
    Sbf = {l: sb("Sbf%d" % l, [128, KC, 2], BF16) for l in layers}
    biasW = sb("biasW", [128, 28, 2], F32)
    normc = sb("normc", [128, KC], F32)
    bmodc = sb("bmodc", [128, 24], F32)
    v4 = sb("v4_s", [128, 4, 4], F32)
    cwc = sb("cwc", [128, 4, 31], F32)
    sinkc = sb("sinkc", [128, 8], F32)
    nsinkc = sb("nsinkc", [128, 8], F32)
    arena_t = nc.alloc_sbuf_tensor("arena", [128, ARENA], U8).ap()
    A = Arena(arena_t, ARENA)

    PS = nc.alloc_psum_tensor("ps", [128, 4096], F32).ap()
    bkb = [Buf() for _ in range(8)]

    def bank(i):
        return PS[:, i * 512:(i + 1) * 512], bkb[i]

    ld = lambda out, in_, R, W: s.dma("sp", out, in_, R, W)
    st = lambda out, in_, R, W: s.dma("pool", out, in_, R, W)
    ldc = lambda out, in_, R, W: s.dma("pool", out, in_, R, W)

    ld(identb.ap, identb_d, [], [identb.b()])
    ld(identf.ap, identf_d, [], [identf.b()])
    ld(csd.ap, csd_d, [], [csd.b()])
    ld(maskt.ap, mask_d, [], [maskt.b()])
    ld(dftc.ap, dftc_d, [], [dftc.b()])
    ld(ccols.ap, ccols_d, [], [ccols.b()])
    ld(fin.ap, fin_d, [], [fin.b()])
    s.op("dve", lambda e: e.memset(onesb.ap, 1.0), [], [onesb.b()])
    s.op("dve", lambda e: e.memset(ones512.ap, 1.0 / 512.0), [], [ones512.b()])
    s.op("dve", lambda e: e.memset(epsc.ap, EPS), [], [epsc.b()])
    s.act(silc.ap, ccols.ap, AF.Silu, [ccols.b()], [silc.b()])

    def mod_calls(l, stage_ap, stage_buf, pbank=7):
        w = LW[l]
        wsrc = w["wmod"].rearrange("(kc p) n -> p kc n", p=128)
        mc = modc[l]

        def load(part):
            ldc(stage_ap, wsrc[:, :, part * D:(part + 1) * D], [], [stage_buf])

        def mm(part):
            pm, pmb = bank(pbank)
            for m in range(8):
                for c in range(KC):
                    s.mm(pm[:, 2 * m:2 * m + 2], stage_ap[:, c, m * 128:(m + 1) * 128],
                         silc.ap[:, c, :], c == 0, c == KC - 1, [stage_buf, silc.b()], [pmb])
            s.tt(mc.ap[:, part * 8:(part + 1) * 8, :], pm[:, 0:16].rearrange("p (a b) -> p a b", a=8),
                 bmodc.ap[:, part * 8:(part + 1) * 8].unsqueeze(2).broadcast_to([128, 8, 2]), ALU.add,
                 [pmb, bmodc.b()], [mc.b()])

        def c0():
            ld(bmodc.ap, w["bmod"], [], [bmodc.b()])
            ld(normc.ap, w["norm"], [], [normc.b()])
            load(0)

        def c1():
            mm(0)
            load(1)

        def c2():
            mm(1)
            load(2)

        def c3():
            mm(2)
            s.stt(Gc[l].ap, mc.ap[:, 8:16, :], 1.0, normc.ap.unsqueeze(2).broadcast_to([128, KC, 2]),
                  ALU.add, ALU.mult, [mc.b(), normc.b()], [Gc[l].b()])
            s.cp(Sbf[l].ap, mc.ap[:, 0:8, :], [mc.b()], [Sbf[l].b()])
        return [c0, c1, c2, c3]

    def win_dma(l, Win, nin):
        w = LW[l]
        wsrc = w["win"].rearrange("(kc p) n -> p kc n", p=128)
        c0 = 0
        while c0 < nin:
            cw_ = min(1024, nin - c0)
            ldc(Win.ap[:, :, c0:c0 + cw_], wsrc[:, :, c0:c0 + cw_], [], [Win.b(c0 // 1024)])
            c0 += cw_

    def win_bias(l, Win, nin):
        nchunk = nin // 128
        pm, pmb = bank(0)
        for p0_ in range(0, nchunk, 8):
            p1_ = min(p0_ + 8, nchunk)
            for m in range(p0_, p1_):
                for c in range(KC):
                    s.mm(pm[:, 2 * m:2 * m + 2], Win.ap[:, c, m * 128:(m + 1) * 128], Sbf[l].ap[:, c, :],
                         c == 0, c == KC - 1, [Win.b(m // 8), Sbf[l].b()], [pmb])
            s.cp(biasW.ap[:, p0_:p1_, :], pm[:, 2 * p0_:2 * p1_].rearrange("p (a b) -> p a b", b=2),
                 [pmb], [biasW.b(p0_ // 8)])

    def front(l, j, hT, sq, rt, from_dram=True, gcol=None, fb=(0, 1), xb=None):
        stx, t0, nt = blocks[j]
        if xb is None:
            xb = X[j % 2]
        if from_dram:
            ld(xb.ap[:, :, :nt], xres.ap[:, :, t0:t0 + nt], [xres.b(j)], [xb.b()])
        s.act(sq.ap[:, :, :nt], xb.ap[:, :, :nt], AF.Square, [xb.b()], [sq.b()])
        p0, p0b = bank(fb[0])
        p1, p1b = bank(fb[1])
        for c in range(KC):
            s.mm(p0[:, :nt], onesb.ap, sq.ap[:, c, :nt], c == 0, c == KC - 1, [sq.b(), onesb.b()], [p0b])
        s.act(rt.ap[:, :nt], p0[:, :nt], AF.Sqrt, [p0b, epsc.b()], [rt.b()], bias=epsc.ap, scale=1.0 / D)
        s.op("dve", lambda e: e.reciprocal(p1[:, :nt], rt.ap[:, :nt]), [rt.b()], [p1b])
        for c in range(KC):
            g = Gc[l].ap[:, c, stx:stx + 1] if gcol is None else gcol[:, c:c + 1]
            s.stt(hT.ap[:, c, :nt], xb.ap[:, c, :nt], g, p1[:, :nt], ALU.mult, ALU.mult,
                  [xb.b(), p1b, Gc[l].b() if gcol is None else fin.b()], [hT.b()])

    def front_parts(l, j, hT, sq, rt, from_dram=True):
        stx, t0, nt = blocks[j]
        xb = X[j % 2]
        p0, p0b = bank(0)
        p1, p1b = bank(1)

        def pre():
            if from_dram:
                ld(xb.ap[:, :, :nt], xres.ap[:, :, t0:t0 + nt], [xres.b(j)], [xb.b()])
            s.act(sq.ap[:, :, :nt], xb.ap[:, :, :nt], AF.Square, [xb.b()], [sq.b()])

        def mid():
            for c in range(KC):
                s.mm(p0[:, :nt], onesb.ap, sq.ap[:, c, :nt], c == 0, c == KC - 1, [sq.b(), onesb.b()], [p0b])
            s.act(rt.ap[:, :nt], p0[:, :nt], AF.Sqrt, [p0b, epsc.b()], [rt.b()], bias=epsc.ap, scale=1.0 / D)
            s.op("dve", lambda e: e.reciprocal(p1[:, :nt], rt.ap[:, :nt]), [rt.b()], [p1b])

        def mk(c):
            def f():
                s.stt(hT.ap[:, c, :nt], xb.ap[:, c, :nt], Gc[l].ap[:, c, stx:stx + 1], p1[:, :nt], ALU.mult, ALU.mult,
                      [xb.b(), p1b, Gc[l].b()], [hT.b()])
            return f
        return {0: [pre], 3: [mid], **{4 + c: [mk(c)] for c in range(KC)}}

    class Ticker:
        def __init__(self):
            self.parts = {}
            self.k = 0

        def arm(self, parts):
            self.flush()
            self.parts = dict(parts)
            self.k = 0

        def tick(self):
            for fn in self.parts.pop(self.k, []):
                fn()
            self.k += 1

        def flush(self):
            for k_ in sorted(self.parts):
                for fn in self.parts[k_]:
                    fn()
            self.parts = {}

    def lin(pb, Win, col0, hT, nt, extraR=()):
        p, b = pb
        for c in range(KC):
            s.mm(p[:, :nt], Win.ap[:, c, col0:col0 + 128], hT.ap[:, c, :nt], c == 0, c == KC - 1,
                 [Win.b(col0 // 1024), hT.b()] + list(extraR), [b])

    init_n = [0]

    def init_block(j, TI):
        stx, t0, nt = blocks[j]
        xb = X[j % 2]
        for i in range(nt // 128):
            n = init_n[0]
            ti = TI[n % 2]
            init_n[0] += 1
            src_ = x_d[t0 + i * 128:t0 + (i + 1) * 128, :] if stx == 0 else \
                ctx_d[t0 - L + i * 128:t0 - L + (i + 1) * 128, :]
            ld(ti.ap, src_, [], [ti.b()])
            for h in range(2):
                pt, ptb = bank(h)
                for c in range(4):
                    cc = h * 4 + c
                    s.tr(pt[:, c * 128:(c + 1) * 128], ti.ap[:, cc * 128:(cc + 1) * 128], identf.ap,
                         [ti.b(), identf.b()], [ptb])
                s.cp(xb.ap[:, h * 4:(h + 1) * 4, i * 128:(i + 1) * 128],
                     pt.rearrange("p (a b) -> p a b", a=4), [ptb], [xb.b()],
                     e="act" if h == 0 else "dve")
        ld(xres.ap[:, :, t0:t0 + nt], xb.ap[:, :, :nt], [xb.b()], [xres.b(j)])

    def out_stage(l, j, Wout, mix, last, fin_bufs, banks=(4, 5, 6, 7), xb=None):
        stx, t0, nt = blocks[j]
        if xb is None:
            xb = X[j % 2]
        for m in range(KC):
            pb = bank(banks[m % len(banks)])
            p, b = pb
            for c in range(KC):
                s.mm(p[:, :nt], Wout.ap[:, c, m * 128:(m + 1) * 128], mix.ap[:, c, :nt], c == 0, c == KC - 1,
                     [Wout.b(), mix.b()], [b])
            s.stt(xb.ap[:, m, :nt], p[:, :nt], modc[l].ap[:, 16 + m, stx:stx + 1], xb.ap[:, m, :nt],
                  ALU.mult, ALU.add, [b, modc[l].b(), xb.b()], [xb.b()])
            if last and stx == 0:
                s.act(fin_bufs[1].ap[:, m, :nt], xb.ap[:, m, :nt], AF.Square, [xb.b()], [fin_bufs[1].b()])
        if not last:
            st(xres.ap[:, :, t0:t0 + nt], xb.ap[:, :, :nt], [xb.b()], [xres.b(j)])
        elif stx == 0:
            finbc, sq, fst, TO = fin_bufs
            if not finbc.bufs:
                ld(finbc.ap, finbc_d, [], [finbc.b()])
            nti = nt // 128
            pss, pssb = bank(banks[-1])
            for i in range(nti):
                for c in range(KC):
                    s.mm(pss[:, 2 * i:2 * i + 2], sq.ap[:, c, i * 128:(i + 1) * 128], onesb.ap[:, 0:2],
                         c == 0, c == KC - 1, [sq.b(), onesb.b()], [pssb])
            s.act(fst.ap[:, 0:2 * nti], pss[:, 0:2 * nti], AF.Ln, [pssb, epsc.b()], [fst.b()],
                  bias=epsc.ap, scale=1.0 / D)
            s.act(fst.ap[:, 8:8 + 2 * nti], fst.ap[:, 0:2 * nti], AF.Exp, [fst.b()], [fst.b()], scale=-0.5)
            for i in range(nti):
                to = TO[i % 2]
                for h in range(2):
                    pt, ptb = bank(banks[-2 + h])
                    for c in range(4):
                        cc = h * 4 + c
                        s.tr(pt[:, c * 128:(c + 1) * 128], xb.ap[:, cc, i * 128:(i + 1) * 128], identf.ap,
                             [xb.b(), identf.b()], [ptb])
                    s.stt(to.ap[:, h * 512:(h + 1) * 512], pt, fst.ap[:, 8 + 2 * i:9 + 2 * i],
                          finbc.ap[:, h * 512:(h + 1) * 512], ALU.mult, ALU.mult,
                          [ptb, fst.b(), finbc.b()], [to.b()])
                st(out_d[t0 + i * 128:t0 + (i + 1) * 128, :], to.ap, [to.b()], [])

    def even_layer(l, last, nxt, first=False, win_pre=None):
        w = LW[l]
        if not first:
            s.barrier()
        A.reset()
        Win = A.alloc_top([128, KC, EVEN_IN], BF16)
        if win_pre is not None:
            Win.bufs = win_pre.bufs
        U = A.alloc([128, 4, UW], BF16)
        hT = [A.alloc([128, KC, 512], BF16) for _ in range(2)]
        sq = A.alloc([128, KC, 512], BF16)
        rt = A.alloc([128, 512], F32)
        sig = [A.alloc([128, 512], F32) for _ in range(2)]
        gAo = [A.alloc([128, 4, 512], BF16) for _ in range(2)]
        gBo = [A.alloc([128, 4, 512], BF16) for _ in range(2)]
        bvT = [A.alloc([128, 4, 512], BF16) for _ in range(2)]
        pqo = [A.alloc([128, 1024], BF16) for _ in range(2)]
        TI = None
        if first:
            TI = [A.alloc([128, D], F32) for _ in range(2)]
            wst0 = A.alloc([128, KC, D], BF16)
            for fn in mod_calls(l, wst0.ap, wst0.b(), pbank=7):
                fn()
        if win_pre is None:
            win_dma(l, Win, EVEN_IN)
        ld(v4.ap, w["v4"], [], [v4.b()])
        ld(cwc.ap, w["cw"], [], [cwc.b()])
        s.op("pool", lambda e: e.memset(U.ap, 0.0), [], [U.b(j) for j in range(NB)])
        if first:
            init_block(0, TI)
        front(l, 0, hT[0], sq, rt, from_dram=not first)
        win_bias(l, Win, EVEN_IN)
        nbk = 0
        ntile = 0
        tk = Ticker()
        for j, (stx, t0, nt) in enumerate(blocks):
            if j + 1 < NB:
                if first:
                    init_block(j + 1, TI)
                tk.arm(front_parts(l, j + 1, hT[(j + 1) % 2], sq, rt, from_dram=not first))
            tk.tick()
            h = hT[j % 2]
            bW = lambda m: biasW.ap[:, m, stx:stx + 1]
            for ci in range(4):
                pg = bank(2 + nbk % 4); nbk += 1
                lin(pg, Win, (4 + ci) * 128, h, nt)
                sg = sig[ci % 2]
                s.act(sg.ap[:, :nt], pg[0][:, :nt], AF.Sigmoid, [pg[1], biasW.b((4 + ci) // 8)], [sg.b()], bias=bW(4 + ci))
                tk.tick()
                pv = bank(2 + nbk % 4); nbk += 1
                lin(pv, Win, ci * 128, h, nt)
                s.stt(U.ap[:, ci, ucol(t0):ucol(t0) + nt], pv[0][:, :nt], bW(ci), sg.ap[:, :nt], ALU.add, ALU.mult,
                      [pv[1], sg.b(), biasW.b((ci) // 8)], [U.b(j)])
                tk.tick()
            go = gAo[j % 2]
            for ci in range(4):
                p = bank(2 + nbk % 4); nbk += 1
                lin(p, Win, (8 + ci) * 128, h, nt)
                s.act(go.ap[:, ci, :nt], p[0][:, :nt], AF.Silu, [p[1], biasW.b((8 + ci) // 8)], [go.b()], bias=bW(8 + ci))
                tk.tick()
            st(sA.ap[:, :, t0:t0 + nt], go.ap[:, :, :nt], [go.b()], [sA.b(j)])
            tk.flush()
            bv = bvT[j % 2]
            for ci in range(4):
                p = bank(2 + nbk % 4); nbk += 1
                lin(p, Win, (12 + ci) * 128, h, nt)
                s.act(bv.ap[:, ci, :nt], p[0][:, :nt], AF.Identity, [p[1], biasW.b((12 + ci) // 8)], [bv.b()], bias=bW(12 + ci))
            for i in range(nt // 128):
                pb0 = 6 if ntile % 2 == 0 else 0
                pq = PS[:, pb0 * 512:(pb0 + 2) * 512]
                for g in range(4):
                    s.mm(pq[:, g * 256:(g + 1) * 256], bv.ap[:, g, i * 128:(i + 1) * 128], csd.ap, True, True,
                         [bv.b(), csd.b()], [bkb[pb0 + g // 2]])
                po = pqo[ntile % 2]
                s.cp(po.ap[:, 0:512], pq[:, 0:512], [bkb[pb0]], [po.b()], e="act")
                s.cp(po.ap[:, 512:1024], pq[:, 512:1024], [bkb[pb0 + 1]], [po.b()], e="dve")
                tile = (t0 // 128) + i
                st(pqd.ap[tile], po.ap, [po.b()], [pqd.b(tile)])
                ntile += 1
            go = gBo[j % 2]
            for ci in range(4):
                p = bank(2 + nbk % 4); nbk += 1
                lin(p, Win, (16 + ci) * 128, h, nt)
                s.act(go.ap[:, ci, :nt], p[0][:, :nt], AF.Silu, [p[1], biasW.b((16 + ci) // 8)], [go.b()], bias=bW(16 + ci))
            st(sB.ap[:, :, t0:t0 + nt], go.ap[:, :, :nt], [go.b()], [sB.b(j)])

        s.barrier()
        A.reset()
        U2 = A.alloc([128, 4, UW], BF16)
        U2.bufs = U.bufs
        diag = A.alloc([128, 4, 31, 128], BF16)
        pwt = A.alloc([128, 4, 512], BF16)
        v32 = [A.alloc([128, 4, 512], F32) for _ in range(2)]
        vb = [A.alloc([128, 4, 512], BF16) for _ in range(2)]
        vsq = [A.alloc([128, 4, 512], BF16) for _ in range(2)]
        m2 = A.alloc([128, 512], F32)
        var = A.alloc([128, 512], F32)
        sd = A.alloc([128, 512], F32)
        z = A.alloc([128, 4, 512], F32)
        u2 = A.alloc([128, 4, 512], BF16)
        gAi = [A.alloc([128, 4, 512], BF16) for _ in range(2)]
        mAo = [A.alloc([128, 4, 512], BF16) for _ in range(2)]
        ldc(pwt.ap, w["pw"].rearrange("(kc p) n -> p kc n", p=128), [], [pwt.b()])
        for c in range(4):
            for k in range(31):
                s.ts(diag.ap[:, c, k, :], identb.ap, cwc.ap[:, c, k:k + 1], None, ALU.mult, ALU.bypass,
                     [identb.b(), cwc.b()], [diag.b()])
        mcalls = mod_calls(nxt, X[1].ap.bitcast(BF16), X[1].b()) if nxt is not None else None
        if mcalls:
            mcalls[0]()

        tk2 = Ticker()

        def conv(j):
            stx, t0, nt = blocks[j]
            vv = j % 2
            gi = gAi[vv]
            ld(gi.ap[:, :, :nt], sA.ap[:, :, t0:t0 + nt], [sA.b(j)], [gi.b()])
            nbrs = [U2.b(jj) for jj in (j - 1, j, j + 1) if 0 <= jj < NB and blocks[jj][0] == stx]
            for c in range(4):
                pc = bank(c)
                for k in range(31):
                    c0 = ucol(t0) + k - 15
                    s.mm(pc[0][:, :nt], diag.ap[:, c, k, :], U2.ap[:, c, c0:c0 + nt], k == 0, k == 30,
                         [diag.b()] + nbrs, [pc[1]])
                s.act(v32[vv].ap[:, c, :nt], pc[0][:, :nt], AF.Identity, [pc[1], v4.b()], [v32[vv].b()],
                      bias=v4.ap[:, c, 0:1])
                s.act(vsq[vv].ap[:, c, :nt], pc[0][:, :nt], AF.Square, [pc[1], v4.b()], [vsq[vv].b()],
                      bias=v4.ap[:, c, 0:1])
                s.cp(vb[vv].ap[:, c, :nt], v32[vv].ap[:, c, :nt], [v32[vv].b()], [vb[vv].b()])
                tk2.tick()

        def lnpw_parts(j):
            stx, t0, nt = blocks[j]
            vv = j % 2
            gi = gAi[vv]
            pmean = bank(4)
            pmsq = bank(5)
            prs = bank(6)

            def p_stats():
                for c in range(4):
                    s.mm(pmean[0][:, :nt], ones512.ap, vb[vv].ap[:, c, :nt], c == 0, c == 3,
                         [ones512.b(), vb[vv].b()], [pmean[1]])
                for c in range(4):
                    s.mm(pmsq[0][:, :nt], ones512.ap, vsq[vv].ap[:, c, :nt], c == 0, c == 3,
                         [ones512.b(), vsq[vv].b()], [pmsq[1]])
                s.act(m2.ap[:, :nt], pmean[0][:, :nt], AF.Square, [pmean[1]], [m2.b()])
                s.tt(var.ap[:, :nt], pmsq[0][:, :nt], m2.ap[:, :nt], ALU.subtract, [pmsq[1], m2.b()], [var.b()])
                s.act(sd.ap[:, :nt], var.ap[:, :nt], AF.Sqrt, [var.b(), epsc.b()], [sd.b()], bias=epsc.ap, scale=1.0)
                s.op("dve", lambda e: e.reciprocal(prs[0][:, :nt], sd.ap[:, :nt]), [sd.b()], [prs[1]])

            def p_z():
                s.tt(z.ap[:, :, :nt], v32[vv].ap[:, :, :nt],
                     pmean[0][:, :nt].unsqueeze(1).broadcast_to([128, 4, nt]),
                     ALU.subtract, [v32[vv].b(), pmean[1]], [z.b()])
                s.tt(z.ap[:, :, :nt], z.ap[:, :, :nt], prs[0][:, :nt].unsqueeze(1).broadcast_to([128, 4, nt]),
                     ALU.mult, [z.b(), prs[1]], [z.b()])

            def p_silu():
                for c in range(4):
                    s.act(u2.ap[:, c, :nt], z.ap[:, c, :nt], AF.Silu, [z.b(), v4.b()], [u2.b()],
                          bias=v4.ap[:, c, 2:3], scale=v4.ap[:, c, 1:2])

            def p_pw():
                mo = mAo[vv]
                for m in range(4):
                    pp = bank(7 if m % 2 == 0 else 5)
                    for c in range(4):
                        s.mm(pp[0][:, :nt], pwt.ap[:, c, m * 128:(m + 1) * 128], u2.ap[:, c, :nt], c == 0, c == 3,
                             [pwt.b(), u2.b()], [pp[1]])
                    s.stt(mo.ap[:, m, :nt], pp[0][:, :nt], v4.ap[:, m, 3:4], gi.ap[:, m, :nt], ALU.add, ALU.mult,
                          [pp[1], v4.b(), gi.b()], [mo.b()])
                st(sC.ap[:, :, t0:t0 + nt], mo.ap[:, :, :nt], [mo.b()], [sC.b(j)])
            return {0: [p_stats], 1: [p_z], 2: [p_silu], 3: [p_pw]}

        conv(0)
        for j in range(NB):
            tk2.arm(lnpw_parts(j))
            if j + 1 < NB:
                conv(j + 1)
            tk2.flush()
            if mcalls and j in (NB // 3 - 1, 2 * NB // 3 - 1, NB - 1):
                mcalls[1 + [NB // 3 - 1, 2 * NB // 3 - 1, NB - 1].index(j)]()

        s.barrier()
        A.reset()
        PQ = A.alloc([128, NTILES, 1024], BF16)
        DS = [A.alloc([128, 2, 8, 256], BF16) for _ in range(4)]
        Wout = A.alloc([128, KC, D], BF16)
        fwt = A.alloc([128, 4, 128], BF16)
        Fr = [A.alloc([128, 4, 512], BF16) for _ in range(4)]
        Bs = A.alloc([128, 4, 256], F32)
        alt = A.alloc([128, 2], BF16)
        gBi = [A.alloc([128, 4, 512], BF16) for _ in range(2)]
        mix = [A.alloc([128, 8, 512], BF16) for _ in range(2)]
        fin_bufs = None
        if last:
            fin_bufs = (A.alloc([128, D], F32), A.alloc([128, KC, 512], BF16), A.alloc([128, 16], F32),
                        [A.alloc([128, D], F32) for _ in range(2)])
        stages = [(khb, sg) for khb in range(NKB - 1, -1, -1) for sg in range(NSG2)]
        ds_issued = [0]

        def issue_ds(upto):
            while ds_issued[0] <= min(upto, len(stages) - 1):
                i_ = ds_issued[0]
                ds_ = DS[i_ % 4]
                ld(ds_.ap, dfth_d[stages[i_][0], stages[i_][1]], [], [ds_.b()])
                ds_issued[0] += 1
        issue_ds(0)
        pq_src = pqd.ap.rearrange("t p n -> p t n")
        for t8 in range(0, NTILES, 4):
            n_ = min(4, NTILES - t8)
            ld(PQ.ap[:, t8:t8 + n_, :], pq_src[:, t8:t8 + n_, :], [pqd.b(t) for t in range(t8, t8 + n_)],
               [PQ.b(t8 // 4)])
            if (t8 // 4) % 2 == 1 and ds_issued[0] <= 3:
                issue_ds(ds_issued[0])
        issue_ds(3)
        ld(alt.ap, alt_d, [], [alt.b()])
        pq_last = [PQ.b((NTILES - 1) // 4)]
        wsrc = w["wout"].rearrange("(kc p) n -> p kc n", p=128)
        ldc(Wout.ap, wsrc, pq_last, [Wout.b()])
        ldc(fwt.ap, w["fw"].rearrange("g d e -> d g e"), pq_last, [fwt.b()])
        cnt = [0]
        nds = [0]

        def prefetch(j):
            stx, t0, nt = blocks[j]
            c_ = cnt[0]
            cnt[0] += 1
            xb = X[c_ % 2]
            st(xb.ap[:, :, :nt], xres.ap[:, :, t0:t0 + nt], [xres.b(j)] + pq_last, [xb.b()])
            mx = mix[c_ % 2]
            st(mx.ap[:, 0:4, :nt], sC.ap[:, :, t0:t0 + nt], [sC.b(j)] + pq_last, [mx.b()])
            gi = gBi[c_ % 2]
            st(gi.ap[:, :, :nt], sB.ap[:, :, t0:t0 + nt], [sB.b(j)] + pq_last, [gi.b()])
            return (j, xb, mx, gi)

        def downstream(pf_, Ft, banks):
            j, xb, mx, gi = pf_
            stx, t0, nt = blocks[j]
            for g in range(4):
                pf = bank(banks[g % len(banks)])
                s.mm(pf[0][:, :nt], fwt.ap[:, g, :], Ft.ap[:, g, :nt], True, True, [fwt.b(), Ft.b()], [pf[1]])
                s.tt(mx.ap[:, 4 + g, :nt], pf[0][:, :nt], gi.ap[:, g, :nt], ALU.mult, [pf[1], gi.b()], [mx.b()])
            out_stage(l, j, Wout, mx, last, fin_bufs, banks=banks, xb=xb)

        NPAIR = NKB // 2

        def nyquist():
            pny, pnyb = bank(4)
            for g in range(4):
                for tile in range(NTX):
                    s.mm(pny[:, 2 * g:2 * g + 2], PQ.ap[:, tile, g * 256:g * 256 + 128], alt.ap, tile == 0,
                         tile == NTX - 1, [PQ.b(tile // 4), alt.b()], [pnyb])
            s.cp(Fr[1].ap[:, :, 0:1], pny[:, 0:8].rearrange("p (g x) -> p g x", g=4)[:, :, 0:1], [pnyb], [Fr[1].b()])

        def dft_khb(khb, aset):
            for sg in range(NSG2):
                assert stages[nds[0]] == (khb, sg)
                issue_ds(nds[0] + 3)
                ds = DS[nds[0] % 4]
                nds[0] += 1
                for n_ in range(8):
                    tile = sg * 8 + n_
                    lastn = (sg == NSG2 - 1 and n_ == 7)
                    for g in range(4):
                        pa, pab = bank(4 * aset + g)
                        s.mm(pa[:, 0:256], PQ.ap[:, tile, g * 256:g * 256 + 128], ds.ap[:, 0, n_, :],
                             (sg == 0 and n_ == 0), False, [PQ.b(tile // 4), ds.b()], [pab])
                        s.mm(pa[:, 256:512], PQ.ap[:, tile, g * 256 + 128:g * 256 + 256], ds.ap[:, 1, n_, :],
                             False, lastn, [PQ.b(tile // 4), ds.b()], [pab])

        def evac_khb(khb, aset, Dt, Mt, Mn, tb_m):
            accv = PS[:, 2048 * aset:2048 * aset + 2048].rearrange("p (g x) -> p g x", g=4)
            accb = [bkb[4 * aset + g] for g in range(4)]
            s.cp(Bs.ap, accv[:, :, 256:512], accb, [Bs.b()], e="act")
            hoff = (khb % 2) * 256
            s.tt(Dt.ap[:, :, hoff:hoff + 256], accv[:, :, 0:256], Bs.ap, ALU.add, accb + [Bs.b()], [Dt.b()])
            if khb % 2 == 1:
                s.tt(Mt.ap[:, :, 256:0:-1], accv[:, :, 0:256], Bs.ap, ALU.subtract, accb + [Bs.b()], [Mt.b()])
            else:
                s.tt(Mt.ap[:, :, 511:256:-1], accv[:, :, 1:256], Bs.ap[:, :, 1:256], ALU.subtract,
                     accb + [Bs.b()], [Mt.b()])
                if tb_m + 1 < NKB:
                    s.tt(Mn.ap[:, :, 0:1], accv[:, :, 0:1], Bs.ap[:, :, 0:1], ALU.subtract,
                         accb + [Bs.b()], [Mn.b()])

        pend = None
        pfs = {}
        pfs[0] = (prefetch(NPAIR - 1), prefetch(NKB - NPAIR))
        for m in range(NPAIR):
            tb_d = NPAIR - 1 - m
            tb_m = NKB - 1 - tb_d
            Dt = Fr[(2 * m) % 4]
            Mt = Fr[(2 * m + 1) % 4]
            Mn = Fr[(2 * m + 3) % 4]
            dft_khb(2 * tb_d + 1, 0)
            if m == 0:
                nyquist()
            evac_khb(2 * tb_d + 1, 0, Dt, Mt, Mn, tb_m)
            if pend is not None:
                pm, pDt, pMt = pend
                downstream(pfs[pm][0], pDt, (4, 5, 6, 7))
                downstream(pfs[pm][1], pMt, (4, 5, 6, 7))
                pfs[m] = (prefetch(tb_d), prefetch(tb_m))
            dft_khb(2 * tb_d, 1)
            evac_khb(2 * tb_d, 1, Dt, Mt, Mn, tb_m)
            pend = (m, Dt, Mt)
        pm, pDt, pMt = pend
        downstream(pfs[pm][0], pDt, (0, 1, 2, 3))
        downstream(pfs[pm][1], pMt, (0, 1, 2, 3))
        jc = NB - 1
        pfc = prefetch(jc)
        Fc = Fr[(2 * NPAIR) % 4]
        acc = [bank(g) for g in range(4)]
        for n_ in range(NTC):
            tile = NTX + n_
            for g in range(4):
                s.mm(acc[g][0][:, :LC], PQ.ap[:, tile, g * 256:g * 256 + 128], dftc.ap[:, 0, n_, :],
                     n_ == 0, False, [PQ.b(tile // 4), dftc.b()], [acc[g][1]])
                s.mm(acc[g][0][:, :LC], PQ.ap[:, tile, g * 256 + 128:g * 256 + 256], dftc.ap[:, 1, n_, :],
                     False, n_ == NTC - 1, [PQ.b(tile // 4), dftc.b()], [acc[g][1]])
        for g in range(4):
            s.cp(Fc.ap[:, g, :LC], acc[g][0][:, :LC], [acc[g][1]], [Fc.b()], e="act" if g % 2 == 0 else "dve")
        downstream(pfc, Fc, (4, 5, 6, 7))

    SHM = [i ^ 16 for i in range(32)]

    def odd_layer(l, last, ctx_needed, nxt):
        win_next = [None]
        w = LW[l]
        SC = 0.125
        s.barrier()
        A.reset()
        kdup = [A.alloc([128, T], BF16) for _ in range(2)]
        Vf = A.alloc([128, NTILES, 128], BF16)
        Win = A.alloc([128, KC, ODD_IN], BF16)
        hT = [A.alloc([128, KC, 512], BF16) for _ in range(2)]
        sq = A.alloc([128, KC, 512], BF16)
        rt = A.alloc([128, 512], F32)
        tmp32 = [A.alloc([128, 512], F32) for _ in range(2)]
        ccxo = A.alloc([128, 4, 512], BF16)
        cbgo = A.alloc([128, 4, 512], BF16)
        gDo = A.alloc([128, 4, 512], BF16)
        qo = A.alloc([128, 4, 512], BF16)
        q32 = [A.alloc([128, 512], F32) for _ in range(2)]
        sh = A.alloc([128, 512], F32)
        t1 = A.alloc([128, 512], F32)
        t2 = A.alloc([128, 512], F32)
        cosb = A.alloc([128, 512], F32)
        sinb = A.alloc([128, 512], F32)
        cosq = A.alloc([128, 512], F32)
        sinq = A.alloc([128, 512], F32)
        vtmp = A.alloc([128, 512], BF16)
        win_dma(l, Win, ODD_IN)
        ld(cwc.ap[:, :, 0:3], w["cw"], [], [cwc.b()])
        ld(sinkc.ap, w["sink"], [], [sinkc.b()])
        s.ts(nsinkc.ap, sinkc.ap, -1.0, None, ALU.mult, ALU.bypass, [sinkc.b()], [nsinkc.b()])
        nq = [0]

        def rope(src, dst_ap, dst_buf, nt, cb=None, sb_=None):
            cb = cosb if cb is None else cb
            sb_ = sinb if sb_ is None else sb_
            s.op("dve", lambda e: e.stream_shuffle(sh.ap[:, :nt], src.ap[:, :nt], SHM), [src.b()], [sh.b()])
            s.tt(t2.ap[:, :nt], sh.ap[:, :nt], sb_.ap[:, :nt], ALU.mult, [sh.b(), sb_.b()], [t2.b()], e="pool")
            s.tt(t1.ap[:, :nt], src.ap[:, :nt], cb.ap[:, :nt], ALU.mult, [src.b(), cb.b()], [t1.b()])
            s.tt(dst_ap, t1.ap[:, :nt], t2.ap[:, :nt], ALU.add, [t1.b(), t2.b()], [dst_buf])

        front(l, 0, hT[0], sq, rt)
        win_bias(l, Win, ODD_IN)
        nbk = 0
        tk = Ticker()
        for j, (stx, t0, nt) in enumerate(blocks):
            if j + 1 < NB:
                tk.arm(front_parts(l, j + 1, hT[(j + 1) % 2], sq, rt))
            tk.tick()
            h = hT[j % 2]
            bW = lambda m: biasW.ap[:, m, stx:stx + 1]
            full = (stx == 0) or ctx_needed

            def nb():
                nonlocal nbk
                nbk += 1
                return bank(2 + nbk % 4)

            def proj(m, func, dst_ap, dst_buf):
                p = nb()
                lin(p, Win, m * 128, h, nt)
                s.act(dst_ap, p[0][:, :nt], func, [p[1], biasW.b((m) // 8)], [dst_buf], bias=bW(m))
                tk.tick()

            def proj_mul(m, other, dst_ap, dst_buf):
                p = nb()
                lin(p, Win, m * 128, h, nt)
                s.stt(dst_ap, p[0][:, :nt], bW(m), other.ap[:, :nt], ALU.add, ALU.mult,
                      [p[1], biasW.b((m) // 8), other.b()], [dst_buf])
                tk.tick()

            if stx == 0:
                ld(cosb.ap[:, :nt], ropec_d[:, t0:t0 + nt], [], [cosb.b()])
                ld(sinb.ap[:, :nt], ropes_d[:, t0:t0 + nt], [], [sinb.b()])
                ld(cosq.ap[:, :nt], ropecq_d[:, t0:t0 + nt], [], [cosq.b()])
                ld(sinq.ap[:, :nt], ropesq_d[:, t0:t0 + nt], [], [sinq.b()])
            if full:
                for ci in range(4):
                    tm = tmp32[ci % 2]
                    proj(4 + ci, AF.Identity, tm.ap[:, :nt], tm.b())
                    proj_mul(8 + ci, tm, ccxo.ap[:, ci, :nt], ccxo.b())
                st(sC.ap[:, :, t0:t0 + nt], ccxo.ap[:, :, :nt], [ccxo.b()], [sC.b(j)])
                for ci in range(4):
                    tm = tmp32[ci % 2]
                    proj(12 + ci, AF.Silu, tm.ap[:, :nt], tm.b())
                    proj_mul(ci, tm, cbgo.ap[:, ci, :nt], cbgo.b())
                st(sA.ap[:, :, t0:t0 + nt], cbgo.ap[:, :, :nt], [cbgo.b()], [sA.b(j)])
                for ci in range(4):
                    qq = q32[nq[0] % 2]; nq[0] += 1
                    proj(16 + ci, AF.Identity, qq.ap[:, :nt], qq.b())
                    if stx == 0:
                        rope(qq, qo.ap[:, ci, :nt], qo.b(), nt, cosq, sinq)
                    else:
                        s.ts(qo.ap[:, ci, :nt], qq.ap[:, :nt], SC, None, ALU.mult, ALU.bypass, [qq.b()], [qo.b()])
                st(sD.ap[:, :, t0:t0 + nt], qo.ap[:, :, :nt], [qo.b()], [sD.b(j)])
                for ci in range(4):
                    proj(23 + ci, AF.Silu, gDo.ap[:, ci, :nt], gDo.b())
                st(sB.ap[:, :, t0:t0 + nt], gDo.ap[:, :, :nt], [gDo.b()], [sB.b(j)])
            for hk in range(2):
                qq = q32[nq[0] % 2]; nq[0] += 1
                proj(20 + hk, AF.Identity, qq.ap[:, :nt], qq.b())
                if stx == 0:
                    rope(qq, kdup[hk].ap[:, t0:t0 + nt], kdup[hk].b(j), nt)
                else:
                    s.cp(kdup[hk].ap[:, t0:t0 + nt], qq.ap[:, :nt], [qq.b()], [kdup[hk].b(j)])
            proj(22, AF.Identity, vtmp.ap[:, :nt], vtmp.b())
            tk.flush()
            for i in range(nt // 128):
                pt, ptb = bank(6 + i % 2)
                ptv = pt.bitcast(BF16)
                s.tr(ptv[:, 0:128], vtmp.ap[:, i * 128:(i + 1) * 128], identb.ap, [vtmp.b(), identb.b()], [ptb])
                s.cp(Vf.ap[:, t0 // 128 + i, :], ptv[:, 0:128], [ptb], [Vf.b(j)])

        s.barrier()
        A.reset()
        kd2 = [A.alloc([128, T], BF16) for _ in range(2)]
        for a_, b_ in zip(kd2, kdup):
            a_.bufs = b_.bufs
        Vf2 = A.alloc([128, NTILES, 128], BF16)
        Vf2.bufs = Vf.bufs
        Wout = A.alloc([128, KC, D], BF16)
        diag = A.alloc([128, 4, 3, 128], BF16)
        qpad = [A.alloc([128, 4, 2, 128], BF16) for _ in range(2)]
        p_sb = [A.alloc([128, 648], BF16) for _ in range(3)]
        sinkb2 = A.alloc([128, 8, 2], BF16)
        pT_sb = [A.alloc([128, 640], BF16) for _ in range(3)]
        SQ = [A.alloc([128, 6, 8], F32) for _ in range(2)]
        att_sb = [A.alloc([128, 512], BF16) for _ in range(2)]
        mix = [A.alloc([128, 8, 512], BF16) for _ in range(2)]
        gDi = [A.alloc([128, 4, 512], BF16) for _ in range(2)]
        cbgi = [A.alloc([128, 4, 512], BF16) for _ in range(2)]
        ccxi = [A.alloc([128, 4, 514], BF16) for _ in range(2)]
        fin_bufs = None
        if last:
            fin_bufs = (A.alloc([128, D], F32), A.alloc([128, KC, 512], BF16), A.alloc([128, 16], F32),
                        [A.alloc([128, D], F32) for _ in range(2)])
        ldc(Wout.ap, w["wout"].rearrange("(kc p) n -> p kc n", p=128), [], [Wout.b()])
        mcalls = None
        if nxt is not None:
            wst = A.alloc([128, KC, D], BF16)
            mcalls = mod_calls(nxt, wst.ap, wst.b(), pbank=4)
            mcalls[0]()
            if nxt % 2 == 0:
                WinN = A.alloc_top([128, KC, EVEN_IN], BF16)
                win_next[0] = WinN
        for c in range(4):
            for k in range(3):
                s.ts(diag.ap[:, c, k, :], identb.ap, cwc.ap[:, c, k:k + 1], None, ALU.mult, ALU.bypass,
                     [identb.b(), cwc.b()], [diag.b()])
        for qp in qpad:
            s.op("pool", lambda e: e.memset(qp.ap, 0.0), [], [qp.b()])
        s.op("dve", lambda e: e.memset(sinkb2.ap, NEG), [], [sinkb2.b()])
        s.cp(sinkb2.ap[:, :, 0:1], sinkc.ap.unsqueeze(2), [sinkc.b(), sinkb2.b()], [sinkb2.b()])
        ablocks = [j for j, (stx, t0, nt) in enumerate(blocks) if stx == 0 or ctx_needed]
        units = []
        for j in ablocks:
            for qi in range(blocks[j][2] // 128):
                for hh in range(8):
                    units.append((j, qi, hh))
        NU = len(units)
        qb_of = {}
        nqb = [0]

        def load_block(j):
            stx, t0, nt = blocks[j]
            sstart = 0 if stx == 0 else L
            send = L if stx == 0 else T
            xb = X[j % 2]
            st(xb.ap[:, :, :nt], xres.ap[:, :, t0:t0 + nt], [xres.b(j)], [xb.b()])
            gi = gDi[j % 2]
            st(gi.ap[:, :, :nt], sB.ap[:, :, t0:t0 + nt], [sB.b(j)], [gi.b()])
            cbi = cbgi[j % 2]
            st(cbi.ap[:, :, :nt], sA.ap[:, :, t0:t0 + nt], [sA.b(j)], [cbi.b()])
            cxi = ccxi[j % 2]
            lo = max(t0 - 1, sstart)
            hi = min(t0 + nt + 1, send)
            nbr = [sC.b(jj) for jj in (j - 1, j, j + 1) if 0 <= jj < NB and blocks[jj][0] == stx]
            if lo == t0:
                s.op("dve", lambda e: e.memset(cxi.ap[:, :, 0:1], 0.0), [], [cxi.b()])
            if hi == t0 + nt:
                s.op("dve", lambda e: e.memset(cxi.ap[:, :, nt + 1:nt + 2], 0.0), [], [cxi.b()])
            st(cxi.ap[:, :, lo - (t0 - 1):hi - (t0 - 1)], sC.ap[:, :, lo:hi], nbr, [cxi.b()])

        def geom(u):
            j, qi, hh = units[u]
            stx, t0, nt = blocks[j]
            tq = t0 + qi * 128
            if stx == 0:
                i_ = tq // 128
                wt = [t for t in (i_ - 1, i_, i_ + 1) if 0 <= t < NTX]
                moff = (wt[0] - (i_ - 1)) * 128
            else:
                wt = []
                moff = 0
            nwin = len(wt) * 128
            ctile = [NTX + t for t in range(NTC)]
            kbufs = sorted(set([t // 4 for t in wt] + [NB - 1]))
            par = u % 2
            base = par * 1024
            Sb = [bkb[2 * par], bkb[2 * par + 1]]
            wc0 = base + 512 - nwin
            cc0 = base + 512
            return dict(j=j, qi=qi, hh=hh, stx=stx, t0=t0, nt=nt, tq=tq, wt=wt, moff=moff, nwin=nwin,
                        kbufs=kbufs, par=par, Sb=Sb, wc0=wc0, cc0=cc0, r0=wc0, nk=nwin + LC,
                        ktiles=wt + ctile, hk=hh // 4, qc=hh // 2, qbi=u // 8)

        def load_q(u):
            g = geom(u)
            qp = qpad[g["qbi"] % 2]
            tq = g["tq"]
            ld(qp.ap[0:64, :, 0, :], sD.ap[0:64, :, tq:tq + 128], [sD.b(g["j"])], [qp.b()])
            ld(qp.ap[64:128, :, 1, :], sD.ap[64:128, :, tq:tq + 128], [sD.b(g["j"])], [qp.b()])

        deferred = {}

        def stA1(u):
            g = geom(u)
            j, qi, hh = g["j"], g["qi"], g["hh"]
            if hh == 0:
                if u == 0:
                    load_block(ablocks[0])
                    if len(ablocks) > 1:
                        load_block(ablocks[1])
                    load_q(0)
                if u + 8 < NU:
                    load_q(u + 8)
            qp = qpad[g["qbi"] % 2]
            sq_ = SQ[g["qbi"] % 2]
            hk, qc, par = g["hk"], g["qc"], g["par"]
            Sb = g["Sb"]
            psb = p_sb[u % 3]
            kR = [kd2[hk].b(kb_) for kb_ in g["kbufs"]]
            nwin, wc0, cc0 = g["nwin"], g["wc0"], g["cc0"]
            lq = qp.ap[:, qc, hh % 2, :]
            if nwin:
                wt0 = g["wt"][0] * 128
                s.mm(PS[:, wc0:wc0 + nwin], identb.ap, maskt.ap[:, g["moff"]:g["moff"] + nwin], True, False,
                     [identb.b(), maskt.b()], [Sb[0]])
                s.mm(PS[:, wc0:wc0 + nwin], lq, kd2[hk].ap[:, wt0:wt0 + nwin], False, True, [qp.b()] + kR, [Sb[0]])
            s.mm(PS[:, cc0:cc0 + LC], lq, kd2[hk].ap[:, L:L + LC], True, True, [qp.b()] + kR, [Sb[1]])
            s.mm(PS[:, cc0 + LC:cc0 + LC + 2], identb.ap, sinkb2.ap[:, hh, :], True, True,
                 [identb.b(), sinkb2.b()], [Sb[1]])
            r0, nk = g["r0"], g["nk"]
            Sr = PS[:, r0:r0 + nk + 2]
            s.op("dve", lambda e: e.reduce_max(sq_.ap[:, 1, hh:hh + 1], Sr, AX.X, negate=True), Sb, [sq_.b(hh)])
            s.act(psb.ap[:, :nk + 2], Sr, AF.Exp, Sb + [sq_.b(hh)], [psb.b(), sq_.b(hh)],
                  bias=sq_.ap[:, 1, hh:hh + 1], scale=1.0, accum=sq_.ap[:, 2, hh:hh + 1])

        def stB(u):
            g = geom(u)
            par, nk = g["par"], g["nk"]
            pTp, pTb = bank(4 + par)
            pTv = pTp.bitcast(BF16)
            for t in range(nk // 128):
                s.tr(pTv[:, t * 128:(t + 1) * 128], p_sb[u % 3].ap[:, t * 128:(t + 1) * 128], identb.ap,
                     [p_sb[u % 3].b(), identb.b()], [pTb])
            s.cp(pT_sb[u % 3].ap[:, :nk], pTv[:, :nk], [pTb], [pT_sb[u % 3].b(0)], e="dve" if u % 3 == 2 else "act")
            if g["hh"] == 7:
                sq_ = SQ[g["qbi"] % 2]
                sqa = [sq_.b(h_) for h_ in range(8)]
                s.op("dve", lambda e: e.reciprocal(sq_.ap[:, 5, :], sq_.ap[:, 2, :]), sqa, [sq_.b(9)])

        def stC(u):
            g = geom(u)
            j, qi, hh = g["j"], g["qi"], g["hh"]
            par, nk, hk = g["par"], g["nk"], g["hk"]
            o_ps, o_b = bank(6 + g["qbi"] % 2)
            vR = [Vf2.b(kb_) for kb_ in g["kbufs"]]
            nkt = nk // 128
            for t in range(nkt):
                s.mm(o_ps[:, hh * 64:(hh + 1) * 64], pT_sb[u % 3].ap[:, t * 128:(t + 1) * 128],
                     Vf2.ap[:, g["ktiles"][t], hk * 64:(hk + 1) * 64], t == 0, t == nkt - 1,
                     [pT_sb[u % 3].b(0)] + vR, [o_b])
            if hh == 7:
                qbi = g["qbi"]
                sq_ = SQ[qbi % 2]
                asb = att_sb[qbi % 2]
                s.tt(asb.ap.rearrange("p (h d) -> p h d", h=8), o_ps.rearrange("p (h d) -> p h d", h=8),
                     sq_.ap[:, 5, :].unsqueeze(2).broadcast_to([128, 8, 64]), ALU.mult, [o_b, sq_.b(9)], [asb.b()])
                lastq = (qi == g["nt"] // 128 - 1)

                def fin2(j=j, qi=qi, qbi=qbi, lastq=lastq):
                    asb_ = att_sb[qbi % 2]
                    mx_ = mix[j % 2]
                    gi = gDi[j % 2]
                    p7, p7b = bank(4 + qbi % 2)
                    p7v = p7.bitcast(BF16)
                    for c in range(4):
                        s.tr(p7v[:, c * 128:(c + 1) * 128], asb_.ap[:, c * 128:(c + 1) * 128], identb.ap,
                             [asb_.b(), identb.b()], [p7b])
                    s.tt(mx_.ap[:, 4:8, qi * 128:(qi + 1) * 128], p7v[:, 0:512].rearrange("p (c q) -> p c q", c=4),
                         gi.ap[:, :, qi * 128:(qi + 1) * 128], ALU.mult, [p7b, gi.b()], [mx_.b()])
                    if lastq:
                        stO3(j)
                deferred.setdefault(u + 4 + 2, []).append(fin2)

        def stO3(j):
            stx, t0, nt = blocks[j]
            mx_ = mix[j % 2]
            cbi = cbgi[j % 2]
            cxi = ccxi[j % 2]
            for c in range(4):
                pc = bank(4 + c % 2)
                for k in range(3):
                    s.mm(pc[0][:, :nt], diag.ap[:, c, k, :], cxi.ap[:, c, k:k + nt], k == 0, k == 2,
                         [diag.b(), cxi.b()], [pc[1]])
                s.tt(mx_.ap[:, c, :nt], pc[0][:, :nt], cbi.ap[:, c, :nt], ALU.mult, [pc[1], cbi.b()], [mx_.b()])
            out_stage(l, j, Wout, mx_, last, fin_bufs, banks=(4, 5))
            nx2 = ablocks.index(j) + 2
            if nx2 < len(ablocks):
                load_block(ablocks[nx2])

        step = 0
        while step < NU + 4 or any(k >= step for k in deferred):
            if step < NU:
                stA1(step)
            if 0 <= step - 2 < NU:
                stB(step - 2)
            if 0 <= step - 4 < NU:
                stC(step - 4)
            for fn in deferred.pop(step, []):
                fn()
            if mcalls and step in (NU // 4, NU // 2, (3 * NU) // 4):
                mcalls[1 + [NU // 4, NU // 2, (3 * NU) // 4].index(step)]()
            if win_next[0] is not None and step == (7 * NU) // 8:
                wsrcn = LW[nxt]["win"].rearrange("(kc p) n -> p kc n", p=128)
                c0 = 0
                while c0 < EVEN_IN:
                    cw_ = min(1024, EVEN_IN - c0)
                    ldc(win_next[0].ap[:, :, c0:c0 + cw_], wsrcn[:, :, c0:c0 + cw_], [], [win_next[0].b(c0 // 1024)])
                    c0 += cw_
            step += 1
        return win_next[0]

    win_pre = None
    for li, l in enumerate(layers):
        last = (li == len(layers) - 1)
        nxt = None if last else layers[li + 1]
        if l % 2 == 0:
            even_layer(l, last, nxt, first=(li == 0), win_pre=win_pre)
            win_pre = None
        else:
            win_pre = odd_layer(l, last, (l < 3), nxt)
    s.barrier()
    return nc


_TABLES = {}


def _cols(v, nchunk):
    return np.ascontiguousarray(np.asarray(v, np.float32).reshape(nchunk, 128).T)


def tables(L, LC):
    key = (L, LC)
    if key in _TABLES:
        return _TABLES[key]
    t = {}
    d = np.arange(128)
    angd = 2 * np.pi * ((d[:, None] * d[None, :]) % 128) / 128.0
    t["csd"] = np.concatenate([np.cos(angd), np.sin(angd)], axis=1) / np.sqrt(128.0)
    t["csd"] = t["csd"].astype(NPBF)
    NKB = L // 512
    NTX = L // 128
    NSG = NTX // NCS
    n = np.arange(L, dtype=np.int64)
    ang = 2 * np.pi * ((n[:, None] * n[None, :]) % L).astype(np.float64) / L
    tab = np.stack([np.cos(ang), -np.sin(ang)]) / np.sqrt(float(L))
    del ang
    NSG2 = NTX // 8
    tab = tab[:, :, :L // 2].astype(np.float32).reshape(2, NSG2, 8, 128, NKB, 256)
    t["dfth"] = np.ascontiguousarray(tab.transpose(4, 1, 3, 0, 2, 5)).astype(NPBF)
    del tab
    t["alt"] = np.repeat((np.where(np.arange(128) % 2 == 0, 1.0, -1.0) / np.sqrt(float(L)))[:, None], 2,
                         axis=1).astype(np.float32).astype(NPBF)
    nch = LC // 128
    m = np.arange(LC, dtype=np.int64)
    angc = 2 * np.pi * ((m[:, None] * m[None, :]) % LC) / float(LC)
    tc = (np.stack([np.cos(angc), -np.sin(angc)]) / np.sqrt(float(LC))).reshape(2, nch, 128, LC)
    t["dftc"] = np.ascontiguousarray(tc.transpose(2, 0, 1, 3)).astype(NPBF)
    p = np.arange(128)
    dd = p % 64
    a = dd // 32
    sgn = (dd % 32) // 16
    f = dd % 16
    inv = np.power(10000.0, -f / 16.0)
    tt_ = np.arange(L)
    pos = np.where(a[:, None] == 0, (tt_ // 64)[None, :], (tt_ % 64)[None, :]).astype(np.float64)
    angr = pos * inv[:, None]
    t["ropec"] = np.cos(angr).astype(np.float32)
    t["ropes"] = (np.sin(angr) * np.where(sgn == 0, -1.0, 1.0)[:, None]).astype(np.float32)
    t["ropecq"] = (t["ropec"] * np.float32(0.125)).astype(np.float32)
    t["ropesq"] = (t["ropes"] * np.float32(0.125)).astype(np.float32)
    r = np.arange(128)[:, None]
    jj = np.arange(384)[None, :]
    t["mask"] = np.where((jj >= r) & (jj <= r + 256), 0.0, NEG).astype(np.float32).astype(NPBF)
    t["identb"] = np.eye(128, dtype=np.float32).astype(NPBF)
    t["identf"] = np.eye(128, dtype=np.float32)
    _TABLES[key] = t
    return t


def prep(inputs, L, LC, layers):
    tb = tables(L, LC)
    f = lambda a: np.ascontiguousarray(np.asarray(a, dtype=np.float32))
    shared = dict(tb)
    shared["fin"] = _cols(inputs["final_norm"], KC)
    shared["finbc"] = np.ascontiguousarray(np.broadcast_to(f(inputs["final_norm"])[None, :], (128, D)))
    for l in layers:
        i = l // 2
        pre = "e_" if l % 2 == 0 else "o_"
        shared["wmod%d" % l] = f(inputs[pre + "w_mod"][i])
        shared["bmod%d" % l] = _cols(inputs[pre + "b_mod"][i], 24)
        shared["norm%d" % l] = _cols(inputs[pre + "norm"][i], KC)
        shared["wout%d" % l] = f(inputs[pre + "w_out"][i])
        if l % 2 == 0:
            shared["win%d" % l] = f(inputs["e_w_in"][i])
            cw = f(inputs["e_a_conv_w"][i])
            shared["cw%d" % l] = np.ascontiguousarray(cw.reshape(31, 4, 128).transpose(2, 1, 0))
            v4 = np.stack([f(inputs["e_a_conv_b"][i]), f(inputs["e_a_ln_g"][i]),
                           f(inputs["e_a_ln_b"][i]), f(inputs["e_a_pw_b"][i])], axis=-1)
            shared["v4%d" % l] = np.ascontiguousarray(v4.reshape(4, 128, 4).transpose(1, 0, 2))
            shared["pw%d" % l] = f(inputs["e_a_pw_w"][i])
            shared["fw%d" % l] = f(inputs["e_b_fw"][i])
        else:
            wi = f(inputs["o_w_in"][i])
            k0 = wi[:, 2560:2624]
            k1 = wi[:, 2624:2688]
            shared["win%d" % l] = np.ascontiguousarray(np.concatenate(
                [wi[:, 0:2560], k0, k0, k1, k1, wi[:, 2688:2816], wi[:, 2816:3328]], axis=1))
            cw = f(inputs["o_c_conv_w"][i])
            shared["cw%d" % l] = np.ascontiguousarray(cw.reshape(3, 4, 128).transpose(2, 1, 0))
            shared["sink%d" % l] = np.ascontiguousarray(
                np.broadcast_to(f(inputs["o_sink"][i])[None, :], (128, 8)))
    x = np.asarray(inputs["x"], np.float32)
    ctx = np.asarray(inputs["ctx"], np.float32)
    c = np.asarray(inputs["c"], np.float32)
    cctx = np.asarray(inputs["c_ctx"], np.float32)
    maps = []
    for b in range(x.shape[0]):
        m = dict(shared)
        m["x"] = np.ascontiguousarray(x[b])
        m["ctx"] = np.ascontiguousarray(ctx[b])
        m["ccols"] = np.ascontiguousarray(np.stack([_cols(c[b], KC), _cols(cctx, KC)], axis=-1))
        maps.append(m)
    return maps


_NC_CACHE = {}


def run(inputs, L, LC, layers):
    key = (L, LC, tuple(layers))
    if key not in _NC_CACHE:
        _NC_CACHE[key] = build(L, LC, layers)
    nc = _NC_CACHE[key]
    maps = prep(inputs, L, LC, layers)
    res = run_bass_kernel_spmd(nc, maps, core_ids=list(range(len(maps))))
    return np.stack([np.asarray(r["out"], np.float32) for r in res.results], axis=0)


def kernel(**inputs):
    return run(inputs, 4096, 256, [0, 1, 2, 3])
```

```python
import math
import numpy as np
import ml_dtypes
import concourse.bass as bass
import concourse.mybir as mybir
from concourse.bass_utils import run_bass_kernel_spmd

F32 = mybir.dt.float32
BF16 = mybir.dt.bfloat16
U8 = mybir.dt.uint8
AF = mybir.ActivationFunctionType
ALU = mybir.AluOpType
AX = mybir.AxisListType
NPBF = ml_dtypes.bfloat16

D = 1024
KC = 8
EPS = 1e-6
NEG = -1e30
UPAD = 16
NCS = 4
NST = 4
ARENA = 166 * 1024


class Buf:
    __slots__ = ("w", "r")

    def __init__(self):
        self.w = None
        self.r = {}


class TT:
    def __init__(self, ap):
        self.ap = ap
        self.bufs = {}

    def b(self, key=0):
        x = self.bufs.get(key)
        if x is None:
            x = self.bufs[key] = Buf()
        return x


class Sched:
    NRING = 20

    def __init__(self, nc):
        self.nc = nc
        self.eng = dict(pe=nc.tensor, dve=nc.vector, act=nc.scalar, pool=nc.gpsimd, sp=nc.sync)
        self.nk = 0
        self.sem = {}
        self.semk = {}
        self.cnt = {}
        self.seen = {}
        for n in self.eng:
            self.sem[n] = nc.alloc_semaphore("e_" + n)
            self.semk[n] = self._k()
            self.cnt[n] = 0
            self.seen[n] = {}
        self.ring = {}
        self.ringk = {}
        self.rval = {}
        self.rpos = {}
        for q in ("sp", "pool"):
            self.ring[q] = [nc.alloc_semaphore("q_%s_%d" % (q, i)) for i in range(self.NRING)]
            self.ringk[q] = [self._k() for _ in range(self.NRING)]
            self.rval[q] = [0] * self.NRING
            self.rpos[q] = 0

    def _k(self):
        self.nk += 1
        return self.nk

    def _wait(self, e, tok):
        k, sem, val, src = tok
        if src == "pe" and e == "pe":
            return
        if self.seen[e].get(k, 0) >= val:
            return
        self.eng[e].wait_ge(sem, val)
        self.seen[e][k] = val

    def _deps(self, e, R, W):
        for b in R:
            if b.w is not None:
                self._wait(e, b.w)
        for b in W:
            if b.w is not None:
                self._wait(e, b.w)
            for t in b.r.values():
                self._wait(e, t)

    @staticmethod
    def _commit(tok, R, W):
        for b in R:
            b.r[tok[0]] = tok
        for b in W:
            b.w = tok
            b.r = {}

    def op(self, e, fn, R=(), W=()):
        self._deps(e, R, W)
        inst = fn(self.eng[e])
        self.cnt[e] += 1
        inst.then_inc(self.sem[e], 1)
        self._commit((self.semk[e], self.sem[e], self.cnt[e], e), R, W)

    def dma(self, q, out, in_, R=(), W=(), **kw):
        self._deps(q, R, W)
        i = self.rpos[q]
        self.rpos[q] = (i + 1) % self.NRING
        sem = self.ring[q][i]
        k = self.ringk[q][i]
        prev = self.rval[q][i]
        if prev:
            self._wait(q, (k, sem, prev, "dma"))
        inst = self.eng[q].dma_start(out=out, in_=in_, **kw)
        inst.then_inc(sem, 16)
        self.rval[q][i] = prev + 16
        self._commit((k, sem, prev + 16, "dma"), R, W)

    def barrier(self):
        toks = []
        for n in self.eng:
            if self.cnt[n] > 0:
                toks.append((self.semk[n], self.sem[n], self.cnt[n], n))
        for q in self.ring:
            for i in range(self.NRING):
                if self.rval[q][i] > 0:
                    toks.append((self.ringk[q][i], self.ring[q][i], self.rval[q][i], "dma"))
        for e in self.eng:
            for t in toks:
                if t[3] == e and e in ("pe", "sp"):
                    continue
                self._wait(e, t)

    def mm(self, out, lhsT, rhs, start, stop, R, W):
        self.op("pe", lambda e: e.matmul(out, lhsT, rhs, start=start, stop=stop), R, W)

    def tr(self, out, in_, ident, R, W):
        self.op("pe", lambda e: e.transpose(out, in_, ident), R, W)

    def act(self, out, in_, func, R, W, bias=None, scale=None, accum=None):
        kw = {}
        if bias is not None:
            kw["bias"] = bias
        if scale is not None:
            kw["scale"] = scale
        if accum is not None:
            kw["accum_out"] = accum
        self.op("act", lambda e: e.activation(out, in_, func, **kw), R, W)

    def tt(self, out, in0, in1, op, R, W, e="dve"):
        self.op(e, lambda g: g.tensor_tensor(out, in0, in1, op), R, W)

    def ts(self, out, in0, s1, s2, op0, op1, R, W, e="dve"):
        self.op(e, lambda g: g.tensor_scalar(out, in0, s1, s2, op0, op1), R, W)

    def stt(self, out, in0, scalar, in1, op0, op1, R, W):
        self.op("dve", lambda g: g.scalar_tensor_tensor(out, in0, scalar, in1, op0, op1), R, W)

    def cp(self, out, in_, R, W, e="dve"):
        if e == "act":
            self.op("act", lambda g: g.copy(out, in_), R, W)
        else:
            self.op(e, lambda g: g.tensor_copy(out, in_), R, W)


class Arena:
    def __init__(self, ap_u8, size):
        self.ap = ap_u8
        self.size = size
        self.off = 0
        self.top = size

    def reset(self):
        self.off = 0
        self.top = self.size

    def alloc_top(self, shape, dtype):
        esz = 4 if dtype == F32 else 2
        n = esz
        for d in shape[1:]:
            n *= d
        off = (self.top - n) // 64 * 64
        save = self.off
        self.off = off
        t = self.alloc(shape, dtype)
        self.off = save
        self.top = off
        assert self.off <= self.top
        return t

    def alloc(self, shape, dtype):
        esz = 4 if dtype == F32 else 2
        n = esz
        for d in shape[1:]:
            n *= d
        off = (self.off + 63) // 64 * 64
        assert off + n <= self.top, ("arena overflow", off + n, self.top)
        v = self.ap[:, off:off + n].bitcast(dtype)
        if len(shape) == 3:
            v = v.rearrange("p (a b) -> p a b", a=shape[1])
        elif len(shape) == 4:
            v = v.rearrange("p (a b c) -> p a b c", a=shape[1], b=shape[2])
        self.off = off + n
        return TT(v)


EVEN_IN = 2560
ODD_IN = 3328 + 128


def build(L, LC, layers, dbg=False):
    T = L + LC
    NTX = L // 128
    NTC = LC // 128
    NTILES = NTX + NTC
    NKB = L // 512
    NSG = NTX // NCS
    UW = T + 4 * UPAD

    def ucol(t):
        return t + UPAD if t < L else t + 3 * UPAD

    blocks = [(0, kb * 512, 512) for kb in range(NKB)] + [(1, L, LC)]
    NB = len(blocks)

    nc = bass.Bass("TRN2", target_bir_lowering=False)
    s = Sched(nc)

    def din(name, shape, dt=F32):
        return nc.dram_tensor(name, list(shape), dt, kind="ExternalInput").ap()

    x_d = din("x", [L, D])
    ctx_d = din("ctx", [LC, D])
    ccols_d = din("ccols", [128, KC, 2])
    fin_d = din("fin", [128, KC])
    finbc_d = din("finbc", [128, D])
    csd_d = din("csd", [128, 256], BF16)
    NSG2 = NTX // 8
    dfth_d = din("dfth", [NKB, NSG2, 128, 2, 8, 256], BF16)
    alt_d = din("alt", [128, 2], BF16)
    dftc_d = din("dftc", [128, 2, NTC, LC], BF16)
    ropec_d = din("ropec", [128, L])
    ropes_d = din("ropes", [128, L])
    ropecq_d = din("ropecq", [128, L])
    ropesq_d = din("ropesq", [128, L])
    mask_d = din("mask", [128, 384], BF16)
    identb_d = din("identb", [128, 128], BF16)
    identf_d = din("identf", [128, 128])
    LW = {}
    for l in layers:
        w = {}
        w["wmod"] = din("wmod%d" % l, [D, 3 * D])
        w["bmod"] = din("bmod%d" % l, [128, 24])
        w["norm"] = din("norm%d" % l, [128, KC])
        w["wout"] = din("wout%d" % l, [D, D])
        if l % 2 == 0:
            w["win"] = din("win%d" % l, [D, EVEN_IN])
            w["cw"] = din("cw%d" % l, [128, 4, 31])
            w["v4"] = din("v4%d" % l, [128, 4, 4])
            w["pw"] = din("pw%d" % l, [512, 512])
            w["fw"] = din("fw%d" % l, [4, 128, 128])
        else:
            w["win"] = din("win%d" % l, [D, ODD_IN])
            w["cw"] = din("cw%d" % l, [128, 4, 3])
            w["sink"] = din("sink%d" % l, [128, 8])
        LW[l] = w
    out_d = nc.dram_tensor("out", [L, D], F32, kind="ExternalOutput").ap()

    def dscr(name, shape, dt):
        return TT(nc.dram_tensor(name, list(shape), dt, kind="Internal").ap())

    xres = dscr("xres", [128, KC, T], F32)
    pqd = dscr("pqd", [NTILES, 128, 1024], BF16)
    sA = dscr("sA", [128, 4, T], BF16)
    sB = dscr("sB", [128, 4, T], BF16)
    sC = dscr("sC", [128, 4, T], BF16)
    sD = dscr("sD", [128, 4, T], BF16)

    def sb(name, shape, dt):
        return TT(nc.alloc_sbuf_tensor(name, list(shape), dt).ap())

    identb = sb("identb_s", [128, 128], BF16)
    identf = sb("identf_s", [128, 128], F32)
    onesb = sb("onesb", [128, 128], BF16)
    ones512 = sb("ones512", [128, 128], BF16)
    epsc = sb("epsc", [128, 1], F32)
    csd = sb("csd_s", [128, 256], BF16)
    maskt = sb("mask_s", [128, 384], BF16)
    dftc = sb("dftc_s", [128, 2, NTC, LC], BF16)
    ccols = sb("ccols_s", [128, KC, 2], F32)
    silc = sb("silc", [128, KC, 2], BF16)
    fin = sb("fin_s", [128, KC], F32)
    X = [sb("X%d" % i, [128, KC, 512], F32) for i in range(2)]
    modc = {l: sb("modc%d" % l, [128, 24, 2], F32) for l in layers}
    Gc = {l: sb("G%d" % l, [128, KC, 2], F32) for l in layers}
    Sbf = {l: sb("Sbf%d" % l, [128, KC, 2], BF16) for l in layers}
    biasW = sb("biasW", [128, 28, 2], F32)
    normc = sb("normc", [128, KC], F32)
    bmodc = sb("bmodc", [128, 24], F32)
    v4 = sb("v4_s", [128, 4, 4], F32)
    cwc = sb("cwc", [128, 4, 31], F32)
    sinkc = sb("sinkc", [128, 8], F32)
    nsinkc = sb("nsinkc", [128, 8], F32)
    arena_t = nc.alloc_sbuf_tensor("arena", [128, ARENA], U8).ap()
    A = Arena(arena_t, ARENA)

    PS = nc.alloc_psum_tensor("ps", [128, 4096], F32).ap()
    bkb = [Buf() for _ in range(8)]

    def bank(i):
        return PS[:, i * 512:(i + 1) * 512], bkb[i]

    ld = lambda out, in_, R, W: s.dma("sp", out, in_, R, W)
    st = lambda out, in_, R, W: s.dma("pool", out, in_, R, W)
    ldc = lambda out, in_, R, W: s.dma("pool", out, in_, R, W)

    ld(identb.ap, identb_d, [], [identb.b()])
    ld(identf.ap, identf_d, [], [identf.b()])
    ld(csd.ap, csd_d, [], [csd.b()])
    ld(maskt.ap, mask_d, [], [maskt.b()])
    ld(dftc.ap, dftc_d, [], [dftc.b()])
    ld(ccols.ap, ccols_d, [], [ccols.b()])
    ld(fin.ap, fin_d, [], [fin.b()])
    s.op("dve", lambda e: e.memset(onesb.ap, 1.0), [], [onesb.b()])
    s.op("dve", lambda e: e.memset(ones512.ap, 1.0 / 512.0), [], [ones512.b()])
    s.op("dve", lambda e: e.memset(epsc.ap, EPS), [], [epsc.b()])
    s.act(silc.ap, ccols.ap, AF.Silu, [ccols.b()], [silc.b()])

    def mod_calls(l, stage_ap, stage_buf, pbank=7):
        w = LW[l]
        wsrc = w["wmod"].rearrange("(kc p) n -> p kc n", p=128)
        mc = modc[l]

        def load(part):
            ldc(stage_ap, wsrc[:, :, part * D:(part + 1) * D], [], [stage_buf])

        def mm(part):
            pm, pmb = bank(pbank)
            for m in range(8):
                for c in range(KC):
                    s.mm(pm[:, 2 * m:2 * m + 2], stage_ap[:, c, m * 128:(m + 1) * 128],
                         silc.ap[:, c, :], c == 0, c == KC - 1, [stage_buf, silc.b()], [pmb])
            s.tt(mc.ap[:, part * 8:(part + 1) * 8, :], pm[:, 0:16].rearrange("p (a b) -> p a b", a=8),
                 bmodc.ap[:, part * 8:(part + 1) * 8].unsqueeze(2).broadcast_to([128, 8, 2]), ALU.add,
                 [pmb, bmodc.b()], [mc.b()])

        def c0():
            ld(bmodc.ap, w["bmod"], [], [bmodc.b()])
            ld(normc.ap, w["norm"], [], [normc.b()])
            load(0)

        def c1():
            mm(0)
            load(1)

        def c2():
            mm(1)
            load(2)

        def c3():
            mm(2)
            s.stt(Gc[l].ap, mc.ap[:, 8:16, :], 1.0, normc.ap.unsqueeze(2).broadcast_to([128, KC, 2]),
                  ALU.add, ALU.mult, [mc.b(), normc.b()], [Gc[l].b()])
            s.cp(Sbf[l].ap, mc.ap[:, 0:8, :], [mc.b()], [Sbf[l].b()])
        return [c0, c1, c2, c3]

    def win_dma(l, Win, nin):
        w = LW[l]
        wsrc = w["win"].rearrange("(kc p) n -> p kc n", p=128)
        c0 = 0
        while c0 < nin:
            cw_ = min(1024, nin - c0)
            ldc(Win.ap[:, :, c0:c0 + cw_], wsrc[:, :, c0:c0 + cw_], [], [Win.b(c0 // 1024)])
            c0 += cw_

    def win_bias(l, Win, nin):
        nchunk = nin // 128
        pm, pmb = bank(0)
        for p0_ in range(0, nchunk, 8):
            p1_ = min(p0_ + 8, nchunk)
            for m in range(p0_, p1_):
                for c in range(KC):
                    s.mm(pm[:, 2 * m:2 * m + 2], Win.ap[:, c, m * 128:(m + 1) * 128], Sbf[l].ap[:, c, :],
                         c == 0, c == KC - 1, [Win.b(m // 8), Sbf[l].b()], [pmb])
            s.cp(biasW.ap[:, p0_:p1_, :], pm[:, 2 * p0_:2 * p1_].rearrange("p (a b) -> p a b", b=2),
                 [pmb], [biasW.b(p0_ // 8)])

    def front(l, j, hT, sq, rt, from_dram=True, gcol=None, fb=(0, 1), xb=None):
        stx, t0, nt = blocks[j]
        if xb is None:
            xb = X[j % 2]
        if from_dram:
            ld(xb.ap[:, :, :nt], xres.ap[:, :, t0:t0 + nt], [xres.b(j)], [xb.b()])
        s.act(sq.ap[:, :, :nt], xb.ap[:, :, :nt], AF.Square, [xb.b()], [sq.b()])
        p0, p0b = bank(fb[0])
        p1, p1b = bank(fb[1])
        for c in range(KC):
            s.mm(p0[:, :nt], onesb.ap, sq.ap[:, c, :nt], c == 0, c == KC - 1, [sq.b(), onesb.b()], [p0b])
        s.act(rt.ap[:, :nt], p0[:, :nt], AF.Ln, [p0b, epsc.b()], [rt.b()], bias=epsc.ap, scale=1.0 / D)
        s.act(p1[:, :nt], rt.ap[:, :nt], AF.Exp, [rt.b()], [p1b], scale=-0.5)
        for c in range(KC):
            g = Gc[l].ap[:, c, stx:stx + 1] if gcol is None else gcol[:, c:c + 1]
            s.stt(hT.ap[:, c, :nt], xb.ap[:, c, :nt], g, p1[:, :nt], ALU.mult, ALU.mult,
                  [xb.b(), p1b, Gc[l].b() if gcol is None else fin.b()], [hT.b()])

    def front_parts(l, j, hT, sq, rt, from_dram=True):
        stx, t0, nt = blocks[j]
        xb = X[j % 2]
        p0, p0b = bank(0)
        p1, p1b = bank(1)

        def pre():
            if from_dram:
                ld(xb.ap[:, :, :nt], xres.ap[:, :, t0:t0 + nt], [xres.b(j)], [xb.b()])
            s.act(sq.ap[:, :, :nt], xb.ap[:, :, :nt], AF.Square, [xb.b()], [sq.b()])

        def mid():
            for c in range(KC):
                s.mm(p0[:, :nt], onesb.ap, sq.ap[:, c, :nt], c == 0, c == KC - 1, [sq.b(), onesb.b()], [p0b])
            s.act(rt.ap[:, :nt], p0[:, :nt], AF.Ln, [p0b, epsc.b()], [rt.b()], bias=epsc.ap, scale=1.0 / D)
            s.act(p1[:, :nt], rt.ap[:, :nt], AF.Exp, [rt.b()], [p1b], scale=-0.5)

        def mk(c):
            def f():
                s.stt(hT.ap[:, c, :nt], xb.ap[:, c, :nt], Gc[l].ap[:, c, stx:stx + 1], p1[:, :nt], ALU.mult, ALU.mult,
                      [xb.b(), p1b, Gc[l].b()], [hT.b()])
            return f
        return {0: [pre], 3: [mid], **{4 + c: [mk(c)] for c in range(KC)}}

    class Ticker:
        def __init__(self):
            self.parts = {}
            self.k = 0

        def arm(self, parts):
            self.flush()
            self.parts = dict(parts)
            self.k = 0

        def tick(self):
            for fn in self.parts.pop(self.k, []):
                fn()
            self.k += 1

        def flush(self):
            for k_ in sorted(self.parts):
                for fn in self.parts[k_]:
                    fn()
            self.parts = {}

    def lin(pb, Win, col0, hT, nt, extraR=()):
        p, b = pb
        for c in range(KC):
            s.mm(p[:, :nt], Win.ap[:, c, col0:col0 + 128], hT.ap[:, c, :nt], c == 0, c == KC - 1,
                 [Win.b(col0 // 1024), hT.b()] + list(extraR), [b])

    init_n = [0]

    def init_block(j, TI):
        stx, t0, nt = blocks[j]
        xb = X[j % 2]
        for i in range(nt // 128):
            n = init_n[0]
            ti = TI[n % 2]
            init_n[0] += 1
            src_ = x_d[t0 + i * 128:t0 + (i + 1) * 128, :] if stx == 0 else \
                ctx_d[t0 - L + i * 128:t0 - L + (i + 1) * 128, :]
            ld(ti.ap, src_, [], [ti.b()])
            for h in range(2):
                pt, ptb = bank(h)
                for c in range(4):
                    cc = h * 4 + c
                    s.tr(pt[:, c * 128:(c + 1) * 128], ti.ap[:, cc * 128:(cc + 1) * 128], identf.ap,
                         [ti.b(), identf.b()], [ptb])
                s.cp(xb.ap[:, h * 4:(h + 1) * 4, i * 128:(i + 1) * 128],
                     pt.rearrange("p (a b) -> p a b", a=4), [ptb], [xb.b()],
                     e="act" if h == 0 else "dve")
        ld(xres.ap[:, :, t0:t0 + nt], xb.ap[:, :, :nt], [xb.b()], [xres.b(j)])

    def out_stage(l, j, Wout, mix, last, fin_bufs, banks=(4, 5, 6, 7), xb=None):
        stx, t0, nt = blocks[j]
        if xb is None:
            xb = X[j % 2]
        for m in range(KC):
            pb = bank(banks[m % len(banks)])
            p, b = pb
            for c in range(KC):
                s.mm(p[:, :nt], Wout.ap[:, c, m * 128:(m + 1) * 128], mix.ap[:, c, :nt], c == 0, c == KC - 1,
                     [Wout.b(), mix.b()], [b])
            s.stt(xb.ap[:, m, :nt], p[:, :nt], modc[l].ap[:, 16 + m, stx:stx + 1], xb.ap[:, m, :nt],
                  ALU.mult, ALU.add, [b, modc[l].b(), xb.b()], [xb.b()])
            if last and stx == 0:
                s.act(fin_bufs[1].ap[:, m, :nt], xb.ap[:, m, :nt], AF.Square, [xb.b()], [fin_bufs[1].b()])
        if not last:
            st(xres.ap[:, :, t0:t0 + nt], xb.ap[:, :, :nt], [xb.b()], [xres.b(j)])
        elif stx == 0:
            finbc, sq, fst, TO = fin_bufs
            if not finbc.bufs:
                ld(finbc.ap, finbc_d, [], [finbc.b()])
            nti = nt // 128
            pss, pssb = bank(banks[-1])
            for i in range(nti):
                for c in range(KC):
                    s.mm(pss[:, 2 * i:2 * i + 2], sq.ap[:, c, i * 128:(i + 1) * 128], onesb.ap[:, 0:2],
                         c == 0, c == KC - 1, [sq.b(), onesb.b()], [pssb])
            s.act(fst.ap[:, 0:2 * nti], pss[:, 0:2 * nti], AF.Ln, [pssb, epsc.b()], [fst.b()],
                  bias=epsc.ap, scale=1.0 / D)
            s.act(fst.ap[:, 8:8 + 2 * nti], fst.ap[:, 0:2 * nti], AF.Exp, [fst.b()], [fst.b()], scale=-0.5)
            for i in range(nti):
                to = TO[i % 2]
                for h in range(2):
                    pt, ptb = bank(banks[-2 + h])
                    for c in range(4):
                        cc = h * 4 + c
                        s.tr(pt[:, c * 128:(c + 1) * 128], xb.ap[:, cc, i * 128:(i + 1) * 128], identf.ap,
                             [xb.b(), identf.b()], [ptb])
                    s.stt(to.ap[:, h * 512:(h + 1) * 512], pt, fst.ap[:, 8 + 2 * i:9 + 2 * i],
                          finbc.ap[:, h * 512:(h + 1) * 512], ALU.mult, ALU.mult,
                          [ptb, fst.b(), finbc.b()], [to.b()])
                st(out_d[t0 + i * 128:t0 + (i + 1) * 128, :], to.ap, [to.b()], [])

    def even_layer(l, last, nxt, first=False, win_pre=None):
        w = LW[l]
        if not first:
            s.barrier()
        A.reset()
        Win = A.alloc_top([128, KC, EVEN_IN], BF16)
        if win_pre is not None:
            Win.bufs = win_pre.bufs
        U = A.alloc([128, 4, UW], BF16)
        hT = [A.alloc([128, KC, 512], BF16) for _ in range(2)]
        sq = A.alloc([128, KC, 512], BF16)
        rt = A.alloc([128, 512], F32)
        sig = [A.alloc([128, 512], F32) for _ in range(2)]
        gAo = [A.alloc([128, 4, 512], BF16) for _ in range(2)]
        gBo = [A.alloc([128, 4, 512], BF16) for _ in range(2)]
        bvT = [A.alloc([128, 4, 512], BF16) for _ in range(2)]
        pqo = [A.alloc([128, 1024], BF16) for _ in range(2)]
        TI = None
        if first:
            TI = [A.alloc([128, D], F32) for _ in range(2)]
            wst0 = A.alloc([128, KC, D], BF16)
            for fn in mod_calls(l, wst0.ap, wst0.b(), pbank=7):
                fn()
        if win_pre is None:
            win_dma(l, Win, EVEN_IN)
        ld(v4.ap, w["v4"], [], [v4.b()])
        ld(cwc.ap, w["cw"], [], [cwc.b()])
        s.op("pool", lambda e: e.memset(U.ap, 0.0), [], [U.b(j) for j in range(NB)])
        if first:
            init_block(0, TI)
        front(l, 0, hT[0], sq, rt, from_dram=not first)
        win_bias(l, Win, EVEN_IN)
        nbk = 0
        ntile = 0
        tk = Ticker()
        for j, (stx, t0, nt) in enumerate(blocks):
            if j + 1 < NB:
                if first:
                    init_block(j + 1, TI)
                tk.arm(front_parts(l, j + 1, hT[(j + 1) % 2], sq, rt, from_dram=not first))
            tk.tick()
            h = hT[j % 2]
            bW = lambda m: biasW.ap[:, m, stx:stx + 1]
            for ci in range(4):
                pg = bank(2 + nbk % 4); nbk += 1
                lin(pg, Win, (4 + ci) * 128, h, nt)
                sg = sig[ci % 2]
                s.act(sg.ap[:, :nt], pg[0][:, :nt], AF.Sigmoid, [pg[1], biasW.b((4 + ci) // 8)], [sg.b()], bias=bW(4 + ci))
                tk.tick()
                pv = bank(2 + nbk % 4); nbk += 1
                lin(pv, Win, ci * 128, h, nt)
                s.stt(U.ap[:, ci, ucol(t0):ucol(t0) + nt], pv[0][:, :nt], bW(ci), sg.ap[:, :nt], ALU.add, ALU.mult,
                      [pv[1], sg.b(), biasW.b((ci) // 8)], [U.b(j)])
                tk.tick()
            go = gAo[j % 2]
            for ci in range(4):
                p = bank(2 + nbk % 4); nbk += 1
                lin(p, Win, (8 + ci) * 128, h, nt)
                s.act(go.ap[:, ci, :nt], p[0][:, :nt], AF.Silu, [p[1], biasW.b((8 + ci) // 8)], [go.b()], bias=bW(8 + ci))
                tk.tick()
            st(sA.ap[:, :, t0:t0 + nt], go.ap[:, :, :nt], [go.b()], [sA.b(j)])
            tk.flush()
            bv = bvT[j % 2]
            for ci in range(4):
                p = bank(2 + nbk % 4); nbk += 1
                lin(p, Win, (12 + ci) * 128, h, nt)
                s.act(bv.ap[:, ci, :nt], p[0][:, :nt], AF.Identity, [p[1], biasW.b((12 + ci) // 8)], [bv.b()], bias=bW(12 + ci))
            for i in range(nt // 128):
                pb0 = 6 if ntile % 2 == 0 else 0
                pq = PS[:, pb0 * 512:(pb0 + 2) * 512]
                for g in range(4):
                    s.mm(pq[:, g * 256:(g + 1) * 256], bv.ap[:, g, i * 128:(i + 1) * 128], csd.ap, True, True,
                         [bv.b(), csd.b()], [bkb[pb0 + g // 2]])
                po = pqo[ntile % 2]
                s.cp(po.ap[:, 0:512], pq[:, 0:512], [bkb[pb0]], [po.b()], e="act")
                s.cp(po.ap[:, 512:1024], pq[:, 512:1024], [bkb[pb0 + 1]], [po.b()], e="dve")
                tile = (t0 // 128) + i
                st(pqd.ap[tile], po.ap, [po.b()], [pqd.b(tile)])
                ntile += 1
            go = gBo[j % 2]
            for ci in range(4):
                p = bank(2 + nbk % 4); nbk += 1
                lin(p, Win, (16 + ci) * 128, h, nt)
                s.act(go.ap[:, ci, :nt], p[0][:, :nt], AF.Silu, [p[1], biasW.b((16 + ci) // 8)], [go.b()], bias=bW(16 + ci))
            st(sB.ap[:, :, t0:t0 + nt], go.ap[:, :, :nt], [go.b()], [sB.b(j)])

        s.barrier()
        A.reset()
        U2 = A.alloc([128, 4, UW], BF16)
        U2.bufs = U.bufs
        diag = A.alloc([128, 4, 31, 128], BF16)
        pwt = A.alloc([128, 4, 512], BF16)
        v32 = [A.alloc([128, 4, 512], F32) for _ in range(2)]
        vb = [A.alloc([128, 4, 512], BF16) for _ in range(2)]
        vsq = [A.alloc([128, 4, 512], BF16) for _ in range(2)]
        m2 = A.alloc([128, 512], F32)
        var = A.alloc([128, 512], F32)
        sd = A.alloc([128, 512], F32)
        z = A.alloc([128, 4, 512], F32)
        u2 = A.alloc([128, 4, 512], BF16)
        gAi = [A.alloc([128, 4, 512], BF16) for _ in range(2)]
        mAo = [A.alloc([128, 4, 512], BF16) for _ in range(2)]
        ldc(pwt.ap, w["pw"].rearrange("(kc p) n -> p kc n", p=128), [], [pwt.b()])
        for c in range(4):
            for k in range(31):
                s.ts(diag.ap[:, c, k, :], identb.ap, cwc.ap[:, c, k:k + 1], None, ALU.mult, ALU.bypass,
                     [identb.b(), cwc.b()], [diag.b()])
        mcalls = mod_calls(nxt, X[1].ap.bitcast(BF16), X[1].b()) if nxt is not None else None
        if mcalls:
            mcalls[0]()

        tk2 = Ticker()

        def conv(j):
            stx, t0, nt = blocks[j]
            vv = j % 2
            gi = gAi[vv]
            ld(gi.ap[:, :, :nt], sA.ap[:, :, t0:t0 + nt], [sA.b(j)], [gi.b()])
            nbrs = [U2.b(jj) for jj in (j - 1, j, j + 1) if 0 <= jj < NB and blocks[jj][0] == stx]
            for c in range(4):
                pc = bank(c)
                for k in range(31):
                    c0 = ucol(t0) + k - 15
                    s.mm(pc[0][:, :nt], diag.ap[:, c, k, :], U2.ap[:, c, c0:c0 + nt], k == 0, k == 30,
                         [diag.b()] + nbrs, [pc[1]])
                s.act(v32[vv].ap[:, c, :nt], pc[0][:, :nt], AF.Identity, [pc[1], v4.b()], [v32[vv].b()],
                      bias=v4.ap[:, c, 0:1])
                s.act(vsq[vv].ap[:, c, :nt], pc[0][:, :nt], AF.Square, [pc[1], v4.b()], [vsq[vv].b()],
                      bias=v4.ap[:, c, 0:1])
                s.cp(vb[vv].ap[:, c, :nt], v32[vv].ap[:, c, :nt], [v32[vv].b()], [vb[vv].b()])
                tk2.tick()

        def lnpw_parts(j):
            stx, t0, nt = blocks[j]
            vv = j % 2
            gi = gAi[vv]
            pmean = bank(4)
            pmsq = bank(5)
            prs = bank(6)

            def p_stats():
                for c in range(4):
                    s.mm(pmean[0][:, :nt], ones512.ap, vb[vv].ap[:, c, :nt], c == 0, c == 3,
                         [ones512.b(), vb[vv].b()], [pmean[1]])
                for c in range(4):
                    s.mm(pmsq[0][:, :nt], ones512.ap, vsq[vv].ap[:, c, :nt], c == 0, c == 3,
                         [ones512.b(), vsq[vv].b()], [pmsq[1]])
                s.act(m2.ap[:, :nt], pmean[0][:, :nt], AF.Square, [pmean[1]], [m2.b()])
                s.tt(var.ap[:, :nt], pmsq[0][:, :nt], m2.ap[:, :nt], ALU.subtract, [pmsq[1], m2.b()], [var.b()])
                s.act(sd.ap[:, :nt], var.ap[:, :nt], AF.Sqrt, [var.b(), epsc.b()], [sd.b()], bias=epsc.ap, scale=1.0)
                s.op("dve", lambda e: e.reciprocal(prs[0][:, :nt], sd.ap[:, :nt]), [sd.b()], [prs[1]])

            def p_z():
                s.tt(z.ap[:, :, :nt], v32[vv].ap[:, :, :nt],
                     pmean[0][:, :nt].unsqueeze(1).broadcast_to([128, 4, nt]),
                     ALU.subtract, [v32[vv].b(), pmean[1]], [z.b()])
                s.tt(z.ap[:, :, :nt], z.ap[:, :, :nt], prs[0][:, :nt].unsqueeze(1).broadcast_to([128, 4, nt]),
                     ALU.mult, [z.b(), prs[1]], [z.b()])

            def p_silu():
                for c in range(4):
                    s.act(u2.ap[:, c, :nt], z.ap[:, c, :nt], AF.Silu, [z.b(), v4.b()], [u2.b()],
                          bias=v4.ap[:, c, 2:3], scale=v4.ap[:, c, 1:2])

            def p_pw():
                mo = mAo[vv]
                for m in range(4):
                    pp = bank(7 if m % 2 == 0 else 5)
                    for c in range(4):
                        s.mm(pp[0][:, :nt], pwt.ap[:, c, m * 128:(m + 1) * 128], u2.ap[:, c, :nt], c == 0, c == 3,
                             [pwt.b(), u2.b()], [pp[1]])
                    s.stt(mo.ap[:, m, :nt], pp[0][:, :nt], v4.ap[:, m, 3:4], gi.ap[:, m, :nt], ALU.add, ALU.mult,
                          [pp[1], v4.b(), gi.b()], [mo.b()])
                st(sC.ap[:, :, t0:t0 + nt], mo.ap[:, :, :nt], [mo.b()], [sC.b(j)])
            return {0: [p_stats], 1: [p_z], 2: [p_silu], 3: [p_pw]}

        conv(0)
        for j in range(NB):
            tk2.arm(lnpw_parts(j))
            if j + 1 < NB:
                conv(j + 1)
            tk2.flush()
            if mcalls and j in (NB // 3 - 1, 2 * NB // 3 - 1, NB - 1):
                mcalls[1 + [NB // 3 - 1, 2 * NB // 3 - 1, NB - 1].index(j)]()

        s.barrier()
        A.reset()
        PQ = A.alloc([128, NTILES, 1024], BF16)
        DS = [A.alloc([128, 2, 8, 256], BF16) for _ in range(4)]
        Wout = A.alloc([128, KC, D], BF16)
        fwt = A.alloc([128, 4, 128], BF16)
        Fr = [A.alloc([128, 4, 512], BF16) for _ in range(4)]
        Bs = A.alloc([128, 4, 256], F32)
        alt = A.alloc([128, 2], BF16)
        gBi = [A.alloc([128, 4, 512], BF16) for _ in range(2)]
        mix = [A.alloc([128, 8, 512], BF16) for _ in range(2)]
        fin_bufs = None
        if last:
            fin_bufs = (A.alloc([128, D], F32), A.alloc([128, KC, 512], BF16), A.alloc([128, 16], F32),
                        [A.alloc([128, D], F32) for _ in range(2)])
        stages = [(khb, sg) for khb in range(NKB - 1, -1, -1) for sg in range(NSG2)]
        ds_issued = [0]

        def issue_ds(upto):
            while ds_issued[0] <= min(upto, len(stages) - 1):
                i_ = ds_issued[0]
                ds_ = DS[i_ % 4]
                ld(ds_.ap, dfth_d[stages[i_][0], stages[i_][1]], [], [ds_.b()])
                ds_issued[0] += 1
        issue_ds(0)
        pq_src = pqd.ap.rearrange("t p n -> p t n")
        for t8 in range(0, NTILES, 4):
            n_ = min(4, NTILES - t8)
            ld(PQ.ap[:, t8:t8 + n_, :], pq_src[:, t8:t8 + n_, :], [pqd.b(t) for t in range(t8, t8 + n_)],
               [PQ.b(t8 // 4)])
            if (t8 // 4) % 2 == 1 and ds_issued[0] <= 3:
                issue_ds(ds_issued[0])
        issue_ds(3)
        ld(alt.ap, alt_d, [], [alt.b()])
        pq_last = [PQ.b((NTILES - 1) // 4)]
        wsrc = w["wout"].rearrange("(kc p) n -> p kc n", p=128)
        ldc(Wout.ap, wsrc, pq_last, [Wout.b()])
        ldc(fwt.ap, w["fw"].rearrange("g d e -> d g e"), pq_last, [fwt.b()])
        cnt = [0]
        nds = [0]

        def prefetch(j):
            stx, t0, nt = blocks[j]
            c_ = cnt[0]
            cnt[0] += 1
            xb = X[c_ % 2]
            st(xb.ap[:, :, :nt], xres.ap[:, :, t0:t0 + nt], [xres.b(j)] + pq_last, [xb.b()])
            mx = mix[c_ % 2]
            st(mx.ap[:, 0:4, :nt], sC.ap[:, :, t0:t0 + nt], [sC.b(j)] + pq_last, [mx.b()])
            gi = gBi[c_ % 2]
            st(gi.ap[:, :, :nt], sB.ap[:, :, t0:t0 + nt], [sB.b(j)] + pq_last, [gi.b()])
            return (j, xb, mx, gi)

        def downstream(pf_, Ft, banks):
            j, xb, mx, gi = pf_
            stx, t0, nt = blocks[j]
            for g in range(4):
                pf = bank(banks[g % len(banks)])
                s.mm(pf[0][:, :nt], fwt.ap[:, g, :], Ft.ap[:, g, :nt], True, True, [fwt.b(), Ft.b()], [pf[1]])
                s.tt(mx.ap[:, 4 + g, :nt], pf[0][:, :nt], gi.ap[:, g, :nt], ALU.mult, [pf[1], gi.b()], [mx.b()])
            out_stage(l, j, Wout, mx, last, fin_bufs, banks=banks, xb=xb)

        NPAIR = NKB // 2

        def nyquist():
            pny, pnyb = bank(4)
            for g in range(4):
                for tile in range(NTX):
                    s.mm(pny[:, 2 * g:2 * g + 2], PQ.ap[:, tile, g * 256:g * 256 + 128], alt.ap, tile == 0,
                         tile == NTX - 1, [PQ.b(tile // 4), alt.b()], [pnyb])
            s.cp(Fr[1].ap[:, :, 0:1], pny[:, 0:8].rearrange("p (g x) -> p g x", g=4)[:, :, 0:1], [pnyb], [Fr[1].b()])

        def dft_khb(khb, aset):
            for sg in range(NSG2):
                assert stages[nds[0]] == (khb, sg)
                issue_ds(nds[0] + 3)
                ds = DS[nds[0] % 4]
                nds[0] += 1
                for n_ in range(8):
                    tile = sg * 8 + n_
                    lastn = (sg == NSG2 - 1 and n_ == 7)
                    for g in range(4):
                        pa, pab = bank(4 * aset + g)
                        s.mm(pa[:, 0:256], PQ.ap[:, tile, g * 256:g * 256 + 128], ds.ap[:, 0, n_, :],
                             (sg == 0 and n_ == 0), False, [PQ.b(tile // 4), ds.b()], [pab])
                        s.mm(pa[:, 256:512], PQ.ap[:, tile, g * 256 + 128:g * 256 + 256], ds.ap[:, 1, n_, :],
                             False, lastn, [PQ.b(tile // 4), ds.b()], [pab])

        def evac_khb(khb, aset, Dt, Mt, Mn, tb_m):
            accv = PS[:, 2048 * aset:2048 * aset + 2048].rearrange("p (g x) -> p g x", g=4)
            accb = [bkb[4 * aset + g] for g in range(4)]
            s.cp(Bs.ap, accv[:, :, 256:512], accb, [Bs.b()], e="act")
            hoff = (khb % 2) * 256
            s.tt(Dt.ap[:, :, hoff:hoff + 256], accv[:, :, 0:256], Bs.ap, ALU.add, accb + [Bs.b()], [Dt.b()])
            if khb % 2 == 1:
                s.tt(Mt.ap[:, :, 256:0:-1], accv[:, :, 0:256], Bs.ap, ALU.subtract, accb + [Bs.b()], [Mt.b()])
            else:
                s.tt(Mt.ap[:, :, 511:256:-1], accv[:, :, 1:256], Bs.ap[:, :, 1:256], ALU.subtract,
                     accb + [Bs.b()], [Mt.b()])
                if tb_m + 1 < NKB:
                    s.tt(Mn.ap[:, :, 0:1], accv[:, :, 0:1], Bs.ap[:, :, 0:1], ALU.subtract,
                         accb + [Bs.b()], [Mn.b()])

        pend = None
        pfs = {}
        pfs[0] = (prefetch(NPAIR - 1), prefetch(NKB - NPAIR))
        for m in range(NPAIR):
            tb_d = NPAIR - 1 - m
            tb_m = NKB - 1 - tb_d
            Dt = Fr[(2 * m) % 4]
            Mt = Fr[(2 * m + 1) % 4]
            Mn = Fr[(2 * m + 3) % 4]
            dft_khb(2 * tb_d + 1, 0)
            if m == 0:
                nyquist()
            evac_khb(2 * tb_d + 1, 0, Dt, Mt, Mn, tb_m)
            if pend is not None:
                pm, pDt, pMt = pend
                downstream(pfs[pm][0], pDt, (4, 5, 6, 7))
                downstream(pfs[pm][1], pMt, (4, 5, 6, 7))
                pfs[m] = (prefetch(tb_d), prefetch(tb_m))
            dft_khb(2 * tb_d, 1)
            evac_khb(2 * tb_d, 1, Dt, Mt, Mn, tb_m)
            pend = (m, Dt, Mt)
        pm, pDt, pMt = pend
        downstream(pfs[pm][0], pDt, (0, 1, 2, 3))
        downstream(pfs[pm][1], pMt, (0, 1, 2, 3))
        jc = NB - 1
        pfc = prefetch(jc)
        Fc = Fr[(2 * NPAIR) % 4]
        acc = [bank(g) for g in range(4)]
        for n_ in range(NTC):
            tile = NTX + n_
            for g in range(4):
                s.mm(acc[g][0][:, :LC], PQ.ap[:, tile, g * 256:g * 256 + 128], dftc.ap[:, 0, n_, :],
                     n_ == 0, False, [PQ.b(tile // 4), dftc.b()], [acc[g][1]])
                s.mm(acc[g][0][:, :LC], PQ.ap[:, tile, g * 256 + 128:g * 256 + 256], dftc.ap[:, 1, n_, :],
                     False, n_ == NTC - 1, [PQ.b(tile // 4), dftc.b()], [acc[g][1]])
        for g in range(4):
            s.cp(Fc.ap[:, g, :LC], acc[g][0][:, :LC], [acc[g][1]], [Fc.b()], e="act" if g % 2 == 0 else "dve")
        downstream(pfc, Fc, (4, 5, 6, 7))

    SHM = [i ^ 16 for i in range(32)]

    def odd_layer(l, last, ctx_needed, nxt):
        win_next = [None]
        w = LW[l]
        SC = 0.125
        s.barrier()
        A.reset()
        kdup = [A.alloc([128, T], BF16) for _ in range(2)]
        Vf = A.alloc([128, NTILES, 128], BF16)
        Win = A.alloc([128, KC, ODD_IN], BF16)
        hT = [A.alloc([128, KC, 512], BF16) for _ in range(2)]
        sq = A.alloc([128, KC, 512], BF16)
        rt = A.alloc([128, 512], F32)
        tmp32 = [A.alloc([128, 512], F32) for _ in range(2)]
        ccxo = A.alloc([128, 4, 512], BF16)
        cbgo = A.alloc([128, 4, 512], BF16)
        gDo = A.alloc([128, 4, 512], BF16)
        qo = A.alloc([128, 4, 512], BF16)
        q32 = [A.alloc([128, 512], F32) for _ in range(2)]
        sh = A.alloc([128, 512], F32)
        t1 = A.alloc([128, 512], F32)
        t2 = A.alloc([128, 512], F32)
        cosb = A.alloc([128, 512], F32)
        sinb = A.alloc([128, 512], F32)
        cosq = A.alloc([128, 512], F32)
        sinq = A.alloc([128, 512], F32)
        vtmp = A.alloc([128, 512], BF16)
        win_dma(l, Win, ODD_IN)
        ld(cwc.ap[:, :, 0:3], w["cw"], [], [cwc.b()])
        ld(sinkc.ap, w["sink"], [], [sinkc.b()])
        s.ts(nsinkc.ap, sinkc.ap, -1.0, None, ALU.mult, ALU.bypass, [sinkc.b()], [nsinkc.b()])
        nq = [0]

        def rope(src, dst_ap, dst_buf, nt, cb=None, sb_=None):
            cb = cosb if cb is None else cb
            sb_ = sinb if sb_ is None else sb_
            s.op("dve", lambda e: e.stream_shuffle(sh.ap[:, :nt], src.ap[:, :nt], SHM), [src.b()], [sh.b()])
            s.tt(t2.ap[:, :nt], sh.ap[:, :nt], sb_.ap[:, :nt], ALU.mult, [sh.b(), sb_.b()], [t2.b()], e="pool")
            s.tt(t1.ap[:, :nt], src.ap[:, :nt], cb.ap[:, :nt], ALU.mult, [src.b(), cb.b()], [t1.b()])
            s.tt(dst_ap, t1.ap[:, :nt], t2.ap[:, :nt], ALU.add, [t1.b(), t2.b()], [dst_buf])

        front(l, 0, hT[0], sq, rt)
        win_bias(l, Win, ODD_IN)
        nbk = 0
        tk = Ticker()
        for j, (stx, t0, nt) in enumerate(blocks):
            if j + 1 < NB:
                tk.arm(front_parts(l, j + 1, hT[(j + 1) % 2], sq, rt))
            tk.tick()
            h = hT[j % 2]
            bW = lambda m: biasW.ap[:, m, stx:stx + 1]
            full = (stx == 0) or ctx_needed

            def nb():
                nonlocal nbk
                nbk += 1
                return bank(2 + nbk % 4)

            def proj(m, func, dst_ap, dst_buf):
                p = nb()
                lin(p, Win, m * 128, h, nt)
                s.act(dst_ap, p[0][:, :nt], func, [p[1], biasW.b((m) // 8)], [dst_buf], bias=bW(m))
                tk.tick()

            def proj_mul(m, other, dst_ap, dst_buf):
                p = nb()
                lin(p, Win, m * 128, h, nt)
                s.stt(dst_ap, p[0][:, :nt], bW(m), other.ap[:, :nt], ALU.add, ALU.mult,
                      [p[1], biasW.b((m) // 8), other.b()], [dst_buf])
                tk.tick()

            if stx == 0:
                ld(cosb.ap[:, :nt], ropec_d[:, t0:t0 + nt], [], [cosb.b()])
                ld(sinb.ap[:, :nt], ropes_d[:, t0:t0 + nt], [], [sinb.b()])
                ld(cosq.ap[:, :nt], ropecq_d[:, t0:t0 + nt], [], [cosq.b()])
                ld(sinq.ap[:, :nt], ropesq_d[:, t0:t0 + nt], [], [sinq.b()])
            if full:
                for ci in range(4):
                    tm = tmp32[ci % 2]
                    proj(4 + ci, AF.Identity, tm.ap[:, :nt], tm.b())
                    proj_mul(8 + ci, tm, ccxo.ap[:, ci, :nt], ccxo.b())
                st(sC.ap[:, :, t0:t0 + nt], ccxo.ap[:, :, :nt], [ccxo.b()], [sC.b(j)])
                for ci in range(4):
                    tm = tmp32[ci % 2]
                    proj(12 + ci, AF.Silu, tm.ap[:, :nt], tm.b())
                    proj_mul(ci, tm, cbgo.ap[:, ci, :nt], cbgo.b())
                st(sA.ap[:, :, t0:t0 + nt], cbgo.ap[:, :, :nt], [cbgo.b()], [sA.b(j)])
                for ci in range(4):
                    qq = q32[nq[0] % 2]; nq[0] += 1
                    proj(16 + ci, AF.Identity, qq.ap[:, :nt], qq.b())
                    if stx == 0:
                        rope(qq, qo.ap[:, ci, :nt], qo.b(), nt, cosq, sinq)
                    else:
                        s.ts(qo.ap[:, ci, :nt], qq.ap[:, :nt], SC, None, ALU.mult, ALU.bypass, [qq.b()], [qo.b()])
                st(sD.ap[:, :, t0:t0 + nt], qo.ap[:, :, :nt], [qo.b()], [sD.b(j)])
                for ci in range(4):
                    proj(23 + ci, AF.Silu, gDo.ap[:, ci, :nt], gDo.b())
                st(sB.ap[:, :, t0:t0 + nt], gDo.ap[:, :, :nt], [gDo.b()], [sB.b(j)])
            for hk in range(2):
                qq = q32[nq[0] % 2]; nq[0] += 1
                proj(20 + hk, AF.Identity, qq.ap[:, :nt], qq.b())
                if stx == 0:
                    rope(qq, kdup[hk].ap[:, t0:t0 + nt], kdup[hk].b(j), nt)
                else:
                    s.cp(kdup[hk].ap[:, t0:t0 + nt], qq.ap[:, :nt], [qq.b()], [kdup[hk].b(j)])
            proj(22, AF.Identity, vtmp.ap[:, :nt], vtmp.b())
            tk.flush()
            for i in range(nt // 128):
                pt, ptb = bank(6 + i % 2)
                ptv = pt.bitcast(BF16)
                s.tr(ptv[:, 0:128], vtmp.ap[:, i * 128:(i + 1) * 128], identb.ap, [vtmp.b(), identb.b()], [ptb])
                s.cp(Vf.ap[:, t0 // 128 + i, :], ptv[:, 0:128], [ptb], [Vf.b(j)])

        s.barrier()
        A.reset()
        kd2 = [A.alloc([128, T], BF16) for _ in range(2)]
        for a_, b_ in zip(kd2, kdup):
            a_.bufs = b_.bufs
        Vf2 = A.alloc([128, NTILES, 128], BF16)
        Vf2.bufs = Vf.bufs
        Wout = A.alloc([128, KC, D], BF16)
        diag = A.alloc([128, 4, 3, 128], BF16)
        qpad = [A.alloc([128, 4, 2, 128], BF16) for _ in range(2)]
        p_sb = [A.alloc([128, 648], BF16) for _ in range(3)]
        sinkb2 = A.alloc([128, 8, 2], BF16)
        pT_sb = [A.alloc([128, 640], BF16) for _ in range(3)]
        SQ = [A.alloc([128, 6, 8], F32) for _ in range(2)]
        att_sb = [A.alloc([128, 512], BF16) for _ in range(2)]
        mix = [A.alloc([128, 8, 512], BF16) for _ in range(2)]
        gDi = [A.alloc([128, 4, 512], BF16) for _ in range(2)]
        cbgi = [A.alloc([128, 4, 512], BF16) for _ in range(2)]
        ccxi = [A.alloc([128, 4, 514], BF16) for _ in range(2)]
        fin_bufs = None
        if last:
            fin_bufs = (A.alloc([128, D], F32), A.alloc([128, KC, 512], BF16), A.alloc([128, 16], F32),
                        [A.alloc([128, D], F32) for _ in range(2)])
        ldc(Wout.ap, w["wout"].rearrange("(kc p) n -> p kc n", p=128), [], [Wout.b()])
        mcalls = None
        if nxt is not None:
            wst = A.alloc([128, KC, D], BF16)
            mcalls = mod_calls(nxt, wst.ap, wst.b(), pbank=4)
            mcalls[0]()
            if nxt % 2 == 0:
                WinN = A.alloc_top([128, KC, EVEN_IN], BF16)
                win_next[0] = WinN
        for c in range(4):
            for k in range(3):
                s.ts(diag.ap[:, c, k, :], identb.ap, cwc.ap[:, c, k:k + 1], None, ALU.mult, ALU.bypass,
                     [identb.b(), cwc.b()], [diag.b()])
        for qp in qpad:
            s.op("pool", lambda e: e.memset(qp.ap, 0.0), [], [qp.b()])
        s.op("dve", lambda e: e.memset(sinkb2.ap, NEG), [], [sinkb2.b()])
        s.cp(sinkb2.ap[:, :, 0:1], sinkc.ap.unsqueeze(2), [sinkc.b(), sinkb2.b()], [sinkb2.b()])
        ablocks = [j for j, (stx, t0, nt) in enumerate(blocks) if stx == 0 or ctx_needed]
        units = []
        for j in ablocks:
            for qi in range(blocks[j][2] // 128):
                for hh in range(8):
                    units.append((j, qi, hh))
        NU = len(units)
        qb_of = {}
        nqb = [0]

        def load_block(j):
            stx, t0, nt = blocks[j]
            sstart = 0 if stx == 0 else L
            send = L if stx == 0 else T
            xb = X[j % 2]
            st(xb.ap[:, :, :nt], xres.ap[:, :, t0:t0 + nt], [xres.b(j)], [xb.b()])
            gi = gDi[j % 2]
            st(gi.ap[:, :, :nt], sB.ap[:, :, t0:t0 + nt], [sB.b(j)], [gi.b()])
            cbi = cbgi[j % 2]
            st(cbi.ap[:, :, :nt], sA.ap[:, :, t0:t0 + nt], [sA.b(j)], [cbi.b()])
            cxi = ccxi[j % 2]
            lo = max(t0 - 1, sstart)
            hi = min(t0 + nt + 1, send)
            nbr = [sC.b(jj) for jj in (j - 1, j, j + 1) if 0 <= jj < NB and blocks[jj][0] == stx]
            if lo == t0:
                s.op("dve", lambda e: e.memset(cxi.ap[:, :, 0:1], 0.0), [], [cxi.b()])
            if hi == t0 + nt:
                s.op("dve", lambda e: e.memset(cxi.ap[:, :, nt + 1:nt + 2], 0.0), [], [cxi.b()])
            st(cxi.ap[:, :, lo - (t0 - 1):hi - (t0 - 1)], sC.ap[:, :, lo:hi], nbr, [cxi.b()])

        def geom(u):
            j, qi, hh = units[u]
            stx, t0, nt = blocks[j]
            tq = t0 + qi * 128
            if stx == 0:
                i_ = tq // 128
                wt = [t for t in (i_ - 1, i_, i_ + 1) if 0 <= t < NTX]
                moff = (wt[0] - (i_ - 1)) * 128
            else:
                wt = []
                moff = 0
            nwin = len(wt) * 128
            ctile = [NTX + t for t in range(NTC)]
            kbufs = sorted(set([t // 4 for t in wt] + [NB - 1]))
            par = u % 2
            base = par * 1024
            Sb = [bkb[2 * par], bkb[2 * par + 1]]
            wc0 = base + 512 - nwin
            cc0 = base + 512
            return dict(j=j, qi=qi, hh=hh, stx=stx, t0=t0, nt=nt, tq=tq, wt=wt, moff=moff, nwin=nwin,
                        kbufs=kbufs, par=par, Sb=Sb, wc0=wc0, cc0=cc0, r0=wc0, nk=nwin + LC,
                        ktiles=wt + ctile, hk=hh // 4, qc=hh // 2, qbi=u // 8)

        def load_q(u):
            g = geom(u)
            qp = qpad[g["qbi"] % 2]
            tq = g["tq"]
            ld(qp.ap[0:64, :, 0, :], sD.ap[0:64, :, tq:tq + 128], [sD.b(g["j"])], [qp.b()])
            ld(qp.ap[64:128, :, 1, :], sD.ap[64:128, :, tq:tq + 128], [sD.b(g["j"])], [qp.b()])

        deferred = {}

        def stA1(u):
            g = geom(u)
            j, qi, hh = g["j"], g["qi"], g["hh"]
            if hh == 0:
                if u == 0:
                    load_block(ablocks[0])
                    if len(ablocks) > 1:
                        load_block(ablocks[1])
                    load_q(0)
                if u + 8 < NU:
                    load_q(u + 8)
            qp = qpad[g["qbi"] % 2]
            sq_ = SQ[g["qbi"] % 2]
            hk, qc, par = g["hk"], g["qc"], g["par"]
            Sb = g["Sb"]
            psb = p_sb[u % 3]
            kR = [kd2[hk].b(kb_) for kb_ in g["kbufs"]]
            nwin, wc0, cc0 = g["nwin"], g["wc0"], g["cc0"]
            lq = qp.ap[:, qc, hh % 2, :]
            if nwin:
                wt0 = g["wt"][0] * 128
                s.mm(PS[:, wc0:wc0 + nwin], identb.ap, maskt.ap[:, g["moff"]:g["moff"] + nwin], True, False,
                     [identb.b(), maskt.b()], [Sb[0]])
                s.mm(PS[:, wc0:wc0 + nwin], lq, kd2[hk].ap[:, wt0:wt0 + nwin], False, True, [qp.b()] + kR, [Sb[0]])
            s.mm(PS[:, cc0:cc0 + LC], lq, kd2[hk].ap[:, L:L + LC], True, True, [qp.b()] + kR, [Sb[1]])
            s.mm(PS[:, cc0 + LC:cc0 + LC + 2], identb.ap, sinkb2.ap[:, hh, :], True, True,
                 [identb.b(), sinkb2.b()], [Sb[1]])
            r0, nk = g["r0"], g["nk"]
            Sr = PS[:, r0:r0 + nk + 2]
            s.op("dve", lambda e: e.reduce_max(sq_.ap[:, 1, hh:hh + 1], Sr, AX.X, negate=True), Sb, [sq_.b(hh)])
            s.act(psb.ap[:, :nk + 2], Sr, AF.Exp, Sb + [sq_.b(hh)], [psb.b(), sq_.b(hh)],
                  bias=sq_.ap[:, 1, hh:hh + 1], scale=1.0, accum=sq_.ap[:, 2, hh:hh + 1])

        def stB(u):
            g = geom(u)
            par, nk = g["par"], g["nk"]
            pTp, pTb = bank(4 + par)
            pTv = pTp.bitcast(BF16)
            for t in range(nk // 128):
                s.tr(pTv[:, t * 128:(t + 1) * 128], p_sb[u % 3].ap[:, t * 128:(t + 1) * 128], identb.ap,
                     [p_sb[u % 3].b(), identb.b()], [pTb])
            s.cp(pT_sb[u % 3].ap[:, :nk], pTv[:, :nk], [pTb], [pT_sb[u % 3].b(0)], e="dve" if u % 3 == 2 else "act")
            if g["hh"] == 7:
                sq_ = SQ[g["qbi"] % 2]
                sqa = [sq_.b(h_) for h_ in range(8)]
                s.op("dve", lambda e: e.reciprocal(sq_.ap[:, 5, :], sq_.ap[:, 2, :]), sqa, [sq_.b(9)])

        def stC(u):
            g = geom(u)
            j, qi, hh = g["j"], g["qi"], g["hh"]
            par, nk, hk = g["par"], g["nk"], g["hk"]
            o_ps, o_b = bank(6 + g["qbi"] % 2)
            vR = [Vf2.b(kb_) for kb_ in g["kbufs"]]
            nkt = nk // 128
            for t in range(nkt):
                s.mm(o_ps[:, hh * 64:(hh + 1) * 64], pT_sb[u % 3].ap[:, t * 128:(t + 1) * 128],
                     Vf2.ap[:, g["ktiles"][t], hk * 64:(hk + 1) * 64], t == 0, t == nkt - 1,
                     [pT_sb[u % 3].b(0)] + vR, [o_b])
            if hh == 7:
                qbi = g["qbi"]
                sq_ = SQ[qbi % 2]
                asb = att_sb[qbi % 2]
                s.tt(asb.ap.rearrange("p (h d) -> p h d", h=8), o_ps.rearrange("p (h d) -> p h d", h=8),
                     sq_.ap[:, 5, :].unsqueeze(2).broadcast_to([128, 8, 64]), ALU.mult, [o_b, sq_.b(9)], [asb.b()])
                lastq = (qi == g["nt"] // 128 - 1)

                def fin2(j=j, qi=qi, qbi=qbi, lastq=lastq):
                    asb_ = att_sb[qbi % 2]
                    mx_ = mix[j % 2]
                    gi = gDi[j % 2]
                    p7, p7b = bank(4 + qbi % 2)
                    p7v = p7.bitcast(BF16)
                    for c in range(4):
                        s.tr(p7v[:, c * 128:(c + 1) * 128], asb_.ap[:, c * 128:(c + 1) * 128], identb.ap,
                             [asb_.b(), identb.b()], [p7b])
                    s.tt(mx_.ap[:, 4:8, qi * 128:(qi + 1) * 128], p7v[:, 0:512].rearrange("p (c q) -> p c q", c=4),
                         gi.ap[:, :, qi * 128:(qi + 1) * 128], ALU.mult, [p7b, gi.b()], [mx_.b()])
                    if lastq:
                        stO3(j)
                deferred.setdefault(u + 4 + 2, []).append(fin2)

        def stO3(j):
            stx, t0, nt = blocks[j]
            mx_ = mix[j % 2]
            cbi = cbgi[j % 2]
            cxi = ccxi[j % 2]
            for c in range(4):
                pc = bank(4 + c % 2)
                for k in range(3):
                    s.mm(pc[0][:, :nt], diag.ap[:, c, k, :], cxi.ap[:, c, k:k + nt], k == 0, k == 2,
                         [diag.b(), cxi.b()], [pc[1]])
                s.tt(mx_.ap[:, c, :nt], pc[0][:, :nt], cbi.ap[:, c, :nt], ALU.mult, [pc[1], cbi.b()], [mx_.b()])
            out_stage(l, j, Wout, mx_, last, fin_bufs, banks=(4, 5))
            nx2 = ablocks.index(j) + 2
            if nx2 < len(ablocks):
                load_block(ablocks[nx2])

        step = 0
        while step < NU + 4 or any(k >= step for k in deferred):
            if step < NU:
                stA1(step)
            if 0 <= step - 2 < NU:
                stB(step - 2)
            if 0 <= step - 4 < NU:
                stC(step - 4)
            for fn in deferred.pop(step, []):
                fn()
            if mcalls and step in (NU // 4, NU // 2, (3 * NU) // 4):
                mcalls[1 + [NU // 4, NU // 2, (3 * NU) // 4].index(step)]()
            if win_next[0] is not None and step == (7 * NU) // 8:
                wsrcn = LW[nxt]["win"].rearrange("(kc p) n -> p kc n", p=128)
                c0 = 0
                while c0 < EVEN_IN:
                    cw_ = min(1024, EVEN_IN - c0)
                    ldc(win_next[0].ap[:, :, c0:c0 + cw_], wsrcn[:, :, c0:c0 + cw_], [], [win_next[0].b(c0 // 1024)])
                    c0 += cw_
            step += 1
        return win_next[0]

    win_pre = None
    for li, l in enumerate(layers):
        last = (li == len(layers) - 1)
        nxt = None if last else layers[li + 1]
        if l % 2 == 0:
            even_layer(l, last, nxt, first=(li == 0), win_pre=win_pre)
            win_pre = None
        else:
            win_pre = odd_layer(l, last, (l < 3), nxt)
    s.barrier()
    return nc


_TABLES = {}


def _cols(v, nchunk):
    return np.ascontiguousarray(np.asarray(v, np.float32).reshape(nchunk, 128).T)


def tables(L, LC):
    key = (L, LC)
    if key in _TABLES:
        return _TABLES[key]
    t = {}
    d = np.arange(128)
    angd = 2 * np.pi * ((d[:, None] * d[None, :]) % 128) / 128.0
    t["csd"] = np.concatenate([np.cos(angd), np.sin(angd)], axis=1) / np.sqrt(128.0)
    t["csd"] = t["csd"].astype(NPBF)
    NKB = L // 512
    NTX = L // 128
    NSG = NTX // NCS
    n = np.arange(L, dtype=np.int64)
    ang = 2 * np.pi * ((n[:, None] * n[None, :]) % L).astype(np.float64) / L
    tab = np.stack([np.cos(ang), -np.sin(ang)]) / np.sqrt(float(L))
    del ang
    NSG2 = NTX // 8
    tab = tab[:, :, :L // 2].astype(np.float32).reshape(2, NSG2, 8, 128, NKB, 256)
    t["dfth"] = np.ascontiguousarray(tab.transpose(4, 1, 3, 0, 2, 5)).astype(NPBF)
    del tab
    t["alt"] = np.repeat((np.where(np.arange(128) % 2 == 0, 1.0, -1.0) / np.sqrt(float(L)))[:, None], 2,
                         axis=1).astype(np.float32).astype(NPBF)
    nch = LC // 128
    m = np.arange(LC, dtype=np.int64)
    angc = 2 * np.pi * ((m[:, None] * m[None, :]) % LC) / float(LC)
    tc = (np.stack([np.cos(angc), -np.sin(angc)]) / np.sqrt(float(LC))).reshape(2, nch, 128, LC)
    t["dftc"] = np.ascontiguousarray(tc.transpose(2, 0, 1, 3)).astype(NPBF)
    p = np.arange(128)
    dd = p % 64
    a = dd // 32
    sgn = (dd % 32) // 16
    f = dd % 16
    inv = np.power(10000.0, -f / 16.0)
    tt_ = np.arange(L)
    pos = np.where(a[:, None] == 0, (tt_ // 64)[None, :], (tt_ % 64)[None, :]).astype(np.float64)
    angr = pos * inv[:, None]
    t["ropec"] = np.cos(angr).astype(np.float32)
    t["ropes"] = (np.sin(angr) * np.where(sgn == 0, -1.0, 1.0)[:, None]).astype(np.float32)
    t["ropecq"] = (t["ropec"] * np.float32(0.125)).astype(np.float32)
    t["ropesq"] = (t["ropes"] * np.float32(0.125)).astype(np.float32)
    r = np.arange(128)[:, None]
    jj = np.arange(384)[None, :]
    t["mask"] = np.where((jj >= r) & (jj <= r + 256), 0.0, NEG).astype(np.float32).astype(NPBF)
    t["identb"] = np.eye(128, dtype=np.float32).astype(NPBF)
    t["identf"] = np.eye(128, dtype=np.float32)
    _TABLES[key] = t
    return t


def prep(inputs, L, LC, layers):
    tb = tables(L, LC)
    f = lambda a: np.ascontiguousarray(np.asarray(a, dtype=np.float32))
    shared = dict(tb)
    shared["fin"] = _cols(inputs["final_norm"], KC)
    shared["finbc"] = np.ascontiguousarray(np.broadcast_to(f(inputs["final_norm"])[None, :], (128, D)))
    for l in layers:
        i = l // 2
        pre = "e_" if l % 2 == 0 else "o_"
        shared["wmod%d" % l] = f(inputs[pre + "w_mod"][i])
        shared["bmod%d" % l] = _cols(inputs[pre + "b_mod"][i], 24)
        shared["norm%d" % l] = _cols(inputs[pre + "norm"][i], KC)
        shared["wout%d" % l] = f(inputs[pre + "w_out"][i])
        if l % 2 == 0:
            shared["win%d" % l] = f(inputs["e_w_in"][i])
            cw = f(inputs["e_a_conv_w"][i])
            shared["cw%d" % l] = np.ascontiguousarray(cw.reshape(31, 4, 128).transpose(2, 1, 0))
            v4 = np.stack([f(inputs["e_a_conv_b"][i]), f(inputs["e_a_ln_g"][i]),
                           f(inputs["e_a_ln_b"][i]), f(inputs["e_a_pw_b"][i])], axis=-1)
            shared["v4%d" % l] = np.ascontiguousarray(v4.reshape(4, 128, 4).transpose(1, 0, 2))
            shared["pw%d" % l] = f(inputs["e_a_pw_w"][i])
            shared["fw%d" % l] = f(inputs["e_b_fw"][i])
        else:
            wi = f(inputs["o_w_in"][i])
            k0 = wi[:, 2560:2624]
            k1 = wi[:, 2624:2688]
            shared["win%d" % l] = np.ascontiguousarray(np.concatenate(
                [wi[:, 0:2560], k0, k0, k1, k1, wi[:, 2688:2816], wi[:, 2816:3328]], axis=1))
            cw = f(inputs["o_c_conv_w"][i])
            shared["cw%d" % l] = np.ascontiguousarray(cw.reshape(3, 4, 128).transpose(2, 1, 0))
            shared["sink%d" % l] = np.ascontiguousarray(
                np.broadcast_to(f(inputs["o_sink"][i])[None, :], (128, 8)))
    x = np.asarray(inputs["x"], np.float32)
    ctx = np.asarray(inputs["ctx"], np.float32)
    c = np.asarray(inputs["c"], np.float32)
    cctx = np.asarray(inputs["c_ctx"], np.float32)
    maps = []
    for b in range(x.shape[0]):
        m = dict(shared)
        m["x"] = np.ascontiguousarray(x[b])
        m["ctx"] = np.ascontiguousarray(ctx[b])
        m["ccols"] = np.ascontiguousarray(np.stack([_cols(c[b], KC), _cols(cctx, KC)], axis=-1))
        maps.append(m)
    return maps


_NC_CACHE = {}


def run(inputs, L, LC, layers):
    key = (L, LC, tuple(layers))
    if key not in _NC_CACHE:
        _NC_CACHE[key] = build(L, LC, layers)
    nc = _NC_CACHE[key]
    maps = prep(inputs, L, LC, layers)
    res = run_bass_kernel_spmd(nc, maps, core_ids=list(range(len(maps))))
    return np.stack([np.asarray(r["out"], np.float32) for r in res.results], axis=0)


def kernel(**inputs):
    return run(inputs, 4096, 256, [0, 1, 2, 3])
```

```python
import math
import numpy as np
import ml_dtypes
import concourse.bass as bass
import concourse.mybir as mybir
from concourse.bass_utils import run_bass_kernel_spmd

F32 = mybir.dt.float32
BF16 = mybir.dt.bfloat16
U8 = mybir.dt.uint8
AF = mybir.ActivationFunctionType
ALU = mybir.AluOpType
AX = mybir.AxisListType
NPBF = ml_dtypes.bfloat16

D = 1024
KC = 8
EPS = 1e-6
NEG = -1e30
UPAD = 16
NCS = 4
NST = 4
ARENA = 166 * 1024


class Buf:
    __slots__ = ("w", "r")

    def __init__(self):
        self.w = None
        self.r = {}


class TT:
    def __init__(self, ap):
        self.ap = ap
        self.bufs = {}

    def b(self, key=0):
        x = self.bufs.get(key)
        if x is None:
            x = self.bufs[key] = Buf()
        return x


class Sched:
    NRING = 20

    def __init__(self, nc):
        self.nc = nc
        self.eng = dict(pe=nc.tensor, dve=nc.vector, act=nc.scalar, pool=nc.gpsimd, sp=nc.sync)
        self.nk = 0
        self.sem = {}
        self.semk = {}
        self.cnt = {}
        self.seen = {}
        for n in self.eng:
            self.sem[n] = nc.alloc_semaphore("e_" + n)
            self.semk[n] = self._k()
            self.cnt[n] = 0
            self.seen[n] = {}
        self.ring = {}
        self.ringk = {}
        self.rval = {}
        self.rpos = {}
        for q in ("sp", "pool"):
            self.ring[q] = [nc.alloc_semaphore("q_%s_%d" % (q, i)) for i in range(self.NRING)]
            self.ringk[q] = [self._k() for _ in range(self.NRING)]
            self.rval[q] = [0] * self.NRING
            self.rpos[q] = 0

    def _k(self):
        self.nk += 1
        return self.nk

    def _wait(self, e, tok):
        k, sem, val, src = tok
        if src == "pe" and e == "pe":
            return
        if self.seen[e].get(k, 0) >= val:
            return
        self.eng[e].wait_ge(sem, val)
        self.seen[e][k] = val

    def _deps(self, e, R, W):
        for b in R:
            if b.w is not None:
                self._wait(e, b.w)
        for b in W:
            if b.w is not None:
                self._wait(e, b.w)
            for t in b.r.values():
                self._wait(e, t)

    @staticmethod
    def _commit(tok, R, W):
        for b in R:
            b.r[tok[0]] = tok
        for b in W:
            b.w = tok
            b.r = {}

    def op(self, e, fn, R=(), W=()):
        self._deps(e, R, W)
        inst = fn(self.eng[e])
        self.cnt[e] += 1
        inst.then_inc(self.sem[e], 1)
        self._commit((self.semk[e], self.sem[e], self.cnt[e], e), R, W)

    def dma(self, q, out, in_, R=(), W=(), **kw):
        self._deps(q, R, W)
        i = self.rpos[q]
        self.rpos[q] = (i + 1) % self.NRING
        sem = self.ring[q][i]
        k = self.ringk[q][i]
        prev = self.rval[q][i]
        if prev:
            self._wait(q, (k, sem, prev, "dma"))
        inst = self.eng[q].dma_start(out=out, in_=in_, **kw)
        inst.then_inc(sem, 16)
        self.rval[q][i] = prev + 16
        self._commit((k, sem, prev + 16, "dma"), R, W)

    def barrier(self):
        toks = []
        for n in self.eng:
            if self.cnt[n] > 0:
                toks.append((self.semk[n], self.sem[n], self.cnt[n], n))
        for q in self.ring:
            for i in range(self.NRING):
                if self.rval[q][i] > 0:
                    toks.append((self.ringk[q][i], self.ring[q][i], self.rval[q][i], "dma"))
        for e in self.eng:
            for t in toks:
                if t[3] == e and e in ("pe", "sp"):
                    continue
                self._wait(e, t)

    def mm(self, out, lhsT, rhs, start, stop, R, W):
        self.op("pe", lambda e: e.matmul(out, lhsT, rhs, start=start, stop=stop), R, W)

    def tr(self, out, in_, ident, R, W):
        self.op("pe", lambda e: e.transpose(out, in_, ident), R, W)

    def act(self, out, in_, func, R, W, bias=None, scale=None, accum=None):
        kw = {}
        if bias is not None:
            kw["bias"] = bias
        if scale is not None:
            kw["scale"] = scale
        if accum is not None:
            kw["accum_out"] = accum
        self.op("act", lambda e: e.activation(out, in_, func, **kw), R, W)

    def tt(self, out, in0, in1, op, R, W, e="dve"):
        self.op(e, lambda g: g.tensor_tensor(out, in0, in1, op), R, W)

    def ts(self, out, in0, s1, s2, op0, op1, R, W, e="dve"):
        self.op(e, lambda g: g.tensor_scalar(out, in0, s1, s2, op0, op1), R, W)

    def stt(self, out, in0, scalar, in1, op0, op1, R, W):
        self.op("dve", lambda g: g.scalar_tensor_tensor(out, in0, scalar, in1, op0, op1), R, W)

    def cp(self, out, in_, R, W, e="dve"):
        if e == "act":
            self.op("act", lambda g: g.copy(out, in_), R, W)
        else:
            self.op(e, lambda g: g.tensor_copy(out, in_), R, W)


class Arena:
    def __init__(self, ap_u8, size):
        self.ap = ap_u8
        self.size = size
        self.off = 0
        self.top = size

    def reset(self):
        self.off = 0
        self.top = self.size

    def alloc_top(self, shape, dtype):
        esz = 4 if dtype == F32 else 2
        n = esz
        for d in shape[1:]:
            n *= d
        off = (self.top - n) // 64 * 64
        save = self.off
        self.off = off
        t = self.alloc(shape, dtype)
        self.off = save
        self.top = off
        assert self.off <= self.top
        return t

    def alloc(self, shape, dtype):
        esz = 4 if dtype == F32 else 2
        n = esz
        for d in shape[1:]:
            n *= d
        off = (self.off + 63) // 64 * 64
        assert off + n <= self.top, ("arena overflow", off + n, self.top)
        v = self.ap[:, off:off + n].bitcast(dtype)
        if len(shape) == 3:
            v = v.rearrange("p (a b) -> p a b", a=shape[1])
        elif len(shape) == 4:
            v = v.rearrange("p (a b c) -> p a b c", a=shape[1], b=shape[2])
        self.off = off + n
        return TT(v)


EVEN_IN = 2560
ODD_IN = 3328 + 128


def build(L, LC, layers, dbg=False):
    T = L + LC
    NTX = L // 128
    NTC = LC // 128
    NTILES = NTX + NTC
    NKB = L // 512
    NSG = NTX // NCS
    UW = T + 4 * UPAD

    def ucol(t):
        return t + UPAD if t < L else t + 3 * UPAD

    blocks = [(0, kb * 512, 512) for kb in range(NKB)] + [(1, L, LC)]
    NB = len(blocks)

    nc = bass.Bass("TRN2", target_bir_lowering=False)
    s = Sched(nc)

    def din(name, shape, dt=F32):
        return nc.dram_tensor(name, list(shape), dt, kind="ExternalInput").ap()

    x_d = din("x", [L, D])
    ctx_d = din("ctx", [LC, D])
    ccols_d = din("ccols", [128, KC, 2])
    fin_d = din("fin", [128, KC])
    finbc_d = din("finbc", [128, D])
    csd_d = din("csd", [128, 256], BF16)
    NSG2 = NTX // 8
    dfth_d = din("dfth", [NKB, NSG2, 128, 2, 8, 256], BF16)
    alt_d = din("alt", [128, 2], BF16)
    dftc_d = din("dftc", [128, 2, NTC, LC], BF16)
    ropec_d = din("ropec", [128, L])
    ropes_d = din("ropes", [128, L])
    ropecq_d = din("ropecq", [128, L])
    ropesq_d = din("ropesq", [128, L])
    mask_d = din("mask", [128, 384], BF16)
    identb_d = din("identb", [128, 128], BF16)
    identf_d = din("identf", [128, 128])
    LW = {}
    for l in layers:
        w = {}
        w["wmod"] = din("wmod%d" % l, [D, 3 * D])
        w["bmod"] = din("bmod%d" % l, [128, 24])
        w["norm"] = din("norm%d" % l, [128, KC])
        w["wout"] = din("wout%d" % l, [D, D])
        if l % 2 == 0:
            w["win"] = din("win%d" % l, [D, EVEN_IN])
            w["cw"] = din("cw%d" % l, [128, 4, 31])
            w["v4"] = din("v4%d" % l, [128, 4, 4])
            w["pw"] = din("pw%d" % l, [512, 512])
            w["fw"] = din("fw%d" % l, [4, 128, 128])
        else:
            w["win"] = din("win%d" % l, [D, ODD_IN])
            w["cw"] = din("cw%d" % l, [128, 4, 3])
            w["sink"] = din("sink%d" % l, [128, 8])
        LW[l] = w
    out_d = nc.dram_tensor("out", [L, D], F32, kind="ExternalOutput").ap()

    def dscr(name, shape, dt):
        return TT(nc.dram_tensor(name, list(shape), dt, kind="Internal").ap())

    xres = dscr("xres", [128, KC, T], F32)
    pqd = dscr("pqd", [NTILES, 128, 1024], BF16)
    sA = dscr("sA", [128, 4, T], BF16)
    sB = dscr("sB", [128, 4, T], BF16)
    sC = dscr("sC", [128, 4, T], BF16)
    sD = dscr("sD", [128, 4, T], BF16)

    def sb(name, shape, dt):
        return TT(nc.alloc_sbuf_tensor(name, list(shape), dt).ap())

    identb = sb("identb_s", [128, 128], BF16)
    identf = sb("identf_s", [128, 128], F32)
    onesb = sb("onesb", [128, 128], BF16)
    ones512 = sb("ones512", [128, 128], BF16)
    epsc = sb("epsc", [128, 1], F32)
    csd = sb("csd_s", [128, 256], BF16)
    maskt = sb("mask_s", [128, 384], BF16)
    dftc = sb("dftc_s", [128, 2, NTC, LC], BF16)
    ccols = sb("ccols_s", [128, KC, 2], F32)
    silc = sb("silc", [128, KC, 2], BF16)
    fin = sb("fin_s", [128, KC], F32)
    X = [sb("X%d" % i, [128, KC, 512], F32) for i in range(2)]
    modc = {l: sb("modc%d" % l, [128, 24, 2], F32) for l in layers}
    Gc = {l: sb("G%d" % l, [128, KC, 2], F32) for l in layers}
    Sbf = {l: sb("Sbf%d" % l, [128, KC, 2], BF16) for l in layers}
    biasW = sb("biasW", [128, 28, 2], F32)
    normc = sb("normc", [128, KC], F32)
    bmodc = sb("bmodc", [128, 24], F32)
    v4 = sb("v4_s", [128, 4, 4], F32)
    cwc = sb("cwc", [128, 4, 31], F32)
    sinkc = sb("sinkc", [128, 8], F32)
    nsinkc = sb("nsinkc", [128, 8], F32)
    arena_t = nc.alloc_sbuf_tensor("arena", [128, ARENA], U8).ap()
    A = Arena(arena_t, ARENA)

    PS = nc.alloc_psum_tensor("ps", [128, 4096], F32).ap()
    bkb = [Buf() for _ in range(8)]

    def bank(i):
        return PS[:, i * 512:(i + 1) * 512], bkb[i]

    ld = lambda out, in_, R, W: s.dma("sp", out, in_, R, W)
    st = lambda out, in_, R, W: s.dma("pool", out, in_, R, W)
    ldc = lambda out, in_, R, W: s.dma("pool", out, in_, R, W)

    ld(identb.ap, identb_d, [], [identb.b()])
    ld(identf.ap, identf_d, [], [identf.b()])
    ld(csd.ap, csd_d, [], [csd.b()])
    ld(maskt.ap, mask_d, [], [maskt.b()])
    ld(dftc.ap, dftc_d, [], [dftc.b()])
    ld(ccols.ap, ccols_d, [], [ccols.b()])
    ld(fin.ap, fin_d, [], [fin.b()])
    s.op("dve", lambda e: e.memset(onesb.ap, 1.0), [], [onesb.b()])
    s.op("dve", lambda e: e.memset(ones512.ap, 1.0 / 512.0), [], [ones512.b()])
    s.op("dve", lambda e: e.memset(epsc.ap, EPS), [], [epsc.b()])
    s.act(silc.ap, ccols.ap, AF.Silu, [ccols.b()], [silc.b()])

    def mod_calls(l, stage_ap, stage_buf, pbank=7):
        w = LW[l]
        wsrc = w["wmod"].rearrange("(kc p) n -> p kc n", p=128)
        mc = modc[l]

        def load(part):
            ldc(stage_ap, wsrc[:, :, part * D:(part + 1) * D], [], [stage_buf])

        def mm(part):
            pm, pmb = bank(pbank)
            for m in range(8):
                for c in range(KC):
                    s.mm(pm[:, 2 * m:2 * m + 2], stage_ap[:, c, m * 128:(m + 1) * 128],
                         silc.ap[:, c, :], c == 0, c == KC - 1, [stage_buf, silc.b()], [pmb])
            s.tt(mc.ap[:, part * 8:(part + 1) * 8, :], pm[:, 0:16].rearrange("p (a b) -> p a b", a=8),
                 bmodc.ap[:, part * 8:(part + 1) * 8].unsqueeze(2).broadcast_to([128, 8, 2]), ALU.add,
                 [pmb, bmodc.b()], [mc.b()])

        def c0():
            ld(bmodc.ap, w["bmod"], [], [bmodc.b()])
            ld(normc.ap, w["norm"], [], [normc.b()])
            load(0)

        def c1():
            mm(0)
            load(1)

        def c2():
            mm(1)
            load(2)

        def c3():
            mm(2)
            s.stt(Gc[l].ap, mc.ap[:, 8:16, :], 1.0, normc.ap.unsqueeze(2).broadcast_to([128, KC, 2]),
                  ALU.add, ALU.mult, [mc.b(), normc.b()], [Gc[l].b()])
            s.cp(Sbf[l].ap, mc.ap[:, 0:8, :], [mc.b()], [Sbf[l].b()])
        return [c0, c1, c2, c3]

    def win_dma(l, Win, nin):
        w = LW[l]
        wsrc = w["win"].rearrange("(kc p) n -> p kc n", p=128)
        c0 = 0
        while c0 < nin:
            cw_ = min(1024, nin - c0)
            ldc(Win.ap[:, :, c0:c0 + cw_], wsrc[:, :, c0:c0 + cw_], [], [Win.b(c0 // 1024)])
            c0 += cw_

    def win_bias(l, Win, nin):
        nchunk = nin // 128
        pm, pmb = bank(0)
        for p0_ in range(0, nchunk, 8):
            p1_ = min(p0_ + 8, nchunk)
            for m in range(p0_, p1_):
                for c in range(KC):
                    s.mm(pm[:, 2 * m:2 * m + 2], Win.ap[:, c, m * 128:(m + 1) * 128], Sbf[l].ap[:, c, :],
                         c == 0, c == KC - 1, [Win.b(m // 8), Sbf[l].b()], [pmb])
            s.cp(biasW.ap[:, p0_:p1_, :], pm[:, 2 * p0_:2 * p1_].rearrange("p (a b) -> p a b", b=2),
                 [pmb], [biasW.b(p0_ // 8)])

    def front(l, j, hT, sq, rt, from_dram=True, gcol=None, fb=(0, 1), xb=None):
        stx, t0, nt = blocks[j]
        if xb is None:
            xb = X[j % 2]
        if from_dram:
            ld(xb.ap[:, :, :nt], xres.ap[:, :, t0:t0 + nt], [xres.b(j)], [xb.b()])
        s.act(sq.ap[:, :, :nt], xb.ap[:, :, :nt], AF.Square, [xb.b()], [sq.b()])
        p0, p0b = bank(fb[0])
        p1, p1b = bank(fb[1])
        for c in range(KC):
            s.mm(p0[:, :nt], onesb.ap, sq.ap[:, c, :nt], c == 0, c == KC - 1, [sq.b(), onesb.b()], [p0b])
        s.act(rt.ap[:, :nt], p0[:, :nt], AF.Sqrt, [p0b, epsc.b()], [rt.b()], bias=epsc.ap, scale=1.0 / D)
        s.op("dve", lambda e: e.reciprocal(p1[:, :nt], rt.ap[:, :nt]), [rt.b()], [p1b])
        for c in range(KC):
            g = Gc[l].ap[:, c, stx:stx + 1] if gcol is None else gcol[:, c:c + 1]
            s.stt(hT.ap[:, c, :nt], xb.ap[:, c, :nt], g, p1[:, :nt], ALU.mult, ALU.mult,
                  [xb.b(), p1b, Gc[l].b() if gcol is None else fin.b()], [hT.b()])

    def front_parts(l, j, hT, sq, rt, from_dram=True):
        stx, t0, nt = blocks[j]
        xb = X[j % 2]
        p0, p0b = bank(0)
        p1, p1b = bank(1)

        def pre():
            if from_dram:
                ld(xb.ap[:, :, :nt], xres.ap[:, :, t0:t0 + nt], [xres.b(j)], [xb.b()])
            s.act(sq.ap[:, :, :nt], xb.ap[:, :, :nt], AF.Square, [xb.b()], [sq.b()])

        def mid():
            for c in range(KC):
                s.mm(p0[:, :nt], onesb.ap, sq.ap[:, c, :nt], c == 0, c == KC - 1, [sq.b(), onesb.b()], [p0b])
            s.act(rt.ap[:, :nt], p0[:, :nt], AF.Sqrt, [p0b, epsc.b()], [rt.b()], bias=epsc.ap, scale=1.0 / D)
            s.op("dve", lambda e: e.reciprocal(p1[:, :nt], rt.ap[:, :nt]), [rt.b()], [p1b])

        def mk(c):
            def f():
                s.stt(hT.ap[:, c, :nt], xb.ap[:, c, :nt], Gc[l].ap[:, c, stx:stx + 1], p1[:, :nt], ALU.mult, ALU.mult,
                      [xb.b(), p1b, Gc[l].b()], [hT.b()])
            return f
        return {0: [pre], 3: [mid], **{4 + c: [mk(c)] for c in range(KC)}}

    class Ticker:
        def __init__(self):
            self.parts = {}
            self.k = 0

        def arm(self, parts):
            self.flush()
            self.parts = dict(parts)
            self.k = 0

        def tick(self):
            for fn in self.parts.pop(self.k, []):
                fn()
            self.k += 1

        def flush(self):
            for k_ in sorted(self.parts):
                for fn in self.parts[k_]:
                    fn()
            self.parts = {}

    def lin(pb, Win, col0, hT, nt, extraR=()):
        p, b = pb
        for c in range(KC):
            s.mm(p[:, :nt], Win.ap[:, c, col0:col0 + 128], hT.ap[:, c, :nt], c == 0, c == KC - 1,
                 [Win.b(col0 // 1024), hT.b()] + list(extraR), [b])

    init_n = [0]

    def init_block(j, TI):
        stx, t0, nt = blocks[j]
        xb = X[j % 2]
        for i in range(nt // 128):
            n = init_n[0]
            ti = TI[n % 2]
            init_n[0] += 1
            src_ = x_d[t0 + i * 128:t0 + (i + 1) * 128, :] if stx == 0 else \
                ctx_d[t0 - L + i * 128:t0 - L + (i + 1) * 128, :]
            ld(ti.ap, src_, [], [ti.b()])
            for h in range(2):
                pt, ptb = bank(h)
                for c in range(4):
                    cc = h * 4 + c
                    s.tr(pt[:, c * 128:(c + 1) * 128], ti.ap[:, cc * 128:(cc + 1) * 128], identf.ap,
                         [ti.b(), identf.b()], [ptb])
                s.cp(xb.ap[:, h * 4:(h + 1) * 4, i * 128:(i + 1) * 128],
                     pt.rearrange("p (a b) -> p a b", a=4), [ptb], [xb.b()],
                     e="act" if h == 0 else "dve")
        ld(xres.ap[:, :, t0:t0 + nt], xb.ap[:, :, :nt], [xb.b()], [xres.b(j)])

    def out_stage(l, j, Wout, mix, last, fin_bufs, banks=(4, 5, 6, 7), xb=None):
        stx, t0, nt = blocks[j]
        if xb is None:
            xb = X[j % 2]
        for m in range(KC):
            pb = bank(banks[m % len(banks)])
            p, b = pb
            for c in range(KC):
                s.mm(p[:, :nt], Wout.ap[:, c, m * 128:(m + 1) * 128], mix.ap[:, c, :nt], c == 0, c == KC - 1,
                     [Wout.b(), mix.b()], [b])
            s.stt(xb.ap[:, m, :nt], p[:, :nt], modc[l].ap[:, 16 + m, stx:stx + 1], xb.ap[:, m, :nt],
                  ALU.mult, ALU.add, [b, modc[l].b(), xb.b()], [xb.b()])
            if last and stx == 0:
                s.act(fin_bufs[1].ap[:, m, :nt], xb.ap[:, m, :nt], AF.Square, [xb.b()], [fin_bufs[1].b()])
        if not last:
            st(xres.ap[:, :, t0:t0 + nt], xb.ap[:, :, :nt], [xb.b()], [xres.b(j)])
        elif stx == 0:
            finbc, sq, fst, TO = fin_bufs
            if not finbc.bufs:
                ld(finbc.ap, finbc_d, [], [finbc.b()])
            nti = nt // 128
            pss, pssb = bank(banks[-1])
            for i in range(nti):
                for c in range(KC):
                    s.mm(pss[:, 2 * i:2 * i + 2], sq.ap[:, c, i * 128:(i + 1) * 128], onesb.ap[:, 0:2],
                         c == 0, c == KC - 1, [sq.b(), onesb.b()], [pssb])
            s.act(fst.ap[:, 0:2 * nti], pss[:, 0:2 * nti], AF.Ln, [pssb, epsc.b()], [fst.b()],
                  bias=epsc.ap, scale=1.0 / D)
            s.act(fst.ap[:, 8:8 + 2 * nti], fst.ap[:, 0:2 * nti], AF.Exp, [fst.b()], [fst.b()], scale=-0.5)
            for i in range(nti):
                to = TO[i % 2]
                for h in range(2):
                    pt, ptb = bank(banks[-2 + h])
                    for c in range(4):
                        cc = h * 4 + c
                        s.tr(pt[:, c * 128:(c + 1) * 128], xb.ap[:, cc, i * 128:(i + 1) * 128], identf.ap,
                             [xb.b(), identf.b()], [ptb])
                    s.stt(to.ap[:, h * 512:(h + 1) * 512], pt, fst.ap[:, 8 + 2 * i:9 + 2 * i],
                          finbc.ap[:, h * 512:(h + 1) * 512], ALU.mult, ALU.mult,
                          [ptb, fst.b(), finbc.b()], [to.b()])
                st(out_d[t0 + i * 128:t0 + (i + 1) * 128, :], to.ap, [to.b()], [])

    def even_layer(l, last, nxt, first=False, win_pre=None):
        w = LW[l]
        if not first:
            s.barrier()
        A.reset()
        Win = A.alloc_top([128, KC, EVEN_IN], BF16)
        if win_pre is not None:
            Win.bufs = win_pre.bufs
        U = A.alloc([128, 4, UW], BF16)
        hT = [A.alloc([128, KC, 512], BF16) for _ in range(2)]
        sq = A.alloc([128, KC, 512], BF16)
        rt = A.alloc([128, 512], F32)
        sig = [A.alloc([128, 512], F32) for _ in range(2)]
        gAo = [A.alloc([128, 4, 512], BF16) for _ in range(2)]
        gBo = [A.alloc([128, 4, 512], BF16) for _ in range(2)]
        bvT = [A.alloc([128, 4, 512], BF16) for _ in range(2)]
        pqo = [A.alloc([128, 1024], BF16) for _ in range(2)]
        TI = None
        if first:
            TI = [A.alloc([128, D], F32) for _ in range(2)]
            wst0 = A.alloc([128, KC, D], BF16)
            for fn in mod_calls(l, wst0.ap, wst0.b(), pbank=7):
                fn()
        if win_pre is None:
            win_dma(l, Win, EVEN_IN)
        ld(v4.ap, w["v4"], [], [v4.b()])
        ld(cwc.ap, w["cw"], [], [cwc.b()])
        s.op("pool", lambda e: e.memset(U.ap, 0.0), [], [U.b(j) for j in range(NB)])
        if first:
            init_block(0, TI)
        front(l, 0, hT[0], sq, rt, from_dram=not first)
        win_bias(l, Win, EVEN_IN)
        nbk = 0
        ntile = 0
        tk = Ticker()
        for j, (stx, t0, nt) in enumerate(blocks):
            if j + 1 < NB:
                if first:
                    init_block(j + 1, TI)
                tk.arm(front_parts(l, j + 1, hT[(j + 1) % 2], sq, rt, from_dram=not first))
            tk.tick()
            h = hT[j % 2]
            bW = lambda m: biasW.ap[:, m, stx:stx + 1]
            for ci in range(4):
                pg = bank(2 + nbk % 4); nbk += 1
                lin(pg, Win, (4 + ci) * 128, h, nt)
                sg = sig[ci % 2]
                s.act(sg.ap[:, :nt], pg[0][:, :nt], AF.Sigmoid, [pg[1], biasW.b((4 + ci) // 8)], [sg.b()], bias=bW(4 + ci))
                tk.tick()
                pv = bank(2 + nbk % 4); nbk += 1
                lin(pv, Win, ci * 128, h, nt)
                s.stt(U.ap[:, ci, ucol(t0):ucol(t0) + nt], pv[0][:, :nt], bW(ci), sg.ap[:, :nt], ALU.add, ALU.mult,
                      [pv[1], sg.b(), biasW.b((ci) // 8)], [U.b(j)])
                tk.tick()
            go = gAo[j % 2]
            for ci in range(4):
                p = bank(2 + nbk % 4); nbk += 1
                lin(p, Win, (8 + ci) * 128, h, nt)
                s.act(go.ap[:, ci, :nt], p[0][:, :nt], AF.Silu, [p[1], biasW.b((8 + ci) // 8)], [go.b()], bias=bW(8 + ci))
                tk.tick()
            st(sA.ap[:, :, t0:t0 + nt], go.ap[:, :, :nt], [go.b()], [sA.b(j)])
            tk.flush()
            bv = bvT[j % 2]
            for ci in range(4):
                p = bank(2 + nbk % 4); nbk += 1
                lin(p, Win, (12 + ci) * 128, h, nt)
                s.act(bv.ap[:, ci, :nt], p[0][:, :nt], AF.Identity, [p[1], biasW.b((12 + ci) // 8)], [bv.b()], bias=bW(12 + ci))
            for i in range(nt // 128):
                pb0 = 6 if ntile % 2 == 0 else 0
                pq = PS[:, pb0 * 512:(pb0 + 2) * 512]
                for g in range(4):
                    s.mm(pq[:, g * 256:(g + 1) * 256], bv.ap[:, g, i * 128:(i + 1) * 128], csd.ap, True, True,
                         [bv.b(), csd.b()], [bkb[pb0 + g // 2]])
                po = pqo[ntile % 2]
                s.cp(po.ap[:, 0:512], pq[:, 0:512], [bkb[pb0]], [po.b()], e="act")
                s.cp(po.ap[:, 512:1024], pq[:, 512:1024], [bkb[pb0 + 1]], [po.b()], e="dve")
                tile = (t0 // 128) + i
                st(pqd.ap[tile], po.ap, [po.b()], [pqd.b(tile)])
                ntile += 1
            go = gBo[j % 2]
            for ci in range(4):
                p = bank(2 + nbk % 4); nbk += 1
                lin(p, Win, (16 + ci) * 128, h, nt)
                s.act(go.ap[:, ci, :nt], p[0][:, :nt], AF.Silu, [p[1], biasW.b((16 + ci) // 8)], [go.b()], bias=bW(16 + ci))
            st(sB.ap[:, :, t0:t0 + nt], go.ap[:, :, :nt], [go.b()], [sB.b(j)])

        s.barrier()
        A.reset()
        U2 = A.alloc([128, 4, UW], BF16)
        U2.bufs = U.bufs
        diag = A.alloc([128, 4, 31, 128], BF16)
        pwt = A.alloc([128, 4, 512], BF16)
        v32 = [A.alloc([128, 4, 512], F32) for _ in range(2)]
        vb = [A.alloc([128, 4, 512], BF16) for _ in range(2)]
        vsq = [A.alloc([128, 4, 512], BF16) for _ in range(2)]
        m2 = A.alloc([128, 512], F32)
        var = A.alloc([128, 512], F32)
        sd = A.alloc([128, 512], F32)
        z = A.alloc([128, 4, 512], F32)
        u2 = A.alloc([128, 4, 512], BF16)
        gAi = [A.alloc([128, 4, 512], BF16) for _ in range(2)]
        mAo = [A.alloc([128, 4, 512], BF16) for _ in range(2)]
        ldc(pwt.ap, w["pw"].rearrange("(kc p) n -> p kc n", p=128), [], [pwt.b()])
        for c in range(4):
            for k in range(31):
                s.ts(diag.ap[:, c, k, :], identb.ap, cwc.ap[:, c, k:k + 1], None, ALU.mult, ALU.bypass,
                     [identb.b(), cwc.b()], [diag.b()])
        mcalls = mod_calls(nxt, X[1].ap.bitcast(BF16), X[1].b()) if nxt is not None else None
        if mcalls:
            mcalls[0]()

        tk2 = Ticker()

        def conv(j):
            stx, t0, nt = blocks[j]
            vv = j % 2
            gi = gAi[vv]
            ld(gi.ap[:, :, :nt], sA.ap[:, :, t0:t0 + nt], [sA.b(j)], [gi.b()])
            nbrs = [U2.b(jj) for jj in (j - 1, j, j + 1) if 0 <= jj < NB and blocks[jj][0] == stx]
            for c in range(4):
                pc = bank(c)
                for k in range(31):
                    c0 = ucol(t0) + k - 15
                    s.mm(pc[0][:, :nt], diag.ap[:, c, k, :], U2.ap[:, c, c0:c0 + nt], k == 0, k == 30,
                         [diag.b()] + nbrs, [pc[1]])
                s.act(v32[vv].ap[:, c, :nt], pc[0][:, :nt], AF.Identity, [pc[1], v4.b()], [v32[vv].b()],
                      bias=v4.ap[:, c, 0:1])
                s.act(vsq[vv].ap[:, c, :nt], pc[0][:, :nt], AF.Square, [pc[1], v4.b()], [vsq[vv].b()],
                      bias=v4.ap[:, c, 0:1])
                s.cp(vb[vv].ap[:, c, :nt], v32[vv].ap[:, c, :nt], [v32[vv].b()], [vb[vv].b()])
                tk2.tick()

        def lnpw_parts(j):
            stx, t0, nt = blocks[j]
            vv = j % 2
            gi = gAi[vv]
            pmean = bank(4)
            pmsq = bank(5)
            prs = bank(6)

            def p_stats():
                for c in range(4):
                    s.mm(pmean[0][:, :nt], ones512.ap, vb[vv].ap[:, c, :nt], c == 0, c == 3,
                         [ones512.b(), vb[vv].b()], [pmean[1]])
                for c in range(4):
                    s.mm(pmsq[0][:, :nt], ones512.ap, vsq[vv].ap[:, c, :nt], c == 0, c == 3,
                         [ones512.b(), vsq[vv].b()], [pmsq[1]])
                s.act(m2.ap[:, :nt], pmean[0][:, :nt], AF.Square, [pmean[1]], [m2.b()])
                s.tt(var.ap[:, :nt], pmsq[0][:, :nt], m2.ap[:, :nt], ALU.subtract, [pmsq[1], m2.b()], [var.b()])
                s.act(sd.ap[:, :nt], var.ap[:, :nt], AF.Sqrt, [var.b(), epsc.b()], [sd.b()], bias=epsc.ap, scale=1.0)
                s.op("dve", lambda e: e.reciprocal(prs[0][:, :nt], sd.ap[:, :nt]), [sd.b()], [prs[1]])

            def p_z():
                s.tt(z.ap[:, :, :nt], v32[vv].ap[:, :, :nt],
                     pmean[0][:, :nt].unsqueeze(1).broadcast_to([128, 4, nt]),
                     ALU.subtract, [v32[vv].b(), pmean[1]], [z.b()])
                s.tt(z.ap[:, :, :nt], z.ap[:, :, :nt], prs[0][:, :nt].unsqueeze(1).broadcast_to([128, 4, nt]),
                     ALU.mult, [z.b(), prs[1]], [z.b()])

            def p_silu():
                for c in range(4):
                    s.act(u2.ap[:, c, :nt], z.ap[:, c, :nt], AF.Silu, [z.b(), v4.b()], [u2.b()],
                          bias=v4.ap[:, c, 2:3], scale=v4.ap[:, c, 1:2])

            def p_pw():
                mo = mAo[vv]
                for m in range(4):
                    pp = bank(7 if m % 2 == 0 else 5)
                    for c in range(4):
                        s.mm(pp[0][:, :nt], pwt.ap[:, c, m * 128:(m + 1) * 128], u2.ap[:, c, :nt], c == 0, c == 3,
                             [pwt.b(), u2.b()], [pp[1]])
                    s.stt(mo.ap[:, m, :nt], pp[0][:, :nt], v4.ap[:, m, 3:4], gi.ap[:, m, :nt], ALU.add, ALU.mult,
                          [pp[1], v4.b(), gi.b()], [mo.b()])
                st(sC.ap[:, :, t0:t0 + nt], mo.ap[:, :, :nt], [mo.b()], [sC.b(j)])
            return {0: [p_stats], 1: [p_z], 2: [p_silu], 3: [p_pw]}

        conv(0)
        for j in range(NB):
            tk2.arm(lnpw_parts(j))
            if j + 1 < NB:
                conv(j + 1)
            tk2.flush()
            if mcalls and j in (NB // 3 - 1, 2 * NB // 3 - 1, NB - 1):
                mcalls[1 + [NB // 3 - 1, 2 * NB // 3 - 1, NB - 1].index(j)]()

        s.barrier()
        A.reset()
        PQ = A.alloc([128, NTILES, 1024], BF16)
        DS = [A.alloc([128, 2, 8, 256], BF16) for _ in range(4)]
        Wout = A.alloc([128, KC, D], BF16)
        fwt = A.alloc([128, 4, 128], BF16)
        Fr = [A.alloc([128, 4, 512], BF16) for _ in range(4)]
        Bs = A.alloc([128, 4, 256], F32)
        alt = A.alloc([128, 2], BF16)
        gBi = [A.alloc([128, 4, 512], BF16) for _ in range(2)]
        mix = [A.alloc([128, 8, 512], BF16) for _ in range(2)]
        fin_bufs = None
        if last:
            fin_bufs = (A.alloc([128, D], F32), A.alloc([128, KC, 512], BF16), A.alloc([128, 16], F32),
                        [A.alloc([128, D], F32) for _ in range(2)])
        stages = [(khb, sg) for khb in range(NKB - 1, -1, -1) for sg in range(NSG2)]
        ds_issued = [0]

        def issue_ds(upto):
            while ds_issued[0] <= min(upto, len(stages) - 1):
                i_ = ds_issued[0]
                ds_ = DS[i_ % 4]
                ld(ds_.ap, dfth_d[stages[i_][0], stages[i_][1]], [], [ds_.b()])
                ds_issued[0] += 1
        issue_ds(0)
        pq_src = pqd.ap.rearrange("t p n -> p t n")
        for t8 in range(0, NTILES, 4):
            n_ = min(4, NTILES - t8)
            ld(PQ.ap[:, t8:t8 + n_, :], pq_src[:, t8:t8 + n_, :], [pqd.b(t) for t in range(t8, t8 + n_)],
               [PQ.b(t8 // 4)])
            if (t8 // 4) % 2 == 1 and ds_issued[0] <= 3:
                issue_ds(ds_issued[0])
        issue_ds(3)
        ld(alt.ap, alt_d, [], [alt.b()])
        pq_last = [PQ.b((NTILES - 1) // 4)]
        wsrc = w["wout"].rearrange("(kc p) n -> p kc n", p=128)
        ldc(Wout.ap, wsrc, pq_last, [Wout.b()])
        ldc(fwt.ap, w["fw"].rearrange("g d e -> d g e"), pq_last, [fwt.b()])
        cnt = [0]
        nds = [0]

        def prefetch(j):
            stx, t0, nt = blocks[j]
            c_ = cnt[0]
            cnt[0] += 1
            xb = X[c_ % 2]
            st(xb.ap[:, :, :nt], xres.ap[:, :, t0:t0 + nt], [xres.b(j)] + pq_last, [xb.b()])
            mx = mix[c_ % 2]
            st(mx.ap[:, 0:4, :nt], sC.ap[:, :, t0:t0 + nt], [sC.b(j)] + pq_last, [mx.b()])
            gi = gBi[c_ % 2]
            st(gi.ap[:, :, :nt], sB.ap[:, :, t0:t0 + nt], [sB.b(j)] + pq_last, [gi.b()])
            return (j, xb, mx, gi)

        def downstream(pf_, Ft, banks):
            j, xb, mx, gi = pf_
            stx, t0, nt = blocks[j]
            for g in range(4):
                pf = bank(banks[g % len(banks)])
                s.mm(pf[0][:, :nt], fwt.ap[:, g, :], Ft.ap[:, g, :nt], True, True, [fwt.b(), Ft.b()], [pf[1]])
                s.tt(mx.ap[:, 4 + g, :nt], pf[0][:, :nt], gi.ap[:, g, :nt], ALU.mult, [pf[1], gi.b()], [mx.b()])
            out_stage(l, j, Wout, mx, last, fin_bufs, banks=banks, xb=xb)

        NPAIR = NKB // 2

        def nyquist():
            pny, pnyb = bank(4)
            for g in range(4):
                for tile in range(NTX):
                    s.mm(pny[:, 2 * g:2 * g + 2], PQ.ap[:, tile, g * 256:g * 256 + 128], alt.ap, tile == 0,
                         tile == NTX - 1, [PQ.b(tile // 4), alt.b()], [pnyb])
            s.cp(Fr[1].ap[:, :, 0:1], pny[:, 0:8].rearrange("p (g x) -> p g x", g=4)[:, :, 0:1], [pnyb], [Fr[1].b()])

        def dft_khb(khb, aset):
            for sg in range(NSG2):
                assert stages[nds[0]] == (khb, sg)
                issue_ds(nds[0] + 3)
                ds = DS[nds[0] % 4]
                nds[0] += 1
                for n_ in range(8):
                    tile = sg * 8 + n_
                    lastn = (sg == NSG2 - 1 and n_ == 7)
                    for g in range(4):
                        pa, pab = bank(4 * aset + g)
                        s.mm(pa[:, 0:256], PQ.ap[:, tile, g * 256:g * 256 + 128], ds.ap[:, 0, n_, :],
                             (sg == 0 and n_ == 0), False, [PQ.b(tile // 4), ds.b()], [pab])
                        s.mm(pa[:, 256:512], PQ.ap[:, tile, g * 256 + 128:g * 256 + 256], ds.ap[:, 1, n_, :],
                             False, lastn, [PQ.b(tile // 4), ds.b()], [pab])

        def evac_khb(khb, aset, Dt, Mt, Mn, tb_m):
            accv = PS[:, 2048 * aset:2048 * aset + 2048].rearrange("p (g x) -> p g x", g=4)
            accb = [bkb[4 * aset + g] for g in range(4)]
            s.cp(Bs.ap, accv[:, :, 256:512], accb, [Bs.b()], e="act")
            hoff = (khb % 2) * 256
            s.tt(Dt.ap[:, :, hoff:hoff + 256], accv[:, :, 0:256], Bs.ap, ALU.add, accb + [Bs.b()], [Dt.b()])
            if khb % 2 == 1:
                s.tt(Mt.ap[:, :, 256:0:-1], accv[:, :, 0:256], Bs.ap, ALU.subtract, accb + [Bs.b()], [Mt.b()])
            else:
                s.tt(Mt.ap[:, :, 511:256:-1], accv[:, :, 1:256], Bs.ap[:, :, 1:256], ALU.subtract,
                     accb + [Bs.b()], [Mt.b()])
                if tb_m + 1 < NKB:
                    s.tt(Mn.ap[:, :, 0:1], accv[:, :, 0:1], Bs.ap[:, :, 0:1], ALU.subtract,
                         accb + [Bs.b()], [Mn.b()])

        pend = None
        pfs = {}
        pfs[0] = (prefetch(NPAIR - 1), prefetch(NKB - NPAIR))
        for m in range(NPAIR):
            tb_d = NPAIR - 1 - m
            tb_m = NKB - 1 - tb_d
            Dt = Fr[(2 * m) % 4]
            Mt = Fr[(2 * m + 1) % 4]
            Mn = Fr[(2 * m + 3) % 4]
            dft_khb(2 * tb_d + 1, 0)
            if m == 0:
                nyquist()
            evac_khb(2 * tb_d + 1, 0, Dt, Mt, Mn, tb_m)
            if pend is not None:
                pm, pDt, pMt = pend
                downstream(pfs[pm][0], pDt, (4, 5, 6, 7))
                downstream(pfs[pm][1], pMt, (4, 5, 6, 7))
                pfs[m] = (prefetch(tb_d), prefetch(tb_m))
            dft_khb(2 * tb_d, 1)
            evac_khb(2 * tb_d, 1, Dt, Mt, Mn, tb_m)
            pend = (m, Dt, Mt)
        pm, pDt, pMt = pend
        downstream(pfs[pm][0], pDt, (0, 1, 2, 3))
        downstream(pfs[pm][1], pMt, (0, 1, 2, 3))
        jc = NB - 1
        pfc = prefetch(jc)
        Fc = Fr[(2 * NPAIR) % 4]
        acc = [bank(g) for g in range(4)]
        for n_ in range(NTC):
            tile = NTX + n_
            for g in range(4):
                s.mm(acc[g][0][:, :LC], PQ.ap[:, tile, g * 256:g * 256 + 128], dftc.ap[:, 0, n_, :],
                     n_ == 0, False, [PQ.b(tile // 4), dftc.b()], [acc[g][1]])
                s.mm(acc[g][0][:, :LC], PQ.ap[:, tile, g * 256 + 128:g * 256 + 256], dftc.ap[:, 1, n_, :],
                     False, n_ == NTC - 1, [PQ.b(tile // 4), dftc.b()], [acc[g][1]])
        for g in range(4):
            s.cp(Fc.ap[:, g, :LC], acc[g][0][:, :LC], [acc[g][1]], [Fc.b()], e="act" if g % 2 == 0 else "dve")
        downstream(pfc, Fc, (4, 5, 6, 7))

    SHM = [i ^ 16 for i in range(32)]

    def odd_layer(l, last, ctx_needed, nxt):
        win_next = [None]
        w = LW[l]
        SC = 0.125
        s.barrier()
        A.reset()
        kdup = [A.alloc([128, T], BF16) for _ in range(2)]
        Vf = A.alloc([128, NTILES, 128], BF16)
        Win = A.alloc([128, KC, ODD_IN], BF16)
        hT = [A.alloc([128, KC, 512], BF16) for _ in range(2)]
        sq = A.alloc([128, KC, 512], BF16)
        rt = A.alloc([128, 512], F32)
        tmp32 = [A.alloc([128, 512], F32) for _ in range(2)]
        ccxo = A.alloc([128, 4, 512], BF16)
        cbgo = A.alloc([128, 4, 512], BF16)
        gDo = A.alloc([128, 4, 512], BF16)
        qo = A.alloc([128, 4, 512], BF16)
        q32 = [A.alloc([128, 512], F32) for _ in range(2)]
        sh = A.alloc([128, 512], F32)
        t1 = A.alloc([128, 512], F32)
        t2 = A.alloc([128, 512], F32)
        cosb = A.alloc([128, 512], F32)
        sinb = A.alloc([128, 512], F32)
        cosq = A.alloc([128, 512], F32)
        sinq = A.alloc([128, 512], F32)
        vtmp = A.alloc([128, 512], BF16)
        win_dma(l, Win, ODD_IN)
        ld(cwc.ap[:, :, 0:3], w["cw"], [], [cwc.b()])
        ld(sinkc.ap, w["sink"], [], [sinkc.b()])
        s.ts(nsinkc.ap, sinkc.ap, -1.0, None, ALU.mult, ALU.bypass, [sinkc.b()], [nsinkc.b()])
        nq = [0]

        def rope(src, dst_ap, dst_buf, nt, cb=None, sb_=None):
            cb = cosb if cb is None else cb
            sb_ = sinb if sb_ is None else sb_
            s.op("dve", lambda e: e.stream_shuffle(sh.ap[:, :nt], src.ap[:, :nt], SHM), [src.b()], [sh.b()])
            s.tt(t2.ap[:, :nt], sh.ap[:, :nt], sb_.ap[:, :nt], ALU.mult, [sh.b(), sb_.b()], [t2.b()], e="pool")
            s.tt(t1.ap[:, :nt], src.ap[:, :nt], cb.ap[:, :nt], ALU.mult, [src.b(), cb.b()], [t1.b()])
            s.tt(dst_ap, t1.ap[:, :nt], t2.ap[:, :nt], ALU.add, [t1.b(), t2.b()], [dst_buf])

        front(l, 0, hT[0], sq, rt)
        win_bias(l, Win, ODD_IN)
        nbk = 0
        tk = Ticker()
        for j, (stx, t0, nt) in enumerate(blocks):
            if j + 1 < NB:
                tk.arm(front_parts(l, j + 1, hT[(j + 1) % 2], sq, rt))
            tk.tick()
            h = hT[j % 2]
            bW = lambda m: biasW.ap[:, m, stx:stx + 1]
            full = (stx == 0) or ctx_needed

            def nb():
                nonlocal nbk
                nbk += 1
                return bank(2 + nbk % 4)

            def proj(m, func, dst_ap, dst_buf):
                p = nb()
                lin(p, Win, m * 128, h, nt)
                s.act(dst_ap, p[0][:, :nt], func, [p[1], biasW.b((m) // 8)], [dst_buf], bias=bW(m))
                tk.tick()

            def proj_mul(m, other, dst_ap, dst_buf):
                p = nb()
                lin(p, Win, m * 128, h, nt)
                s.stt(dst_ap, p[0][:, :nt], bW(m), other.ap[:, :nt], ALU.add, ALU.mult,
                      [p[1], biasW.b((m) // 8), other.b()], [dst_buf])
                tk.tick()

            if stx == 0:
                ld(cosb.ap[:, :nt], ropec_d[:, t0:t0 + nt], [], [cosb.b()])
                ld(sinb.ap[:, :nt], ropes_d[:, t0:t0 + nt], [], [sinb.b()])
                ld(cosq.ap[:, :nt], ropecq_d[:, t0:t0 + nt], [], [cosq.b()])
                ld(sinq.ap[:, :nt], ropesq_d[:, t0:t0 + nt], [], [sinq.b()])
            if full:
                for ci in range(4):
                    tm = tmp32[ci % 2]
                    proj(4 + ci, AF.Identity, tm.ap[:, :nt], tm.b())
                    proj_mul(8 + ci, tm, ccxo.ap[:, ci, :nt], ccxo.b())
                st(sC.ap[:, :, t0:t0 + nt], ccxo.ap[:, :, :nt], [ccxo.b()], [sC.b(j)])
                for ci in range(4):
                    tm = tmp32[ci % 2]
                    proj(12 + ci, AF.Silu, tm.ap[:, :nt], tm.b())
                    proj_mul(ci, tm, cbgo.ap[:, ci, :nt], cbgo.b())
                st(sA.ap[:, :, t0:t0 + nt], cbgo.ap[:, :, :nt], [cbgo.b()], [sA.b(j)])
                for ci in range(4):
                    qq = q32[nq[0] % 2]; nq[0] += 1
                    proj(16 + ci, AF.Identity, qq.ap[:, :nt], qq.b())
                    if stx == 0:
                        rope(qq, qo.ap[:, ci, :nt], qo.b(), nt, cosq, sinq)
                    else:
                        s.ts(qo.ap[:, ci, :nt], qq.ap[:, :nt], SC, None, ALU.mult, ALU.bypass, [qq.b()], [qo.b()])
                st(sD.ap[:, :, t0:t0 + nt], qo.ap[:, :, :nt], [qo.b()], [sD.b(j)])
                for ci in range(4):
                    proj(23 + ci, AF.Silu, gDo.ap[:, ci, :nt], gDo.b())
                st(sB.ap[:, :, t0:t0 + nt], gDo.ap[:, :, :nt], [gDo.b()], [sB.b(j)])
            for hk in range(2):
                qq = q32[nq[0] % 2]; nq[0] += 1
                proj(20 + hk, AF.Identity, qq.ap[:, :nt], qq.b())
                if stx == 0:
                    rope(qq, kdup[hk].ap[:, t0:t0 + nt], kdup[hk].b(j), nt)
                else:
                    s.cp(kdup[hk].ap[:, t0:t0 + nt], qq.ap[:, :nt], [qq.b()], [kdup[hk].b(j)])
            proj(22, AF.Identity, vtmp.ap[:, :nt], vtmp.b())
            tk.flush()
            for i in range(nt // 128):
                pt, ptb = bank(6 + i % 2)
                ptv = pt.bitcast(BF16)
                s.tr(ptv[:, 0:128], vtmp.ap[:, i * 128:(i + 1) * 128], identb.ap, [vtmp.b(), identb.b()], [ptb])
                s.cp(Vf.ap[:, t0 // 128 + i, :], ptv[:, 0:128], [ptb], [Vf.b(j)])

        s.barrier()
        A.reset()
        kd2 = [A.alloc([128, T], BF16) for _ in range(2)]
        for a_, b_ in zip(kd2, kdup):
            a_.bufs = b_.bufs
        Vf2 = A.alloc([128, NTILES, 128], BF16)
        Vf2.bufs = Vf.bufs
        Wout = A.alloc([128, KC, D], BF16)
        diag = A.alloc([128, 4, 3, 128], BF16)
        qpad = [A.alloc([128, 4, 2, 128], BF16) for _ in range(2)]
        p_sb = [A.alloc([128, 648], BF16) for _ in range(3)]
        sinkb2 = A.alloc([128, 8, 2], BF16)
        pT_sb = [A.alloc([128, 640], BF16) for _ in range(3)]
        SQ = [A.alloc([128, 6, 8], F32) for _ in range(2)]
        att_sb = [A.alloc([128, 512], BF16) for _ in range(2)]
        mix = [A.alloc([128, 8, 512], BF16) for _ in range(2)]
        gDi = [A.alloc([128, 4, 512], BF16) for _ in range(2)]
        cbgi = [A.alloc([128, 4, 512], BF16) for _ in range(2)]
        ccxi = [A.alloc([128, 4, 514], BF16) for _ in range(2)]
        fin_bufs = None
        if last:
            fin_bufs = (A.alloc([128, D], F32), A.alloc([128, KC, 512], BF16), A.alloc([128, 16], F32),
                        [A.alloc([128, D], F32) for _ in range(2)])
        ldc(Wout.ap, w["wout"].rearrange("(kc p) n -> p kc n", p=128), [], [Wout.b()])
        mcalls = None
        if nxt is not None:
            wst = A.alloc([128, KC, D], BF16)
            mcalls = mod_calls(nxt, wst.ap, wst.b(), pbank=4)
            mcalls[0]()
            if nxt % 2 == 0:
                WinN = A.alloc_top([128, KC, EVEN_IN], BF16)
                win_next[0] = WinN
        for c in range(4):
            for k in range(3):
                s.ts(diag.ap[:, c, k, :], identb.ap, cwc.ap[:, c, k:k + 1], None, ALU.mult, ALU.bypass,
                     [identb.b(), cwc.b()], [diag.b()])
        for qp in qpad:
            s.op("pool", lambda e: e.memset(qp.ap, 0.0), [], [qp.b()])
        s.op("dve", lambda e: e.memset(sinkb2.ap, NEG), [], [sinkb2.b()])
        s.cp(sinkb2.ap[:, :, 0:1], sinkc.ap.unsqueeze(2), [sinkc.b(), sinkb2.b()], [sinkb2.b()])
        ablocks = [j for j, (stx, t0, nt) in enumerate(blocks) if stx == 0 or ctx_needed]
        units = []
        for j in ablocks:
            for qi in range(blocks[j][2] // 128):
                for hh in range(8):
                    units.append((j, qi, hh))
        NU = len(units)
        qb_of = {}
        nqb = [0]

        def load_block(j):
            stx, t0, nt = blocks[j]
            sstart = 0 if stx == 0 else L
            send = L if stx == 0 else T
            xb = X[j % 2]
            st(xb.ap[:, :, :nt], xres.ap[:, :, t0:t0 + nt], [xres.b(j)], [xb.b()])
            gi = gDi[j % 2]
            st(gi.ap[:, :, :nt], sB.ap[:, :, t0:t0 + nt], [sB.b(j)], [gi.b()])
            cbi = cbgi[j % 2]
            st(cbi.ap[:, :, :nt], sA.ap[:, :, t0:t0 + nt], [sA.b(j)], [cbi.b()])
            cxi = ccxi[j % 2]
            lo = max(t0 - 1, sstart)
            hi = min(t0 + nt + 1, send)
            nbr = [sC.b(jj) for jj in (j - 1, j, j + 1) if 0 <= jj < NB and blocks[jj][0] == stx]
            if lo == t0:
                s.op("dve", lambda e: e.memset(cxi.ap[:, :, 0:1], 0.0), [], [cxi.b()])
            if hi == t0 + nt:
                s.op("dve", lambda e: e.memset(cxi.ap[:, :, nt + 1:nt + 2], 0.0), [], [cxi.b()])
            st(cxi.ap[:, :, lo - (t0 - 1):hi - (t0 - 1)], sC.ap[:, :, lo:hi], nbr, [cxi.b()])

        def geom(u):
            j, qi, hh = units[u]
            stx, t0, nt = blocks[j]
            tq = t0 + qi * 128
            if stx == 0:
                i_ = tq // 128
                wt = [t for t in (i_ - 1, i_, i_ + 1) if 0 <= t < NTX]
                moff = (wt[0] - (i_ - 1)) * 128
            else:
                wt = []
                moff = 0
            nwin = len(wt) * 128
            ctile = [NTX + t for t in range(NTC)]
            kbufs = sorted(set([t // 4 for t in wt] + [NB - 1]))
            par = u % 2
            base = par * 1024
            Sb = [bkb[2 * par], bkb[2 * par + 1]]
            wc0 = base + 512 - nwin
            cc0 = base + 512
            return dict(j=j, qi=qi, hh=hh, stx=stx, t0=t0, nt=nt, tq=tq, wt=wt, moff=moff, nwin=nwin,
                        kbufs=kbufs, par=par, Sb=Sb, wc0=wc0, cc0=cc0, r0=wc0, nk=nwin + LC,
                        ktiles=wt + ctile, hk=hh // 4, qc=hh // 2, qbi=u // 8)

        def load_q(u):
            g = geom(u)
            qp = qpad[g["qbi"] % 2]
            tq = g["tq"]
            ld(qp.ap[0:64, :, 0, :], sD.ap[0:64, :, tq:tq + 128], [sD.b(g["j"])], [qp.b()])
            ld(qp.ap[64:128, :, 1, :], sD.ap[64:128, :, tq:tq + 128], [sD.b(g["j"])], [qp.b()])

        deferred = {}

        def stA1(u):
            g = geom(u)
            j, qi, hh = g["j"], g["qi"], g["hh"]
            if hh == 0:
                if u == 0:
                    load_block(ablocks[0])
                    if len(ablocks) > 1:
                        load_block(ablocks[1])
                    load_q(0)
                if u + 8 < NU:
                    load_q(u + 8)
            qp = qpad[g["qbi"] % 2]
            sq_ = SQ[g["qbi"] % 2]
            hk, qc, par = g["hk"], g["qc"], g["par"]
            Sb = g["Sb"]
            psb = p_sb[u % 3]
            kR = [kd2[hk].b(kb_) for kb_ in g["kbufs"]]
            nwin, wc0, cc0 = g["nwin"], g["wc0"], g["cc0"]
            lq = qp.ap[:, qc, hh % 2, :]
            if nwin:
                wt0 = g["wt"][0] * 128
                s.mm(PS[:, wc0:wc0 + nwin], identb.ap, maskt.ap[:, g["moff"]:g["moff"] + nwin], True, False,
                     [identb.b(), maskt.b()], [Sb[0]])
                s.mm(PS[:, wc0:wc0 + nwin], lq, kd2[hk].ap[:, wt0:wt0 + nwin], False, True, [qp.b()] + kR, [Sb[0]])
            s.mm(PS[:, cc0:cc0 + LC], lq, kd2[hk].ap[:, L:L + LC], True, True, [qp.b()] + kR, [Sb[1]])
            s.mm(PS[:, cc0 + LC:cc0 + LC + 2], identb.ap, sinkb2.ap[:, hh, :], True, True,
                 [identb.b(), sinkb2.b()], [Sb[1]])
            r0, nk = g["r0"], g["nk"]
            Sr = PS[:, r0:r0 + nk + 2]
            s.op("dve", lambda e: e.reduce_max(sq_.ap[:, 1, hh:hh + 1], Sr, AX.X, negate=True), Sb, [sq_.b(hh)])
            s.act(psb.ap[:, :nk + 2], Sr, AF.Exp, Sb + [sq_.b(hh)], [psb.b(), sq_.b(hh)],
                  bias=sq_.ap[:, 1, hh:hh + 1], scale=1.0, accum=sq_.ap[:, 2, hh:hh + 1])

        def stB(u):
            g = geom(u)
            par, nk = g["par"], g["nk"]
            pTp, pTb = bank(4 + par)
            pTv = pTp.bitcast(BF16)
            for t in range(nk // 128):
                s.tr(pTv[:, t * 128:(t + 1) * 128], p_sb[u % 3].ap[:, t * 128:(t + 1) * 128], identb.ap,
                     [p_sb[u % 3].b(), identb.b()], [pTb])
            s.cp(pT_sb[u % 3].ap[:, :nk], pTv[:, :nk], [pTb], [pT_sb[u % 3].b(0)], e="dve" if u % 4 == 3 else "act")
            if g["hh"] == 7:
                sq_ = SQ[g["qbi"] % 2]
                sqa = [sq_.b(h_) for h_ in range(8)]
                s.op("dve", lambda e: e.reciprocal(sq_.ap[:, 5, :], sq_.ap[:, 2, :]), sqa, [sq_.b(9)])

        def stC(u):
            g = geom(u)
            j, qi, hh = g["j"], g["qi"], g["hh"]
            par, nk, hk = g["par"], g["nk"], g["hk"]
            o_ps, o_b = bank(6 + g["qbi"] % 2)
            vR = [Vf2.b(kb_) for kb_ in g["kbufs"]]
            nkt = nk // 128
            for t in range(nkt):
                s.mm(o_ps[:, hh * 64:(hh + 1) * 64], pT_sb[u % 3].ap[:, t * 128:(t + 1) * 128],
                     Vf2.ap[:, g["ktiles"][t], hk * 64:(hk + 1) * 64], t == 0, t == nkt - 1,
                     [pT_sb[u % 3].b(0)] + vR, [o_b])
            if hh == 7:
                qbi = g["qbi"]
                sq_ = SQ[qbi % 2]
                asb = att_sb[qbi % 2]
                s.tt(asb.ap.rearrange("p (h d) -> p h d", h=8), o_ps.rearrange("p (h d) -> p h d", h=8),
                     sq_.ap[:, 5, :].unsqueeze(2).broadcast_to([128, 8, 64]), ALU.mult, [o_b, sq_.b(9)], [asb.b()])
                lastq = (qi == g["nt"] // 128 - 1)

                def fin2(j=j, qi=qi, qbi=qbi, lastq=lastq):
                    asb_ = att_sb[qbi % 2]
                    mx_ = mix[j % 2]
                    gi = gDi[j % 2]
                    p7, p7b = bank(4 + qbi % 2)
                    p7v = p7.bitcast(BF16)
                    for c in range(4):
                        s.tr(p7v[:, c * 128:(c + 1) * 128], asb_.ap[:, c * 128:(c + 1) * 128], identb.ap,
                             [asb_.b(), identb.b()], [p7b])
                    s.tt(mx_.ap[:, 4:8, qi * 128:(qi + 1) * 128], p7v[:, 0:512].rearrange("p (c q) -> p c q", c=4),
                         gi.ap[:, :, qi * 128:(qi + 1) * 128], ALU.mult, [p7b, gi.b()], [mx_.b()])
                    if lastq:
                        stO3(j)
                deferred.setdefault(u + 4 + 2, []).append(fin2)

        def stO3(j):
            stx, t0, nt = blocks[j]
            mx_ = mix[j % 2]
            cbi = cbgi[j % 2]
            cxi = ccxi[j % 2]
            for c in range(4):
                pc = bank(4 + c % 2)
                for k in range(3):
                    s.mm(pc[0][:, :nt], diag.ap[:, c, k, :], cxi.ap[:, c, k:k + nt], k == 0, k == 2,
                         [diag.b(), cxi.b()], [pc[1]])
                s.tt(mx_.ap[:, c, :nt], pc[0][:, :nt], cbi.ap[:, c, :nt], ALU.mult, [pc[1], cbi.b()], [mx_.b()])
            out_stage(l, j, Wout, mx_, last, fin_bufs, banks=(4, 5))
            nx2 = ablocks.index(j) + 2
            if nx2 < len(ablocks):
                load_block(ablocks[nx2])

        step = 0
        while step < NU + 4 or any(k >= step for k in deferred):
            if step < NU:
                stA1(step)
            if 0 <= step - 2 < NU:
                stB(step - 2)
            if 0 <= step - 4 < NU:
                stC(step - 4)
            for fn in deferred.pop(step, []):
                fn()
            if mcalls and step in (NU // 4, NU // 2, (3 * NU) // 4):
                mcalls[1 + [NU // 4, NU // 2, (3 * NU) // 4].index(step)]()
            if win_next[0] is not None and step == (7 * NU) // 8:
                wsrcn = LW[nxt]["win"].rearrange("(kc p) n -> p kc n", p=128)
                c0 = 0
                while c0 < EVEN_IN:
                    cw_ = min(1024, EVEN_IN - c0)
                    ldc(win_next[0].ap[:, :, c0:c0 + cw_], wsrcn[:, :, c0:c0 + cw_], [], [win_next[0].b(c0 // 1024)])
                    c0 += cw_
            step += 1
        return win_next[0]

    win_pre = None
    for li, l in enumerate(layers):
        last = (li == len(layers) - 1)
        nxt = None if last else layers[li + 1]
        if l % 2 == 0:
            even_layer(l, last, nxt, first=(li == 0), win_pre=win_pre)
            win_pre = None
        else:
            win_pre = odd_layer(l, last, (l < 3), nxt)
    s.barrier()
    return nc


_TABLES = {}


def _cols(v, nchunk):
    return np.ascontiguousarray(np.asarray(v, np.float32).reshape(nchunk, 128).T)


def tables(L, LC):
    key = (L, LC)
    if key in _TABLES:
        return _TABLES[key]
    t = {}
    d = np.arange(128)
    angd = 2 * np.pi * ((d[:, None] * d[None, :]) % 128) / 128.0
    t["csd"] = np.concatenate([np.cos(angd), np.sin(angd)], axis=1) / np.sqrt(128.0)
    t["csd"] = t["csd"].astype(NPBF)
    NKB = L // 512
    NTX = L // 128
    NSG = NTX // NCS
    n = np.arange(L, dtype=np.int64)
    ang = 2 * np.pi * ((n[:, None] * n[None, :]) % L).astype(np.float64) / L
    tab = np.stack([np.cos(ang), -np.sin(ang)]) / np.sqrt(float(L))
    del ang
    NSG2 = NTX // 8
    tab = tab[:, :, :L // 2].astype(np.float32).reshape(2, NSG2, 8, 128, NKB, 256)
    t["dfth"] = np.ascontiguousarray(tab.transpose(4, 1, 3, 0, 2, 5)).astype(NPBF)
    del tab
    t["alt"] = np.repeat((np.where(np.arange(128) % 2 == 0, 1.0, -1.0) / np.sqrt(float(L)))[:, None], 2,
                         axis=1).astype(np.float32).astype(NPBF)
    nch = LC // 128
    m = np.arange(LC, dtype=np.int64)
    angc = 2 * np.pi * ((m[:, None] * m[None, :]) % LC) / float(LC)
    tc = (np.stack([np.cos(angc), -np.sin(angc)]) / np.sqrt(float(LC))).reshape(2, nch, 128, LC)
    t["dftc"] = np.ascontiguousarray(tc.transpose(2, 0, 1, 3)).astype(NPBF)
    p = np.arange(128)
    dd = p % 64
    a = dd // 32
    sgn = (dd % 32) // 16
    f = dd % 16
    inv = np.power(10000.0, -f / 16.0)
    tt_ = np.arange(L)
    pos = np.where(a[:, None] == 0, (tt_ // 64)[None, :], (tt_ % 64)[None, :]).astype(np.float64)
    angr = pos * inv[:, None]
    t["ropec"] = np.cos(angr).astype(np.float32)
    t["ropes"] = (np.sin(angr) * np.where(sgn == 0, -1.0, 1.0)[:, None]).astype(np.float32)
    t["ropecq"] = (t["ropec"] * np.float32(0.125)).astype(np.float32)
    t["ropesq"] = (t["ropes"] * np.float32(0.125)).astype(np.float32)
    r = np.arange(128)[:, None]
    jj = np.arange(384)[None, :]
    t["mask"] = np.where((jj >= r) & (jj <= r + 256), 0.0, NEG).astype(np.float32).astype(NPBF)
    t["identb"] = np.eye(128, dtype=np.float32).astype(NPBF)
    t["identf"] = np.eye(128, dtype=np.float32)
    _TABLES[key] = t
    return t


def prep(inputs, L, LC, layers):
    tb = tables(L, LC)
    f = lambda a: np.ascontiguousarray(np.asarray(a, dtype=np.float32))
    shared = dict(tb)
    shared["fin"] = _cols(inputs["final_norm"], KC)
    shared["finbc"] = np.ascontiguousarray(np.broadcast_to(f(inputs["final_norm"])[None, :], (128, D)))
    for l in layers:
        i = l // 2
        pre = "e_" if l % 2 == 0 else "o_"
        shared["wmod%d" % l] = f(inputs[pre + "w_mod"][i])
        shared["bmod%d" % l] = _cols(inputs[pre + "b_mod"][i], 24)
        shared["norm%d" % l] = _cols(inputs[pre + "norm"][i], KC)
        shared["wout%d" % l] = f(inputs[pre + "w_out"][i])
        if l % 2 == 0:
            shared["win%d" % l] = f(inputs["e_w_in"][i])
            cw = f(inputs["e_a_conv_w"][i])
            shared["cw%d" % l] = np.ascontiguousarray(cw.reshape(31, 4, 128).transpose(2, 1, 0))
            v4 = np.stack([f(inputs["e_a_conv_b"][i]), f(inputs["e_a_ln_g"][i]),
                           f(inputs["e_a_ln_b"][i]), f(inputs["e_a_pw_b"][i])], axis=-1)
            shared["v4%d" % l] = np.ascontiguousarray(v4.reshape(4, 128, 4).transpose(1, 0, 2))
            shared["pw%d" % l] = f(inputs["e_a_pw_w"][i])
            shared["fw%d" % l] = f(inputs["e_b_fw"][i])
        else:
            wi = f(inputs["o_w_in"][i])
            k0 = wi[:, 2560:2624]
            k1 = wi[:, 2624:2688]
            shared["win%d" % l] = np.ascontiguousarray(np.concatenate(
                [wi[:, 0:2560], k0, k0, k1, k1, wi[:, 2688:2816], wi[:, 2816:3328]], axis=1))
            cw = f(inputs["o_c_conv_w"][i])
            shared["cw%d" % l] = np.ascontiguousarray(cw.reshape(3, 4, 128).transpose(2, 1, 0))
            shared["sink%d" % l] = np.ascontiguousarray(
                np.broadcast_to(f(inputs["o_sink"][i])[None, :], (128, 8)))
    x = np.asarray(inputs["x"], np.float32)
    ctx = np.asarray(inputs["ctx"], np.float32)
    c = np.asarray(inputs["c"], np.float32)
    cctx = np.asarray(inputs["c_ctx"], np.float32)
    maps = []
    for b in range(x.shape[0]):
        m = dict(shared)
        m["x"] = np.ascontiguousarray(x[b])
        m["ctx"] = np.ascontiguousarray(ctx[b])
        m["ccols"] = np.ascontiguousarray(np.stack([_cols(c[b], KC), _cols(cctx, KC)], axis=-1))
        maps.append(m)
    return maps


_NC_CACHE = {}


def run(inputs, L, LC, layers):
    key = (L, LC, tuple(layers))
    if key not in _NC_CACHE:
        _NC_CACHE[key] = build(L, LC, layers)
    nc = _NC_CACHE[key]
    maps = prep(inputs, L, LC, layers)
    res = run_bass_kernel_spmd(nc, maps, core_ids=list(range(len(maps))))
    return np.stack([np.asarray(r["out"], np.float32) for r in res.results], axis=0)


def kernel(**inputs):
    return run(inputs, 4096, 256, [0, 1, 2, 3])
```
